# Optimizing a Trainium2 kernel written in Bass

```python
import jax, jax.numpy as jnp
from jax import lax
import numpy as np

D_MODEL = 1024
BATCH = 8
SEQ = 2048
DEPTH = 4
DEC_BATCH = 128
DEC_SEQ = 4
PAST_LEN = 2048
PAGE_SIZE = 128

N_SSM = (DEPTH + 1) // 2
N_ATT = DEPTH // 2
NORM_EPS = 1e-5
D_FF = 4 * D_MODEL
NEG_INF = -1e30

S5_WIDTH = D_MODEL // 2
S5_GROUP = 16
S5_GROUPS = S5_WIDTH // S5_GROUP
S5_STATE = 64

SSD_INNER = D_MODEL
SSD_HEAD_DIM = 64
SSD_HEADS = SSD_INNER // SSD_HEAD_DIM
SSD_STATE = 128
SSD_GROUPS = 4
SSD_CONV = 4
SSD_CONV_DIM = SSD_INNER + 2 * SSD_GROUPS * SSD_STATE
SSD_CHUNK = 128
MIX_EVEN = S5_WIDTH + SSD_INNER
IN_EVEN = S5_WIDTH + SSD_INNER + SSD_CONV_DIM + SSD_HEADS

N_HEADS = 16
HEAD_DIM = D_MODEL // N_HEADS
N_KV = 2
HEADS_PER_KV = N_HEADS // N_KV
KV_W = N_KV * HEAD_DIM
ROPE_DIM = HEAD_DIM // 4
ROPE_THETA = 500000.0
ATT_SCALE = HEAD_DIM ** -0.5
CMP_LEN = 32
CMP_STRIDE = 16
CMP_HIDDEN = 2 * HEAD_DIM
SEL_LEN = 64
SEL_TOPK = 16
FORCE_BONUS = 1e4
WINDOW = 512
ATT_QBLOCK = 128
SEL_QBLOCK = 64
IN_ODD = N_HEADS * HEAD_DIM + 6 * KV_W + 3 * N_HEADS

kernel_name = 'hybrid_s5_ssd_nsa_decode_step'


def rms_norm(x, g):
    xf = x.astype(jnp.float32)
    y = xf * lax.rsqrt(jnp.mean(xf * xf, -1, keepdims=True) + NORM_EPS)
    return (y * g.astype(jnp.float32)).astype(x.dtype)


def _block(n, pref):
    return pref if n % pref == 0 else n


def masked_softmax(s, mask):
    p = jax.nn.softmax(jnp.where(mask, s, NEG_INF), axis=-1)
    return jnp.where(mask, p, 0.0)


def rope(x, pos):
    half = ROPE_DIM // 2
    inv = ROPE_THETA ** (-jnp.arange(half, dtype=jnp.float32) / half)
    ang = pos.astype(jnp.float32)[:, None] * inv
    cos, sin = jnp.cos(ang)[:, None, :], jnp.sin(ang)[:, None, :]
    xr = x[..., :ROPE_DIM].astype(jnp.float32)
    x1, x2 = xr[..., :half], xr[..., half:]
    rot = jnp.concatenate([x1 * cos - x2 * sin, x2 * cos + x1 * sin], -1)
    return jnp.concatenate([rot.astype(x.dtype), x[..., ROPE_DIM:]], -1)


def _cmul(ar, ai, br, bi):
    return ar * br - ai * bi, ar * bi + ai * br


def _s5_combine(e1, e2):
    a1r, a1i, b1r, b1i = e1
    a2r, a2i, b2r, b2i = e2
    ar, ai = _cmul(a2r, a2i, a1r, a1i)
    br, bi = _cmul(a2r, a2i, b1r, b1i)
    return ar, ai, br + b2r, bi + b2i


def s5_mixer(u, h0, a_re, a_im, log_dt, b_re, b_im, c_re, c_im, d_skip, glu_w, glu_b):
    f32 = jnp.float32
    bt, s, _ = u.shape
    ug = u.astype(f32).reshape(bt, s, S5_GROUPS, S5_GROUP)
    a_re = a_re.astype(f32)
    a_im = a_im.astype(f32)
    dt = jnp.exp(log_dt.astype(f32))[:, None]
    mag = jnp.exp(a_re * dt)
    abar_re, abar_im = mag * jnp.cos(a_im * dt), mag * jnp.sin(a_im * dt)
    den = a_re * a_re + a_im * a_im
    f_re = ((abar_re - 1.0) * a_re + abar_im * a_im) / den
    f_im = (abar_im * a_re - (abar_re - 1.0) * a_im) / den
    bbar_re, bbar_im = _cmul(f_re[..., None], f_im[..., None], b_re.astype(f32), b_im.astype(f32))
    bu_re = jnp.einsum('bsgc,gpc->sbgp', ug, bbar_re)
    bu_im = jnp.einsum('bsgc,gpc->sbgp', ug, bbar_im)
    shape_a = (s, 1, S5_GROUPS, S5_STATE)
    acr, aci, hr, hi = lax.associative_scan(
        _s5_combine,
        (jnp.broadcast_to(abar_re, shape_a), jnp.broadcast_to(abar_im, shape_a), bu_re, bu_im),
        axis=0)
    h0f = h0.astype(f32)
    dr, di = _cmul(acr, aci, h0f[None, ..., 0], h0f[None, ..., 1])
    hr = hr + dr
    hi = hi + di
    y = (jnp.einsum('sbgp,gcp->bsgc', hr, c_re.astype(f32))
         - jnp.einsum('sbgp,gcp->bsgc', hi, c_im.astype(f32)))
    y = y + d_skip.astype(f32).reshape(S5_GROUPS, S5_GROUP) * ug
    zg = jnp.einsum('bsgc,gce->bsge', y, glu_w.astype(f32)) + glu_b.astype(f32)
    out = zg[..., :S5_GROUP] * jax.nn.sigmoid(zg[..., S5_GROUP:])
    h_last = jnp.stack([hr[-1], hi[-1]], -1)
    return out.reshape(bt, s, S5_WIDTH), h_last


def _segsum_exp(a):
    t = a.shape[-1]
    cs = jnp.cumsum(a, -1)
    tril = np.tril(np.ones((t, t), dtype=bool))
    return jnp.exp(jnp.where(tril, cs[..., :, None] - cs[..., None, :], -jnp.inf))


def ssd_mixer(z, xbc, dt_raw, conv_buf, h0, conv_w, conv_b, dt_bias, a_log, d_skip, norm_g):
    f32 = jnp.float32
    bt, s, _ = xbc.shape
    xpad = jnp.concatenate([conv_buf.astype(xbc.dtype), xbc], 1)
    conv = conv_b + sum(xpad[:, k:k + s] * conv_w[k] for k in range(SSD_CONV))
    new_buf = xpad[:, s:]
    xbc_a = jax.nn.silu(conv.astype(f32))
    n_bc = SSD_GROUPS * SSD_STATE
    x = xbc_a[..., :SSD_INNER].reshape(bt, s, SSD_HEADS, SSD_HEAD_DIM)
    bm = xbc_a[..., SSD_INNER:SSD_INNER + n_bc]
    cm = xbc_a[..., SSD_INNER + n_bc:]
    dt = jax.nn.softplus(dt_raw.astype(f32) + dt_bias.astype(f32))
    a = -jnp.exp(a_log.astype(f32))
    q = _block(s, SSD_CHUNK)
    nc = s // q
    r = SSD_HEADS // SSD_GROUPS
    xdt = (x * dt[..., None]).reshape(bt, nc, q, SSD_GROUPS, r, SSD_HEAD_DIM)
    bm = bm.reshape(bt, nc, q, SSD_GROUPS, SSD_STATE)
    cm = cm.reshape(bt, nc, q, SSD_GROUPS, SSD_STATE)
    a_dt = (dt * a).reshape(bt, nc, q, SSD_GROUPS, r).transpose(0, 3, 4, 1, 2)
    a_cs = jnp.cumsum(a_dt, -1)
    lmat = _segsum_exp(a_dt)
    cb = jnp.einsum('bclgn,bcsgn->bcgls', cm, bm)
    y_diag = jnp.einsum('bcgls,bgrcls,bcsgrp->bclgrp', cb, lmat, xdt)
    decay = jnp.exp(a_cs[..., -1:] - a_cs)
    states = jnp.einsum('bclgn,bgrcl,bclgrp->bcgrpn', bm, decay, xdt)
    h0g = h0.astype(f32).reshape(bt, 1, SSD_GROUPS, r, SSD_HEAD_DIM, SSD_STATE)
    states = jnp.concatenate([h0g, states], 1)
    chunk_decay = _segsum_exp(jnp.pad(a_cs[..., -1], [(0, 0)] * 3 + [(1, 0)]))
    states = jnp.einsum('bgrzc,bcgrpn->bzgrpn', chunk_decay, states)
    y_off = jnp.einsum('bclgn,bcgrpn,bgrcl->bclgrp', cm, states[:, :-1], jnp.exp(a_cs))
    y = (y_diag + y_off).reshape(bt, s, SSD_HEADS, SSD_HEAD_DIM) + x * d_skip.astype(f32)[:, None]
    y = y.reshape(bt, s, SSD_INNER) * jax.nn.silu(z.astype(f32))
    y = rms_norm(y, norm_g)
    h_last = states[:, -1].reshape(bt, SSD_HEADS, SSD_HEAD_DIM, SSD_STATE)
    return y, h_last, new_buf


def ssm_block(xn, s5_h0, ssd_h0, conv_buf, w_in, a_re, a_im, log_dt, b_re, b_im, c_re, c_im,
              d5, glu_w, glu_b, conv_w, conv_b, dt_bias, a_log, d_ssd, norm_g, w_out):
    proj = xn @ w_in
    o1 = S5_WIDTH
    o2 = o1 + SSD_INNER
    o3 = o2 + SSD_CONV_DIM
    u, z, xbc, dt_raw = proj[..., :o1], proj[..., o1:o2], proj[..., o2:o3], proj[..., o3:]
    ya, s5_h = s5_mixer(u, s5_h0, a_re, a_im, log_dt, b_re, b_im, c_re, c_im, d5, glu_w, glu_b)
    yb, ssd_h, conv_new = ssd_mixer(z, xbc, dt_raw, conv_buf, ssd_h0, conv_w, conv_b, dt_bias, a_log, d_ssd, norm_g)
    y = jnp.concatenate([ya, yb.astype(ya.dtype)], -1).astype(xn.dtype) @ w_out
    return y, s5_h, ssd_h, conv_new


def compress_blocks(k, w1, w2, pe):
    bt, t = k.shape[:2]
    ratio = CMP_LEN // CMP_STRIDE
    n_chunk = t // CMP_STRIDE
    n_cmp = n_chunk - ratio + 1
    ch = k[:, :n_chunk * CMP_STRIDE].reshape(bt, n_chunk, CMP_STRIDE, N_KV, HEAD_DIM)
    blocks = jnp.concatenate([ch[:, j:j + n_cmp] for j in range(ratio)], axis=2)
    blocks = blocks + pe[:, None, :]
    flat = blocks.transpose(0, 1, 3, 2, 4).reshape(bt, n_cmp, N_KV, CMP_LEN * HEAD_DIM)
    return jax.nn.silu(flat @ w1) @ w2


def nsa_compressed(q, full_cmp, q_off, w1, w2, pe):
    s = q.shape[1]
    kc = compress_blocks(full_cmp[:, :, 0], w1[0], w2[0], pe[0])
    vc = compress_blocks(full_cmp[:, :, 1], w1[1], w2[1], pe[1])
    n_cmp = kc.shape[1]
    q_pos = q_off + np.arange(s)
    ends = np.arange(n_cmp) * CMP_STRIDE + CMP_LEN - 1
    mask = ends[None, :] <= q_pos[:, None]
    sc = jnp.einsum('bsgrd,bngd->bsgrn', q, kc).astype(jnp.float32) * ATT_SCALE
    p = masked_softmax(sc, mask[None, :, None, None, :])
    return jnp.einsum('bsgrn,bngd->bsgrd', p, vc.astype(jnp.float32)), p


def nsa_selected(q, full_sel, p_cmp, q_off):
    bt, s = q.shape[:2]
    t = full_sel.shape[1]
    n_slc = -(-t // SEL_LEN)
    n_cmp = p_cmp.shape[-1]
    ci = np.arange(n_cmp)[:, None]
    sj = np.arange(n_slc)[None, :]
    overlap = ((ci * CMP_STRIDE < (sj + 1) * SEL_LEN)
               & (ci * CMP_STRIDE + CMP_LEN > sj * SEL_LEN)).astype(np.float32)
    imp = jnp.einsum('bsgn,nj->bsgj', p_cmp.sum(3), overlap)
    q_pos = q_off + np.arange(s)
    q_blk = q_pos // SEL_LEN
    jj = np.arange(n_slc)[None, :]
    valid = jj * SEL_LEN <= q_pos[:, None]
    forced = (jj == 0) | (jj == q_blk[:, None]) | (jj == q_blk[:, None] - 1)
    score = jnp.where(valid[None, :, None, :],
                      imp + np.where(forced, FORCE_BONUS, 0.0).astype(np.float32)[None, :, None, :],
                      NEG_INF)
    kk = min(SEL_TOPK, n_slc)
    _, idx = lax.top_k(score, kk)
    kv = jnp.pad(full_sel, ((0, 0), (0, n_slc * SEL_LEN - t), (0, 0), (0, 0), (0, 0)))
    kv = kv.reshape(bt, n_slc, SEL_LEN, 2, N_KV, HEAD_DIM).transpose(0, 4, 1, 2, 3, 5)
    kt, vt = kv[..., 0, :], kv[..., 1, :]
    qb = _block(s, SEL_QBLOCK)
    nb = s // qb
    q_blocks = q.reshape(bt, nb, qb, N_KV, HEADS_PER_KV, HEAD_DIM).swapaxes(0, 1)
    idx_blocks = idx.reshape(bt, nb, qb, N_KV, kk).swapaxes(0, 1)
    pos_blocks = jnp.asarray(q_pos.reshape(nb, qb))
    bi = jnp.arange(bt)[:, None, None, None]
    gi = jnp.arange(N_KV)[None, None, :, None]
    offs = jnp.arange(SEL_LEN)

    def one_block(args):
        qblk, iblk, pblk = args
        kg = kt[bi, gi, iblk]
        vg = vt[bi, gi, iblk]
        kpos = iblk[..., None] * SEL_LEN + offs
        mask = (kpos <= pblk[None, :, None, None, None])[:, :, :, None]
        sc = jnp.einsum('bqgrd,bqgkld->bqgrkl', qblk, kg).astype(jnp.float32) * ATT_SCALE
        shp = sc.shape
        p = masked_softmax(sc.reshape(shp[:4] + (kk * SEL_LEN,)),
                           jnp.broadcast_to(mask, shp).reshape(shp[:4] + (kk * SEL_LEN,))).reshape(shp)
        return jnp.einsum('bqgrkl,bqgkld->bqgrd', p, vg.astype(jnp.float32))

    o = lax.map(one_block, (q_blocks, idx_blocks, pos_blocks))
    return o.swapaxes(0, 1).reshape(bt, s, N_KV, HEADS_PER_KV, HEAD_DIM)


def nsa_window(q, full_win, n_prev):
    bt, s = q.shape[:2]
    qb = _block(s, ATT_QBLOCK)
    nb = s // qb
    band = qb + WINDOW - 1
    kv = jnp.pad(full_win, ((0, 0), (WINDOW - 1, 0), (0, 0), (0, 0), (0, 0)))
    q_blocks = q.reshape(bt, nb, qb, N_KV, HEADS_PER_KV, HEAD_DIM).swapaxes(0, 1)

    def one_block(args):
        blk, qblk = args
        start = blk * qb + n_prev
        kvb = lax.dynamic_slice_in_dim(kv, start, band, axis=1)
        k_idx = start - (WINDOW - 1) + jnp.arange(band)
        q_idx = start + jnp.arange(qb)
        rel = q_idx[:, None] - k_idx[None, :]
        mask = (k_idx[None, :] >= 0) & (rel >= 0) & (rel < WINDOW)
        sc = jnp.einsum('bqgrd,blgd->bqgrl', qblk, kvb[:, :, 0]).astype(jnp.float32) * ATT_SCALE
        p = masked_softmax(sc, mask[None, :, None, None, :])
        return jnp.einsum('bqgrl,blgd->bqgrd', p, kvb[:, :, 1].astype(jnp.float32))

    o = lax.map(one_block, (jnp.arange(nb), q_blocks))
    return o.swapaxes(0, 1).reshape(bt, s, N_KV, HEADS_PER_KV, HEAD_DIM)


def _heads(t, n):
    return t.reshape(t.shape[0], t.shape[1], n, HEAD_DIM)


def nsa_block(xn, q_off, past_cmp, past_sel, win_buf, w_in, cmp_w1, cmp_w2, cmp_pos, w_out):
    bt, s, _ = xn.shape
    q_w = N_HEADS * HEAD_DIM
    cuts = np.cumsum([q_w] + [KV_W] * 6).tolist()
    q, k_cmp, v_cmp, k_sel, v_sel, k_win, v_win, g = jnp.split(xn @ w_in, cuts, axis=-1)
    pos = q_off + jnp.arange(s)
    q = rope(_heads(q, N_HEADS), pos).reshape(bt, s, N_KV, HEADS_PER_KV, HEAD_DIM)
    new_cmp = jnp.stack([rope(_heads(k_cmp, N_KV), pos), _heads(v_cmp, N_KV)], 2)
    new_sel = jnp.stack([rope(_heads(k_sel, N_KV), pos), _heads(v_sel, N_KV)], 2)
    new_win = jnp.stack([rope(_heads(k_win, N_KV), pos), _heads(v_win, N_KV)], 2)
    full_cmp = jnp.concatenate([past_cmp.astype(new_cmp.dtype), new_cmp], 1)
    full_sel = jnp.concatenate([past_sel.astype(new_sel.dtype), new_sel], 1)
    full_win = jnp.concatenate([win_buf.astype(new_win.dtype), new_win], 1)
    o_cmp, p_cmp = nsa_compressed(q, full_cmp, q_off, cmp_w1, cmp_w2, cmp_pos)
    o_sel = nsa_selected(q, full_sel, p_cmp, q_off)
    o_win = nsa_window(q, full_win, win_buf.shape[1])
    gate = jax.nn.sigmoid(g.astype(jnp.float32)).reshape(bt, s, N_KV, HEADS_PER_KV, 3)
    o = gate[..., 0:1] * o_cmp + gate[..., 1:2] * o_sel + gate[..., 2:3] * o_win
    y = o.reshape(bt, s, q_w).astype(xn.dtype) @ w_out
    keep = min(WINDOW, full_win.shape[1])
    return y, new_cmp, new_sel, full_win[:, full_win.shape[1] - keep:]


def sq_relu_mlp(xn, w_up, w_down):
    h = jax.nn.relu(xn @ w_up)
    return (h * h) @ w_down


def setup_inputs(seed: int = 0) -> dict:
    key = jax.random.key(seed)
    ks = iter(jax.random.split(key, 64))
    f32 = jnp.float32
    nrm = lambda shape, std=1.0: jax.random.normal(next(ks), shape, f32) * std
    n_pages = PAST_LEN // PAGE_SIZE
    n_pool = (DEC_BATCH * n_pages * 5) // 4
    wbuf = min(WINDOW, PAST_LEN)
    page_table = jax.random.permutation(next(ks), n_pool)[:DEC_BATCH * n_pages].reshape(DEC_BATCH, n_pages).astype(jnp.int32)
    dt_ssd = jnp.exp(jax.random.uniform(next(ks), (N_SSM, SSD_HEADS), f32, np.log(1e-3), np.log(1e-1)))
    return {
        'x_prompt': nrm((BATCH, SEQ, D_MODEL)),
        'x_sample': nrm((DEC_BATCH, DEC_SEQ, D_MODEL)),
        'state_s5': nrm((N_SSM, DEC_BATCH, S5_GROUPS, S5_STATE, 2), 0.1),
        'state_ssd': nrm((N_SSM, DEC_BATCH, SSD_HEADS, SSD_HEAD_DIM, SSD_STATE), 0.1),
        'state_conv': nrm((N_SSM, DEC_BATCH, SSD_CONV - 1, SSD_CONV_DIM)),
        'cache_cmp_kv': nrm((N_ATT, n_pool, PAGE_SIZE, 2, N_KV, HEAD_DIM)),
        'cache_sel_kv': nrm((N_ATT, n_pool, PAGE_SIZE, 2, N_KV, HEAD_DIM)),
        'state_win_kv': nrm((N_ATT, DEC_BATCH, wbuf, 2, N_KV, HEAD_DIM)),
        'page_table': page_table,
        'norm_mix_even': 1.0 + nrm((N_SSM, D_MODEL), 0.01),
        'w_in_even': nrm((N_SSM, D_MODEL, IN_EVEN), D_MODEL ** -0.5),
        's5_a_re': -0.5 + nrm((N_SSM, S5_GROUPS, S5_STATE), 0.01),
        's5_a_im': jnp.pi * jnp.arange(S5_STATE, dtype=f32) + nrm((N_SSM, S5_GROUPS, S5_STATE), 0.01),
        's5_log_dt': jax.random.uniform(next(ks), (N_SSM, S5_GROUPS), f32, np.log(1e-3), np.log(1e-1)),
        's5_b_re': nrm((N_SSM, S5_GROUPS, S5_STATE, S5_GROUP), (2 * S5_GROUP) ** -0.5),
        's5_b_im': nrm((N_SSM, S5_GROUPS, S5_STATE, S5_GROUP), (2 * S5_GROUP) ** -0.5),
        's5_c_re': nrm((N_SSM, S5_GROUPS, S5_GROUP, S5_STATE), S5_STATE ** -0.5),
        's5_c_im': nrm((N_SSM, S5_GROUPS, S5_GROUP, S5_STATE), S5_STATE ** -0.5),
        's5_d': nrm((N_SSM, S5_WIDTH)),
        's5_glu_w': nrm((N_SSM, S5_GROUPS, S5_GROUP, 2 * S5_GROUP), S5_GROUP ** -0.5),
        's5_glu_b': nrm((N_SSM, S5_GROUPS, 2 * S5_GROUP), 0.01),
        'ssd_conv_w': nrm((N_SSM, SSD_CONV, SSD_CONV_DIM), 0.5),
        'ssd_conv_b': nrm((N_SSM, SSD_CONV_DIM), 0.01),
        'ssd_dt_bias': dt_ssd + jnp.log(-jnp.expm1(-dt_ssd)),
        'ssd_a_log': jnp.log(jax.random.uniform(next(ks), (N_SSM, SSD_HEADS), f32, 1.0, 16.0)),
        'ssd_d': 1.0 + nrm((N_SSM, SSD_HEADS), 0.1),
        'ssd_norm': 1.0 + nrm((N_SSM, SSD_INNER), 0.01),
        'w_out_even': nrm((N_SSM, MIX_EVEN, D_MODEL), MIX_EVEN ** -0.5),
        'norm_mix_odd': 1.0 + nrm((N_ATT, D_MODEL), 0.01),
        'w_in_odd': nrm((N_ATT, D_MODEL, IN_ODD), D_MODEL ** -0.5),
        'cmp_w1': nrm((N_ATT, 2, CMP_LEN * HEAD_DIM, CMP_HIDDEN), (CMP_LEN * HEAD_DIM) ** -0.5),
        'cmp_w2': nrm((N_ATT, 2, CMP_HIDDEN, HEAD_DIM), CMP_HIDDEN ** -0.5),
        'cmp_pos': nrm((N_ATT, 2, CMP_LEN, HEAD_DIM), 0.1),
        'w_out_odd': nrm((N_ATT, N_HEADS * HEAD_DIM, D_MODEL), (N_HEADS * HEAD_DIM) ** -0.5),
        'norm_mlp': 1.0 + nrm((DEPTH, D_MODEL), 0.01),
        'w_up': nrm((DEPTH, D_MODEL, D_FF), D_MODEL ** -0.5),
        'w_down': nrm((DEPTH, D_FF, D_MODEL), D_FF ** -0.5),
        'norm_final': 1.0 + nrm((D_MODEL,), 0.01),
    }


def reference(x_prompt, x_sample, state_s5, state_ssd, state_conv, cache_cmp_kv, cache_sel_kv,
              state_win_kv, page_table, norm_mix_even, w_in_even, s5_a_re, s5_a_im, s5_log_dt,
              s5_b_re, s5_b_im, s5_c_re, s5_c_im, s5_d, s5_glu_w, s5_glu_b, ssd_conv_w, ssd_conv_b,
              ssd_dt_bias, ssd_a_log, ssd_d, ssd_norm, w_out_even, norm_mix_odd, w_in_odd, cmp_w1,
              cmp_w2, cmp_pos, w_out_odd, norm_mlp, w_up, w_down, norm_final):
    bp, bs = x_prompt.shape[0], x_sample.shape[0]
    past_len = page_table.shape[1] * PAGE_SIZE
    dt_ = x_prompt.dtype
    hp, hs = x_prompt, x_sample
    s5_p, s5_s, ssd_p, ssd_s, conv_p, conv_s = [], [], [], [], [], []
    cmp_p, cmp_s, sel_p, sel_s, win_p, win_s = [], [], [], [], [], []
    for layer in range(DEPTH):
        i = layer // 2
        if layer % 2 == 0:
            w = (w_in_even[i], s5_a_re[i], s5_a_im[i], s5_log_dt[i], s5_b_re[i], s5_b_im[i],
                 s5_c_re[i], s5_c_im[i], s5_d[i], s5_glu_w[i], s5_glu_b[i], ssd_conv_w[i],
                 ssd_conv_b[i], ssd_dt_bias[i], ssd_a_log[i], ssd_d[i], ssd_norm[i], w_out_even[i])
            y, a, b, c = ssm_block(rms_norm(hp, norm_mix_even[i]),
                                   jnp.zeros((bp, S5_GROUPS, S5_STATE, 2), dt_),
                                   jnp.zeros((bp, SSD_HEADS, SSD_HEAD_DIM, SSD_STATE), dt_),
                                   jnp.zeros((bp, SSD_CONV - 1, SSD_CONV_DIM), dt_), *w)
            hp = hp + y.astype(hp.dtype)
            s5_p.append(a)
            ssd_p.append(b)
            conv_p.append(c)
            y, a, b, c = ssm_block(rms_norm(hs, norm_mix_even[i]), state_s5[i], state_ssd[i], state_conv[i], *w)
            hs = hs + y.astype(hs.dtype)
            s5_s.append(a)
            ssd_s.append(b)
            conv_s.append(c)
        else:
            w = (w_in_odd[i], cmp_w1[i], cmp_w2[i], cmp_pos[i], w_out_odd[i])
            empty = jnp.zeros((bp, 0, 2, N_KV, HEAD_DIM), dt_)
            y, a, b, c = nsa_block(rms_norm(hp, norm_mix_odd[i]), 0, empty, empty, empty, *w)
            hp = hp + y.astype(hp.dtype)
            cmp_p.append(a)
            sel_p.append(b)
            win_p.append(c)
            past_cmp = cache_cmp_kv[i][page_table].reshape(bs, past_len, 2, N_KV, HEAD_DIM)
            past_sel = cache_sel_kv[i][page_table].reshape(bs, past_len, 2, N_KV, HEAD_DIM)
            y, a, b, c = nsa_block(rms_norm(hs, norm_mix_odd[i]), past_len, past_cmp, past_sel, state_win_kv[i], *w)
            hs = hs + y.astype(hs.dtype)
            cmp_s.append(a)
            sel_s.append(b)
            win_s.append(c)
        hp = hp + sq_relu_mlp(rms_norm(hp, norm_mlp[layer]), w_up[layer], w_down[layer])
        hs = hs + sq_relu_mlp(rms_norm(hs, norm_mlp[layer]), w_up[layer], w_down[layer])
    y_prompt = rms_norm(hp, norm_final)
    y_sample = rms_norm(hs, norm_final)
    return (y_prompt, y_sample, jnp.stack(s5_p), jnp.stack(s5_s), jnp.stack(ssd_p), jnp.stack(ssd_s),
            jnp.stack(conv_p), jnp.stack(conv_s), jnp.stack(cmp_p), jnp.stack(cmp_s),
            jnp.stack(sel_p), jnp.stack(sel_s), jnp.stack(win_p), jnp.stack(win_s))
```

```python
import numpy as np
import concourse.bass as bass
import concourse.mybir as mybir
from concourse.bass_utils import run_bass_kernel_spmd

F32 = mybir.dt.float32
BF16 = mybir.dt.bfloat16
I32 = mybir.dt.int32
AF = mybir.ActivationFunctionType
ALU = mybir.AluOpType
AX = mybir.AxisListType

EPOCH = 30000
SELF_WAIT = True


class T:
    def __init__(self, name, ap):
        self.name = name
        self.ap = ap
        self.last_w = None
        self.reads = {}
        self.root = self


class Eng:
    def __init__(self, S, name, obj, self_wait=True):
        self.S = S
        self.name = name
        self.obj = obj
        self.epoch = 0
        self.count = 0
        self.sem = S.new_sem(name + "_e0")
        self.seen = {}
        self.self_wait = self_wait

    def key(self):
        return (self.name, self.epoch)

    def next_token(self):
        if self.count >= EPOCH:
            self.epoch += 1
            self.count = 0
            self.sem = self.S.new_sem("%s_e%d" % (self.name, self.epoch))
        self.count += 1
        k = self.key()
        self.S.semmap[k] = self.sem
        return (k, self.count)


class Sched:
    def __init__(self, nc, n_dma_sems=12):
        self.nc = nc
        self.semmap = {}
        self.nsem = 0
        self.pe_e = Eng(self, "pe", nc.tensor, self_wait=False)
        sw = SELF_WAIT
        self.act_e = Eng(self, "act", nc.scalar, self_wait=sw)
        self.dve_e = Eng(self, "dve", nc.vector, self_wait=sw)
        self.pool_e = Eng(self, "pool", nc.gpsimd, self_wait=True)
        self.sp_e = Eng(self, "sp", nc.sync)
        self.engs = [self.pe_e, self.act_e, self.dve_e, self.pool_e, self.sp_e]
        self.dma_pools = {}
        for qn in ("sp", "pool", "act"):
            sems = []
            for i in range(n_dma_sems):
                s = self.new_sem("dma_%s_%d" % (qn, i))
                k = ("dma_" + qn, i)
                self.semmap[k] = s
                sems.append([k, 0])
            self.dma_pools[qn] = [sems, 0]
        self.out_tokens = []
        self.nins = 0

    def new_sem(self, name):
        self.nsem += 1
        return self.nc.alloc_semaphore(name)

    def sb(self, name, shape, dtype):
        h = self.nc.alloc_sbuf_tensor(name, list(shape), dtype)
        return T(name, h[:])

    def ps(self, name, shape, dtype):
        h = self.nc.alloc_psum_tensor(name, list(shape), dtype)
        return T(name, h[:])

    def view(self, name, ap):
        return T(name, ap)

    def alias(self, name, base, ap):
        t = T(name, ap)
        t.root = base.root
        return t

    def _wait(self, eng, deps):
        best = {}
        for d in deps:
            if d is None:
                continue
            k, v = d
            if best.get(k, 0) < v:
                best[k] = v
        for k, v in best.items():
            if k == eng.key() and not eng.self_wait:
                continue
            if eng.seen.get(k, 0) >= v:
                continue
            eng.obj.wait_ge(self.semmap[k], v)
            eng.seen[k] = v

    def _deps(self, reads, writes):
        deps = []
        for t in reads:
            deps.append(t.root.last_w)
        for t in writes:
            t = t.root
            deps.append(t.last_w)
            for k, v in t.reads.items():
                deps.append((k, v))
        return deps

    def _commit(self, tok, reads, writes):
        k, v = tok
        for t in reads:
            t = t.root
            if t.reads.get(k, 0) < v:
                t.reads[k] = v
        for t in writes:
            t = t.root
            t.last_w = tok
            t.reads = {}

    def op(self, eng, fn, reads, writes):
        self._wait(eng, self._deps(reads, writes))
        ins = fn(eng.obj)
        tok = eng.next_token()
        ins.then_inc(eng.sem, 1)
        self._commit(tok, reads, writes)
        self.nins += 1
        return tok

    def pe(self, fn, reads, writes):
        return self.op(self.pe_e, fn, reads, writes)

    def act(self, fn, reads, writes):
        return self.op(self.act_e, fn, reads, writes)

    def dve(self, fn, reads, writes):
        return self.op(self.dve_e, fn, reads, writes)

    def pool(self, fn, reads, writes):
        return self.op(self.pool_e, fn, reads, writes)

    def dma_fn(self, fn, reads, writes, q="sp"):
        eng = {"sp": self.sp_e, "pool": self.pool_e, "act": self.act_e}[q]
        pool = self.dma_pools[q]
        sems, idx = pool
        ent = sems[idx]
        pool[1] = (idx + 1) % len(sems)
        k = ent[0]
        deps = self._deps(reads, writes)
        if ent[1] > 0:
            deps.append((k, ent[1]))
        self._wait(eng, deps)
        ins = fn(eng.obj)
        ent[1] += 16
        tok = (k, ent[1])
        ins.then_inc(self.semmap[k], 16)
        self._commit(tok, reads, writes)
        self.nins += 1
        return tok

    def dma(self, dst_t, dst_ap, src_ap, q="sp", src_t=None, **kw):
        kw.setdefault("allow_slow_non_contiguous", True)
        reads = [src_t] if src_t is not None else []
        return self.dma_fn(lambda e: e.dma_start(out=dst_ap, in_=src_ap, **kw), reads, [dst_t], q=q)

    def dma_out(self, dst_ap, src_ap, src_t, q="sp", dst_t=None, **kw):
        kw.setdefault("allow_slow_non_contiguous", True)
        writes = [dst_t] if dst_t is not None else []
        tok = self.dma_fn(lambda e: e.dma_start(out=dst_ap, in_=src_ap, **kw), [src_t], writes, q=q)
        self.out_tokens.append(tok)
        return tok

    def make_identity(self, t):
        n = t.ap.shape[-1]
        self.pool(lambda e: e.memset(t.ap[:], 0.0), [], [t])
        self.pool(lambda e: e.affine_select(out=t.ap[:], in_=t.ap[:], compare_op=ALU.not_equal, fill=1.0,
                                            base=0, pattern=[[-1, n]], channel_multiplier=1), [t], [t])

    def barrier(self):
        deps = []
        for qn, (sems, _) in self.dma_pools.items():
            for k, v in sems:
                if v > 0:
                    deps.append((k, v))
        for e in self.engs:
            if e.count > 0:
                deps.append((e.key(), e.count))
        for e in self.engs:
            self._wait(e, deps)

    def finish(self):
        eng = self.sp_e
        deps = list(self.out_tokens)
        for qn, (sems, _) in self.dma_pools.items():
            for k, v in sems:
                if v > 0:
                    deps.append((k, v))
        for e in self.engs:
            if e.count > 0:
                deps.append((e.key(), e.count))
        self._wait(eng, deps)


D = 1024
NT = 17
TP = 2048
TS = 64
TT = TP + TS
DFF = 4096
EPS = 1e-5
IN_EVEN = 3600
IN_ODD = 1840
NPOOL = 2560


def rows(i):
    return 128 if i < 16 else 64


import math
TWO_PI = 2.0 * math.pi


def _prod(xs):
    r = 1
    for v in xs:
        r *= v
    return r


class MK:
    ARENA = 96 * 1024

    def __init__(self, parts=("mlp",), depth=4, dbg=False):
        self.parts = set(parts)
        self.depth = depth
        self.dbg = dbg
        nc = bass.Bass("TRN2", target_bir_lowering=False)
        self.nc = nc
        self.S = Sched(nc)
        self.din = {}
        self.dout = {}

    def inp(self, name, shape, dtype=F32):
        t = self.nc.dram_tensor(name, list(shape), dtype, kind="ExternalInput").ap()
        self.din[name] = t
        return t

    def outp(self, name, shape, dtype=F32):
        t = self.nc.dram_tensor(name, list(shape), dtype, kind="ExternalOutput").ap()
        self.dout[name] = t
        return t

    def arena_reset(self):
        self.S.barrier()
        self.aoff = 0

    def av(self, name, shape, dtype):
        esz = 2 if dtype == BF16 else 4
        n = _prod(shape[1:])
        nbytes = (n * esz + 63) // 64 * 64
        off = self.aoff
        self.aoff += nbytes
        assert self.aoff <= self.ARENA, ("arena overflow", name, self.aoff)
        ap = self.arena[:, off // 2: off // 2 + nbytes // 2]
        if dtype != BF16:
            ap = ap.bitcast(dtype)
        ap = ap[:shape[0], :n]
        if len(shape) > 2:
            names = "abcd"[:len(shape) - 1]
            pat = "p (%s) -> p %s" % (" ".join(names), " ".join(names))
            ap = ap.rearrange(pat, **{names[j]: shape[1 + j] for j in range(len(shape) - 1)})
        return T(name, ap)

    def build(self):
        S = self.S
        nc = self.nc
        xp = self.inp("x_prompt", [TP, D])
        xs = self.inp("x_sample", [TS, D])
        self.norm_mlp = self.inp("norm_mlp", [4, D])
        self.w_up = self.inp("w_up", [4, D, DFF])
        self.w_down = self.inp("w_down", [4, DFF, D])
        self.norm_final = self.inp("norm_final", [1, D])
        yp = self.outp("y_prompt", [TP, D])
        ys = self.outp("y_sample", [TS, D])
        if "even" in self.parts:
            self.decl_even()
        if "odd" in self.parts:
            self.decl_odd()

        self.X = S.sb("X", [128, NT, D], F32)
        self.xt = [S.view("X%d" % i, None) for i in range(NT)]
        self.wslot = [S.sb("wslot%d" % i, [128, 4096], BF16) for i in range(4)]
        self.wrr = 0
        self.ident = S.sb("ident", [128, 128], F32)
        self.identb = S.sb("identb", [128, 128], BF16)
        self.tri = S.sb("tri", [128, 128], F32)
        self.negm = S.sb("negm", [128, 128], F32)
        self.onesf = S.sb("onesf", [128, 128], F32)
        self.onesb = S.sb("onesb", [128, 256], BF16)
        self.gb = S.sb("gb", [128, D], F32)
        self.ss = [S.sb("ss%d" % i, [128, 2], F32) for i in range(2)]
        self.xnb = [S.sb("xnb%d" % i, [128, D], BF16) for i in range(2)]
        self.arena = nc.alloc_sbuf_tensor("arena", [128, self.ARENA // 2], BF16)
        self.aoff = 0
        self.PS = nc.alloc_psum_tensor("PS", [128, 8, 512], F32)
        self.P = [S.view("P%d" % i, self.PS[:, i, :]) for i in range(8)]
        self.Pb = [self.PS[:, i, :].bitcast(BF16) for i in range(8)]
        self.nrm_i = 0

        S.make_identity(self.ident)
        S.dve(lambda e: e.tensor_copy(self.identb.ap[:], self.ident.ap[:]), [self.ident], [self.identb])
        S.pool(lambda e: e.memset(self.onesf.ap[:], 1.0), [], [self.onesf])
        S.pool(lambda e: e.memset(self.onesb.ap[:], 1.0), [], [self.onesb])
        S.pool(lambda e: e.memset(self.tri.ap[:], 1.0), [], [self.tri])
        S.pool(lambda e: e.affine_select(out=self.tri.ap[:], in_=self.tri.ap[:], compare_op=ALU.is_ge, fill=0.0, base=0,
                                         pattern=[[1, 128]], channel_multiplier=-1), [self.tri], [self.tri])
        S.pool(lambda e: e.memset(self.negm.ap[:], 0.0), [], [self.negm])
        S.pool(lambda e: e.affine_select(out=self.negm.ap[:], in_=self.negm.ap[:], compare_op=ALU.is_ge, fill=-30000.0, base=0,
                                         pattern=[[1, 128]], channel_multiplier=-1), [self.negm], [self.negm])
        for i in range(16):
            S.dma(self.xt[i], self.X.ap[:, i, :], xp[i * 128:(i + 1) * 128, :])
        S.dma(self.xt[16], self.X.ap[:64, 16, :], xs[:, :])

        for l in range(self.depth):
            if l % 2 == 0 and "even" in self.parts:
                self.even_layer(l)
            if l % 2 == 1 and "odd" in self.parts:
                self.odd_layer(l)
            if "mlp" in self.parts:
                self.mlp_layer(l)
        self.arena_reset()
        self.xn = [self.av("xnf%d" % i, [128, D], F32) for i in range(2)]
        self.load_gb(self.norm_final[0:1, :])
        for i in range(NT):
            r = rows(i)
            xn = self.norm_tile(i)
            dst = yp[i * 128:(i + 1) * 128, :] if i < 16 else ys[:, :]
            S.dma_out(dst, xn.ap[:r, :], xn)
        S.finish()
        return nc

    def load_gb(self, row_ap):
        self.S.dma(self.gb, self.gb.ap[:], row_ap.partition_broadcast(128))

    def norm_tile(self, i):
        S = self.S
        r = rows(i)
        k = self.nrm_i % 2
        self.nrm_i += 1
        xn, ss = self.xn[k], self.ss[k]
        xi = self.X.ap[:r, i, :]
        S.pool(lambda e: e.memset(ss.ap[:], 0.0), [], [ss])
        S.act(lambda e: e.activation(out=xn.ap[:r, :], in_=xi, func=AF.Square, accum_out=ss.ap[:r, 0:1]),
              [self.xt[i]], [xn, ss])
        S.dve(lambda e: e.tensor_scalar(out=ss.ap[:r, 1:2], in0=ss.ap[:r, 0:1], scalar1=1.0 / D, scalar2=EPS,
                                        op0=ALU.mult, op1=ALU.add), [ss], [ss])
        S.act(lambda e: e.activation(out=ss.ap[:r, 1:2], in_=ss.ap[:r, 1:2], func=AF.Sqrt), [ss], [ss])
        S.dve(lambda e: e.reciprocal(out=ss.ap[:r, 1:2], in_=ss.ap[:r, 1:2]), [ss], [ss])
        S.dve(lambda e: e.scalar_tensor_tensor(out=xn.ap[:r, :], in0=xi, scalar=ss.ap[:r, 1:2], in1=self.gb.ap[:r, :],
                                               op0=ALU.mult, op1=ALU.mult), [self.xt[i], ss, self.gb], [xn])
        return xn

    def norm_T(self, i, dstT, dst_t, col0, pbase=0):
        S = self.S
        r = rows(i)
        k = self.nrm_i % 2
        self.nrm_i += 1
        ss, xb = self.ss[k], self.xnb[k]
        xi = self.X.ap[:r, i, :]
        S.pool(lambda e: e.memset(ss.ap[:], 0.0), [], [ss])
        S.act(lambda e: e.activation(out=xb.ap[:r, :], in_=xi, func=AF.Square, accum_out=ss.ap[:r, 0:1]),
              [self.xt[i]], [xb, ss])
        S.dve(lambda e: e.tensor_scalar(out=ss.ap[:r, 1:2], in0=ss.ap[:r, 0:1], scalar1=1.0 / D, scalar2=EPS,
                                        op0=ALU.mult, op1=ALU.add), [ss], [ss])
        S.act(lambda e: e.activation(out=ss.ap[:r, 1:2], in_=ss.ap[:r, 1:2], func=AF.Sqrt), [ss], [ss])
        S.dve(lambda e: e.reciprocal(out=ss.ap[:r, 1:2], in_=ss.ap[:r, 1:2]), [ss], [ss])
        S.dve(lambda e: e.scalar_tensor_tensor(out=xb.ap[:r, :], in0=xi, scalar=ss.ap[:r, 1:2], in1=self.gb.ap[:r, :],
                                               op0=ALU.mult, op1=ALU.mult), [self.xt[i], ss, self.gb], [xb])
        PP = self.P[pbase]
        pb = self.Pb[pbase]
        for cc in range(8):
            S.pe(lambda e: e.transpose(pb[:, cc * 128:cc * 128 + r], xb.ap[:r, cc * 128:(cc + 1) * 128],
                                       self.identb.ap[:r, :r]), [xb, self.identb], [PP])
        S.act(lambda e: e.activation(out=dstT[:, :, col0:col0 + r],
                                     in_=pb[:, :].rearrange("p (c n) -> p c n", c=8)[:, :, :r],
                                     func=AF.Copy), [PP], [dst_t])

    def load_w(self, dram_ap, slot=None):
        if slot is None:
            slot = self.wrr % 4
            self.wrr += 1
        slot = self.wslot[slot]
        a, b = dram_ap.shape[1], dram_ap.shape[2]
        assert a * b <= 4096
        ap = slot.ap[:, :a * b].rearrange("p (a b) -> p a b", a=a)
        self.S.dma(slot, ap, dram_ap, q="pool")
        return slot, ap

    def mlp_layer(self, l):
        S = self.S
        self.arena_reset()
        xnT = self.av("xnT", [128, 8, TT], BF16)
        xnT_t = [S.view("xnT%d" % i, None) for i in range(NT)]
        hTs = [self.av("hT%d" % i, [128, 4, 256], BF16) for i in range(2)]
        hrs = [self.av("hr%d" % i, [128, 256], F32) for i in range(2)]
        self.load_gb(self.norm_mlp[l:l + 1, :])
        for i in range(NT):
            self.norm_T(i, xnT.ap, xnT_t[i], i * 128, pbase=(i % 2) * 2)
        wu_d = self.w_up[l].rearrange("(c p) n -> p c n", p=128)
        wd_d = self.w_down[l].rearrange("(c p) n -> p c n", p=128)
        groups = [(g * 256, 256, [2 * g, 2 * g + 1]) for g in range(8)] + [(TP, 64, [16])]
        gi = 0
        pend = None

        def emit_down(pd):
            (hT_, wd_t_, wd_, tiles_) = pd
            for ti, t in enumerate(tiles_):
                r = rows(t)
                for half in range(2):
                    P = self.P[2 + (ti * 2 + half)]
                    for j in range(4):
                        S.pe(lambda e: e.matmul(P.ap[:r, :], hT_.ap[:, j, ti * 128:ti * 128 + r],
                                                wd_[:, j, half * 512:(half + 1) * 512], start=(j == 0), stop=(j == 3)),
                             [hT_, wd_t_], [P])
                    S.dve(lambda e: e.tensor_tensor(out=self.X.ap[:r, t, half * 512:(half + 1) * 512],
                                                    in0=self.X.ap[:r, t, half * 512:(half + 1) * 512],
                                                    in1=P.ap[:r, :], op=ALU.add), [P, self.xt[t]], [self.xt[t]])

        for q in range(8):
            wu_t, wu = self.load_w(wu_d[:, :, q * 512:(q + 1) * 512])
            wd_t, wd = self.load_w(wd_d[:, q * 4:(q + 1) * 4, :])
            for (t0, nt, tiles) in groups:
                hT = hTs[gi % 2]
                gi += 1
                for j in range(4):
                    P = self.P[j % 2]
                    hr = hrs[j % 2]
                    for k in range(8):
                        S.pe(lambda e: e.matmul(P.ap[:, :nt], wu[:, k, j * 128:(j + 1) * 128], xnT.ap[:, k, t0:t0 + nt],
                                                start=(k == 0), stop=(k == 7)),
                             [wu_t] + [xnT_t[t] for t in tiles], [P])
                    S.act(lambda e: e.activation(out=hr.ap[:, :nt], in_=P.ap[:, :nt], func=AF.Relu), [P], [hr])
                    S.pool(lambda e: e.tensor_tensor(out=hT.ap[:, j, :nt], in0=hr.ap[:, :nt], in1=hr.ap[:, :nt], op=ALU.mult),
                           [hr], [hT])
                if pend is not None:
                    emit_down(pend)
                pend = (hT, wd_t, wd, tiles)
        emit_down(pend)

    def decl_even(self):
        I = self.inp
        O = self.outp
        self.norm_mix_even = I("norm_mix_even", [2, D])
        self.w_in_even = I("w_in_even", [2, D, IN_EVEN])
        self.s5_a_re = I("s5_a_re", [2, 32, 64])
        self.s5_a_im = I("s5_a_im", [2, 32, 64])
        self.s5_log_dt = I("s5_log_dt", [2, 32])
        self.s5_b_re = I("s5_b_re", [2, 32, 64, 16])
        self.s5_b_im = I("s5_b_im", [2, 32, 64, 16])
        self.s5_c_re = I("s5_c_re", [2, 32, 16, 64])
        self.s5_c_im = I("s5_c_im", [2, 32, 16, 64])
        self.s5_d = I("s5_d", [2, 512])
        self.s5_glu_w = I("s5_glu_w", [2, 32, 16, 32])
        self.s5_glu_b = I("s5_glu_b", [2, 32, 32])
        self.ssd_conv_w = I("ssd_conv_w", [2, 4, 2048])
        self.ssd_conv_b = I("ssd_conv_b", [2, 2048])
        self.ssd_dt_bias = I("ssd_dt_bias", [2, 16])
        self.ssd_a_log = I("ssd_a_log", [2, 16])
        self.ssd_d = I("ssd_d", [2, 16])
        self.ssd_norm = I("ssd_norm", [2, 1024])
        self.w_out_even = I("w_out_even", [2, 1536, 1024])
        self.state_s5 = I("state_s5", [2, 16, 32, 64, 2])
        self.state_ssd = I("state_ssd", [2, 16, 16, 64, 128])
        self.state_conv = I("state_conv", [2, 16, 3, 2048])
        self.o_s5p = O("s5_prompt", [2, 32, 64, 2])
        self.o_s5s = O("s5_sample", [2, 16, 32, 64, 2])
        self.o_ssdp = O("ssd_prompt", [2, 16, 64, 128])
        self.o_ssds = O("ssd_sample", [2, 16, 16, 64, 128])
        self.o_convp = O("conv_prompt", [2, 3, 2048])
        self.o_convs = O("conv_sample", [2, 16, 3, 2048])

    def tt(self, eng, out, a, b, op, R, W):
        return self.S.op(eng, lambda e: e.tensor_tensor(out=out, in0=a, in1=b, op=op), R, W)

    def ts(self, eng, out, a, s1, op0, R, W, s2=None, op1=None):
        if op1 is None:
            return self.S.op(eng, lambda e: e.tensor_scalar(out=out, in0=a, scalar1=s1, scalar2=None, op0=op0), R, W)
        return self.S.op(eng, lambda e: e.tensor_scalar(out=out, in0=a, scalar1=s1, scalar2=s2, op0=op0, op1=op1), R, W)

    def stt(self, eng, out, a, sc, b, op0, op1, R, W):
        return self.S.op(eng, lambda e: e.scalar_tensor_tensor(out=out, in0=a, scalar=sc, in1=b, op0=op0, op1=op1), R, W)

    def actf(self, out, a, func, R, W, bias=None, scale=None):
        kw = {}
        if bias is not None:
            kw["bias"] = bias
        if scale is not None:
            kw["scale"] = scale
        return self.S.act(lambda e: e.activation(out=out, in_=a, func=func, **kw), R, W)

    def mm(self, out, lhsT, rhs, R, W, start=True, stop=True):
        return self.S.pe(lambda e: e.matmul(out, lhsT, rhs, start=start, stop=stop), R, W)

    def tr(self, out, in_, ident, R, W):
        return self.S.pe(lambda e: e.transpose(out, in_, ident), R, W)

    def sincos(self, x, x_t, sin_out, cos_out, out_t, scr):
        dve = self.S.dve_e
        kf, ki, y = scr
        self.ts(dve, kf.ap, x, 1.0 / TWO_PI, ALU.mult, [x_t], [kf])
        self.S.dve(lambda e: e.tensor_copy(ki.ap, kf.ap), [kf], [ki])
        self.S.dve(lambda e: e.tensor_copy(kf.ap, ki.ap), [ki], [kf])
        for shift, o in ((0.0, sin_out), (math.pi / 2, cos_out)):
            self.stt(dve, y.ap, kf.ap, -TWO_PI, x, ALU.mult, ALU.add, [kf, x_t], [y])
            if shift:
                self.ts(dve, y.ap, y.ap, shift, ALU.add, [y], [y])
            for thr, opc, add in ((math.pi, ALU.is_gt, -TWO_PI), (-math.pi, ALU.is_lt, TWO_PI)):
                self.ts(dve, ki.ap.bitcast(F32), y.ap, thr, opc, [y], [ki])
                self.stt(dve, y.ap, ki.ap.bitcast(F32), add, y.ap, ALU.mult, ALU.add, [ki, y], [y])
            self.actf(o, y.ap, AF.Sin, [y], [out_t])

    def even_layer(self, l):
        S = self.S
        ii = l // 2
        dve, pool = S.dve_e, S.pool_e
        self.arena_reset()
        A = self.av
        P, Pb = self.P, self.Pb
        W = [A("W%d" % i, [128, 1024], F32) for i in range(6)]
        Wb = [w.ap.bitcast(BF16) for w in W]
        w0off = 0
        Rbig = self.arena[:, 0:4096].bitcast(F32)
        uT = A("uT", [128, 4, 256], BF16)
        zs = A("zs", [128, 2, 1024], BF16)
        xbcT = A("xbcT", [128, 16, 259], BF16)
        carry = A("carry", [128, 16, 3], BF16)
        dtr = A("dtr", [128, 2, 16], F32)
        dtT = A("dtT", [16, 64], F32)
        yaT = A("yaT", [128, 4, 256], BF16)
        ybT = A("ybT", [128, 8, 256], BF16)
        xnT = ybT
        Ecos = A("Ecos", [128, 16, 128], BF16)
        Esin = A("Esin", [128, 16, 128], BF16)
        Blre = A("Blre", [128, 16, 128], BF16)
        Blim = A("Blim", [128, 16, 128], BF16)
        Clre = A("Clre", [128, 16, 128], BF16)
        Clim = A("Clim", [128, 16, 128], BF16)
        Gv = A("Gv", [128, 4, 128], BF16)
        Gg = A("Gg", [128, 4, 128], BF16)
        gbr = A("gbr", [1, 4, 2, 128], BF16)
        sm = A("sm", [128, 16, 16], F32)
        hp = A("hp", [128, 2, 16], F32)
        HT = A("HT", [128, 1024], F32)
        hs = S.alias("hs", HT, HT.ap[:, 0:512].rearrange("p (b g r) -> p b g r", b=16, g=16))
        HTb = A("HTb", [128, 1024], BF16)
        ssdc = A("ssdc", [128, 6, 16], F32)
        cw = A("cw", [128, 16, 4], F32)
        cb = A("cb", [128, 16], F32)
        gncol = A("gncol", [128, 8], F32)
        dcol = A("dcol", [128, 4], F32)
        wdt = A("wdt", [128, 8, 16], BF16)
        dg = [A("dg%d" % i, [128, 128], BF16) for i in range(8)]
        tk = A("tk", [128, 8, 16], F32)
        xtok = A("xtok", [128, 1024], BF16)
        btok = A("btok", [128, 512], BF16)
        xdt = A("xdt", [128, 1024], BF16)
        xdd = A("xdd", [128, 1024], BF16)
        cbT = A("cbT", [128, 512], BF16)
        yn = A("yn", [128, 1024], BF16)
        hre = S.alias("hre", xtok, xtok.ap.rearrange("p (a b) -> p a b", a=8))
        him = S.alias("him", yn, yn.ap.rearrange("p (a b) -> p a b", a=8))
        ztk = S.alias("ztk", yn, yn.ap[:4, :])
        clast = S.alias("clast", W[4], W[4].ap[:, 0:768].rearrange("p (c b k) -> p c b k", c=16, b=16))
        hnat = S.alias("hnat", W[5], W[5].ap.rearrange("p (a b) -> p a b", a=8))

        SM = lambda j: sm.ap[:, j, :]
        ARE, AIM, DT, ZR, ZI, RC, SN, CS, LR, LI, FRE, FIM, T0, T1, T2, T3 = range(16)

        S.dma(sm, SM(ARE), self.s5_a_re[ii].rearrange("(gp g2) p -> (g2 p) gp", g2=2))
        S.dma(sm, SM(AIM), self.s5_a_im[ii].rearrange("(gp g2) p -> (g2 p) gp", g2=2))
        ldv = self.s5_log_dt[ii:ii + 1, :].rearrange("o (gp g2) -> o g2 gp", g2=2)
        for g2 in range(2):
            S.dma(sm, sm.ap[g2 * 64:(g2 + 1) * 64, DT, :], ldv[:, g2, :].partition_broadcast(64))
        self.actf(SM(DT), SM(DT), AF.Exp, [sm], [sm])
        self.tt(dve, SM(ZR), SM(ARE), SM(DT), ALU.mult, [sm], [sm])
        self.tt(dve, SM(ZI), SM(AIM), SM(DT), ALU.mult, [sm], [sm])
        self.actf(SM(RC), SM(ZR), AF.Exp, [sm], [sm])
        scr_s = (S.view("s1", W[0].ap[:, 0:16]), S.view("s2", W[0].ap[:, 16:32].bitcast(I32)), S.view("s3", W[0].ap[:, 32:48]))
        self.sincos(SM(ZI), sm, SM(SN), SM(CS), sm, scr_s)
        self.tt(dve, SM(LR), SM(RC), SM(CS), ALU.mult, [sm], [sm])
        self.tt(dve, SM(LI), SM(RC), SM(SN), ALU.mult, [sm], [sm])
        self.tt(dve, SM(T0), SM(ARE), SM(ARE), ALU.mult, [sm], [sm])
        self.tt(dve, SM(T1), SM(AIM), SM(AIM), ALU.mult, [sm], [sm])
        self.tt(dve, SM(T0), SM(T0), SM(T1), ALU.add, [sm], [sm])
        S.dve(lambda e: e.reciprocal(out=SM(T0), in_=SM(T0)), [sm], [sm])
        self.ts(dve, SM(T1), SM(LR), -1.0, ALU.add, [sm], [sm])
        self.tt(dve, SM(T2), SM(T1), SM(ARE), ALU.mult, [sm], [sm])
        self.tt(dve, SM(T3), SM(LI), SM(AIM), ALU.mult, [sm], [sm])
        self.tt(dve, SM(T2), SM(T2), SM(T3), ALU.add, [sm], [sm])
        self.tt(dve, SM(FRE), SM(T2), SM(T0), ALU.mult, [sm], [sm])
        self.tt(dve, SM(T2), SM(LI), SM(ARE), ALU.mult, [sm], [sm])
        self.tt(dve, SM(T3), SM(T1), SM(AIM), ALU.mult, [sm], [sm])
        self.tt(dve, SM(T2), SM(T2), SM(T3), ALU.subtract, [sm], [sm])
        self.tt(dve, SM(FIM), SM(T2), SM(T0), ALU.mult, [sm], [sm])
        iot = S.view("iot", W[5].ap[:, 0:128])
        ioti = S.view("ioti", W[5].ap[:, 128:256].bitcast(I32))
        S.pool(lambda e: e.iota(ioti.ap, pattern=[[1, 128]], base=1, channel_multiplier=0), [], [ioti])
        S.dve(lambda e: e.tensor_copy(iot.ap, ioti.ap), [ioti], [iot])
        v3 = lambda t: t.ap.rearrange("p (a b) -> p a b", a=8)
        S.barrier()
        ang = S.view("ang", v3(W[0]))
        scr = (S.view("k1", v3(W[1])), S.view("k2", v3(W[2]).bitcast(I32)), S.view("k3", v3(W[3])))
        for half in range(2):
            g0 = half * 8
            self.tt(dve, ang.ap, sm.ap[:, ZI, g0:g0 + 8].unsqueeze(2).to_broadcast([128, 8, 128]),
                    iot.ap.unsqueeze(1).to_broadcast([128, 8, 128]), ALU.mult, [sm, iot], [ang, W[0]])
            self.sincos(ang.ap, ang, Esin.ap[:, g0:g0 + 8, :], Ecos.ap[:, g0:g0 + 8, :], Esin, scr)
        S.barrier()
        bre = S.view("bre", W[4].ap[:, 0:256].rearrange("p (a b) -> p a b", a=16))
        bim = S.view("bim", W[4].ap[:, 256:512].rearrange("p (a b) -> p a b", a=16))
        bbr = S.view("bbr", W[4].ap[:, 512:768].rearrange("p (a b) -> p a b", a=16))
        bbi = S.view("bbi", W[5].ap[:, 256:512].rearrange("p (a b) -> p a b", a=16))
        btmp = S.view("btmp", W[5].ap[:, 512:768].rearrange("p (a b) -> p a b", a=16))
        S.dma(bre, bre.ap, self.s5_b_re[ii].rearrange("(gp g2) p c -> (g2 p) gp c", g2=2))
        S.dma(bim, bim.ap, self.s5_b_im[ii].rearrange("(gp g2) p c -> (g2 p) gp c", g2=2))
        fre_b = sm.ap[:, FRE, :].unsqueeze(2).to_broadcast([128, 16, 16])
        fim_b = sm.ap[:, FIM, :].unsqueeze(2).to_broadcast([128, 16, 16])
        self.tt(dve, bbr.ap, bre.ap, fre_b, ALU.mult, [bre, sm], [bbr])
        self.tt(dve, btmp.ap, bim.ap, fim_b, ALU.mult, [bim, sm], [btmp])
        self.tt(dve, bbr.ap, bbr.ap, btmp.ap, ALU.subtract, [bbr, btmp], [bbr])
        self.tt(dve, bbi.ap, bim.ap, fre_b, ALU.mult, [bim, sm], [bbi])
        self.tt(dve, btmp.ap, bre.ap, fim_b, ALU.mult, [bre, sm], [btmp])
        self.tt(dve, bbi.ap, bbi.ap, btmp.ap, ALU.add, [bbi, btmp], [bbi])
        BBv = [S.view("BB%d" % i, W[i].ap) for i in range(2)]
        for (src, dstB) in ((bbr, Blre), (bbi, Blim)):
            for half in range(2):
                BB = BBv[half]
                S.pool(lambda e: e.memset(BB.ap, 0.0), [], [BB])
                BB6 = BB.ap.rearrange("q (cc gl a b c) -> q cc gl a b c", cc=2, gl=4, a=4, b=2)
                s5v = src.ap.rearrange("q (cc gl) c -> q cc gl c", gl=4)
                for gl in range(4):
                    for g2 in range(2):
                        S.pool(lambda e: e.tensor_copy(BB6[g2 * 64:(g2 + 1) * 64, :, gl, gl, g2, :],
                                                       s5v[g2 * 64:(g2 + 1) * 64, half * 2:half * 2 + 2, gl, :]), [src], [BB])
                for j in range(2):
                    PP = P[half * 2 + j]
                    for k in range(4):
                        self.tr(PP.ap[:, k * 128:(k + 1) * 128], BB.ap[:, (j * 4 + k) * 128:(j * 4 + k + 1) * 128], self.ident.ap, [BB, self.ident], [PP])
                    gp0 = half * 8 + j * 4
                    self.actf(dstB.ap[:, gp0:gp0 + 4, :], PP.ap.rearrange("p (a b) -> p a b", a=4), AF.Copy, [PP], [dstB])
        S.barrier()
        for (csrc, dstC, sgn) in ((self.s5_c_re, Clre, 1.0), (self.s5_c_im, Clim, -1.0)):
            cn = S.view("cn", Rbig[:, 0:512].rearrange("p (a b c) -> p a b c", a=4, b=2))
            for dup in range(2):
                S.dma(cn, cn.ap[:, :, dup, :], csrc[ii].rearrange("(cc g8) c p -> (g8 c) cc p", g8=8))
            S.pool(lambda e: e.memset(dstC.ap, 0.0), [], [dstC])
            PP = P[4]
            for cc in range(4):
                self.tr(PP.ap[:, cc * 128:(cc + 1) * 128], cn.ap[:, cc, :, :].rearrange("p a b -> p (a b)"), self.ident.ap, [cn, self.ident], [PP])
            for cc in range(4):
                for gl in range(4):
                    for g2 in range(2):
                        col = (2 * gl + g2) * 16
                        self.actf(dstC.ap[g2 * 64:(g2 + 1) * 64, cc * 4 + gl, col:col + 16],
                                  PP.ap[g2 * 64:(g2 + 1) * 64, cc * 128 + col:cc * 128 + col + 16], AF.Copy, [PP], [dstC], scale=sgn)
            S.barrier()
        gwn = S.view("gwn", W[2].ap[:, 0:128].rearrange("p (a b) -> p a b", a=4))
        bd = S.view("bd", W[2].ap[:, 128:256].rearrange("p (a b) -> p a b", a=8))
        S.dma(gwn, gwn.ap, self.s5_glu_w[ii].rearrange("(cc g8) c e -> (g8 c) cc e", g8=8))
        S.pool(lambda e: e.memset(bd.ap, 1.0), [], [bd])
        S.pool(lambda e: e.affine_select(out=bd.ap, in_=bd.ap, compare_op=ALU.is_ge, fill=0.0, base=0,
                                         pattern=[[-16, 8], [0, 16]], channel_multiplier=1), [bd], [bd])
        S.pool(lambda e: e.affine_select(out=bd.ap, in_=bd.ap, compare_op=ALU.is_ge, fill=0.0, base=15,
                                         pattern=[[16, 8], [0, 16]], channel_multiplier=-1), [bd], [bd])
        for cc in range(4):
            self.tt(dve, Gv.ap[:, cc, :].rearrange("p (a b) -> p a b", a=8), bd.ap,
                    gwn.ap[:, cc, 0:16].unsqueeze(1).to_broadcast([128, 8, 16]), ALU.mult, [bd, gwn], [Gv])
            self.tt(dve, Gg.ap[:, cc, :].rearrange("p (a b) -> p a b", a=8), bd.ap,
                    gwn.ap[:, cc, 16:32].unsqueeze(1).to_broadcast([128, 8, 16]), ALU.mult, [bd, gwn], [Gg])
        gbf = S.view("gbf", W[3].ap[0:1, 0:1024].rearrange("p (a b) -> p a b", a=32))
        S.dma(gbf, gbf.ap, self.s5_glu_b[ii:ii + 1, :, :])
        for vg in range(2):
            S.dve(lambda e: e.tensor_copy(gbr.ap[:, :, vg, :].rearrange("p a (b c) -> p a b c", b=8),
                                          gbf.ap.rearrange("p (a b) c -> p a b c", a=4)[:, :, :, vg * 16:(vg + 1) * 16]), [gbf], [gbr])
        S.dma(dcol, dcol.ap, self.s5_d[ii:ii + 1, :].rearrange("o (cc q) -> q (o cc)", q=128))
        S.dma(ssdc, ssdc.ap[:, 0, :], self.ssd_dt_bias[ii:ii + 1, :].partition_broadcast(128))
        S.dma(ssdc, ssdc.ap[:, 1, :], self.ssd_a_log[ii:ii + 1, :].partition_broadcast(128))
        S.dma(ssdc, ssdc.ap[:, 2, :], self.ssd_d[ii:ii + 1, :].partition_broadcast(128))
        self.actf(ssdc.ap[:, 1, :], ssdc.ap[:, 1, :], AF.Exp, [ssdc], [ssdc])
        self.ts(dve, ssdc.ap[:, 1, :], ssdc.ap[:, 1, :], -1.0, ALU.mult, [ssdc], [ssdc])
        for k in range(4):
            S.dma(cw, cw.ap[:, :, k], self.ssd_conv_w[ii, k:k + 1, :].rearrange("o (c q) -> q (o c)", q=128))
        S.dma(cb, cb.ap, self.ssd_conv_b[ii:ii + 1, :].rearrange("o (c q) -> q (o c)", q=128))
        S.dma(gncol, gncol.ap, self.ssd_norm[ii:ii + 1, :].rearrange("o (c q) -> q (o c)", q=128))
        win = self.w_in_even[ii].rearrange("(c p) n -> p c n", p=128)
        S.dma(wdt, wdt.ap, win[:, :, 3584:3600], q="pool")
        wout = self.w_out_even[ii].rearrange("(c p) n -> p c n", p=128)
        self.load_gb(self.norm_mix_even[ii:ii + 1, :])
        S.barrier()
        self._ev = dict(locals())
        for sc in range(9):
            self.even_superchunk(l, sc)

    def even_superchunk(self, l, sc):
        S = self.S
        E = self._ev
        ii = l // 2
        dve, pool = S.dve_e, S.pool_e
        P, Pb = self.P, self.Pb
        W, Wb, Rbig = E["W"], E["Wb"], E["Rbig"]
        xnT, uT, zs, xbcT, carry, dtr, dtT, yaT, ybT = (E[k] for k in ("xnT", "uT", "zs", "xbcT", "carry", "dtr", "dtT", "yaT", "ybT"))
        cw, cb, wdt, dg, clast, win, wout = (E[k] for k in ("cw", "cb", "wdt", "dg", "clast", "win", "wout"))
        HT, HTb, hnat, hp, tk, ztk = (E[k] for k in ("HT", "HTb", "hnat", "hp", "tk", "ztk"))
        sample = (sc == 8)
        ntok = 64 if sample else 256
        tiles = [16] if sample else [2 * sc, 2 * sc + 1]
        xv = xbcT.ap[:, :, 0:112].rearrange("p c (b k) -> p c b k", k=7)
        zT = zs.ap[:, 0, 0:512].rearrange("p (c n) -> p c n", c=8)
        for j, t in enumerate(tiles):
            self.norm_T(t, xnT.ap, xnT, j * 128, pbase=(j % 2) * 2)
        if sample:
            stc = Rbig[:48, :]
            S.dma_fn(lambda e: e.dma_start(out=stc, in_=self.state_conv[ii].rearrange("b k c -> (b k) c")), [], [W[0], W[1]])
            for half in range(2):
                PP = P[half]
                for c8 in range(8):
                    c = half * 8 + c8
                    self.tr(PP.ap[:, c8 * 48:(c8 + 1) * 48], stc[:, c * 128:(c + 1) * 128], self.ident.ap[:48, :48], [W[0], W[1], self.ident], [PP])
                self.actf(xv[:, half * 8:half * 8 + 8, :, 0:3], PP.ap[:, 0:384].rearrange("p (c b k) -> p c b k", c=8, b=16), AF.Copy, [PP], [xbcT])
        elif sc == 0:
            S.pool(lambda e: e.memset(xbcT.ap[:, :, 0:3], 0.0), [], [xbcT])
            S.pool(lambda e: e.memset(HT.ap, 0.0), [], [HT])
            S.pool(lambda e: e.memset(HTb.ap, 0.0), [], [HTb])
            S.pool(lambda e: e.memset(hp.ap, 0.0), [], [hp])
        else:
            S.pool(lambda e: e.tensor_copy(xbcT.ap[:, :, 0:3], carry.ap), [carry], [xbcT])
        wt, w = self.load_w(win[:, :, 0:512])
        for m in range(4):
            PP = P[m % 2]
            for k in range(8):
                self.mm(PP.ap[:, :ntok], w[:, k, m * 128:(m + 1) * 128], xnT.ap[:, k, :ntok], [wt, xnT], [PP], start=(k == 0), stop=(k == 7))
            self.actf(uT.ap[:, m, :ntok], PP.ap[:, :ntok], AF.Copy, [PP], [uT])
        for piece in range(2):
            wt, w = self.load_w(win[:, :, 512 + piece * 512:1024 + piece * 512])
            if not sample:
                for j in range(2):
                    PP = P[2 + j]
                    for k in range(8):
                        self.mm(PP.ap[:, :], xnT.ap[:, k, j * 128:(j + 1) * 128], w[:, k, :], [wt, xnT], [PP], start=(k == 0), stop=(k == 7))
                    self.actf(zs.ap[:, j, piece * 512:(piece + 1) * 512], PP.ap[:, :], AF.Silu, [PP], [zs])
            else:
                for m in range(4):
                    PP = P[2 + m % 2]
                    for k in range(8):
                        self.mm(PP.ap[:, :64], w[:, k, m * 128:(m + 1) * 128], xnT.ap[:, k, :64], [wt, xnT], [PP], start=(k == 0), stop=(k == 7))
                    self.actf(zT[:, piece * 4 + m, :], PP.ap[:, :64], AF.Silu, [PP], [zs])
        for piece in range(4):
            wt, w = self.load_w(win[:, :, 1536 + piece * 512:2048 + piece * 512])
            for m in range(4):
                c = piece * 4 + m
                PP = P[4 + m % 2]
                for k in range(8):
                    self.mm(PP.ap[:, :ntok], w[:, k, m * 128:(m + 1) * 128], xnT.ap[:, k, :ntok], [wt, xnT], [PP], start=(k == 0), stop=(k == 7))
                if sample:
                    pv = PP.ap[:, :64].rearrange("p (b k) -> p b k", k=4)
                    self.actf(xv[:, c, :, 3:7], pv, AF.Copy, [PP], [xbcT])
                    self.actf(clast.ap[:, c, :, :], pv[:, :, 1:4], AF.Copy, [PP], [clast])
                else:
                    self.actf(xbcT.ap[:, c, 3:3 + 256], PP.ap[:, :256], AF.Copy, [PP], [xbcT])
                    if sc == 7:
                        self.actf(clast.ap[:, c, 0, :], PP.ap[:, 253:256], AF.Copy, [PP], [clast])
        if sc == 7 or sample:
            nr = 48 if sample else 3
            cl2 = clast.ap.rearrange("p c b k -> p c (b k)")
            cout = Rbig[:nr, :]
            for half in range(2):
                PP = P[half]
                for c8 in range(8):
                    c = half * 8 + c8
                    self.tr(PP.ap[:nr, c8 * 64:c8 * 64 + 64].bitcast(F32)[:, 0:64] if False else PP.ap[:nr, c8 * 128:(c8 + 1) * 128] if c8 < 4 else P[2 + half].ap[:nr, (c8 - 4) * 128:(c8 - 3) * 128],
                            cl2[:, c, 0:nr], self.ident.ap, [clast, self.ident], [PP, P[2 + half]])
                self.actf(cout[:, half * 1024:half * 1024 + 512], PP.ap[:nr, :], AF.Copy, [PP], [W[0], W[1]])
                self.actf(cout[:, half * 1024 + 512:half * 1024 + 1024], P[2 + half].ap[:nr, :], AF.Copy, [P[2 + half]], [W[0], W[1]])
            dst = self.o_convs[ii].rearrange("b k c -> (b k) c") if sample else self.o_convp[ii]
            S.dma_fn(lambda e: e.dma_start(out=dst, in_=cout), [W[0], W[1]], [])
            S.out_tokens.append(None)
        P7 = P[7]
        if not sample:
            for j in range(2):
                for k in range(8):
                    self.mm(P7.ap[:, j * 16:(j + 1) * 16], xnT.ap[:, k, j * 128:(j + 1) * 128], wdt.ap[:, k, :], [wdt, xnT], [P7], start=(k == 0), stop=(k == 7))
            self.actf(dtr.ap.rearrange("p a b -> p (a b)"), P7.ap[:, 0:32], AF.Copy, [P7], [dtr])
        else:
            for k in range(8):
                self.mm(P7.ap[:16, :64], wdt.ap[:, k, :], xnT.ap[:, k, :64], [wdt, xnT], [P7], start=(k == 0), stop=(k == 7))
            self.actf(dtT.ap, P7.ap[:16, :64], AF.Copy, [P7], [dtT])
        if not sample:
            S.pool(lambda e: e.tensor_copy(carry.ap, xbcT.ap[:, :, 256:259]), [xbcT], [carry])
        for c in range(16):
            PP = P[c % 2]
            dgc = dg[(c % 2) * 4:(c % 2) * 4 + 4]
            for k in range(4):
                self.actf(dgc[k].ap, self.identb.ap, AF.Copy, [self.identb, cw], [dgc[k]], scale=cw.ap[:, c, k:k + 1])
            for k in range(4):
                if sample:
                    self.mm(PP.ap[:, :64].rearrange("p (b k) -> p b k", k=4), dgc[k].ap, xv[:, c, :, k:k + 4], [dgc[k], xbcT], [PP], start=(k == 0), stop=(k == 3))
                else:
                    self.mm(PP.ap[:, :256], dgc[k].ap, xbcT.ap[:, c, k:k + 256], [dgc[k], xbcT], [PP], start=(k == 0), stop=(k == 3))
            if sample:
                self.actf(xv[:, c, :, 3:7], PP.ap[:, :64].rearrange("p (b k) -> p b k", k=4), AF.Silu, [PP, cb], [xbcT], bias=cb.ap[:, c:c + 1])
            else:
                self.actf(xbcT.ap[:, c, 3:259], PP.ap[:, :256], AF.Silu, [PP, cb], [xbcT], bias=cb.ap[:, c:c + 1])
        if "s5" in self.parts:
            if not sample:
                for ch in range(2):
                    self.s5_chunk(ch * 128)
            else:
                self.s5_sample(ii)
        else:
            S.pool(lambda e: e.memset(yaT.ap, 0.0), [], [yaT])
        if "ssd" in self.parts:
            if not sample:
                for ch in range(2):
                    t0 = ch * 128
                    self.ssd_chunk(128, lambda c, t0=t0: xbcT.ap[:, c, 3 + t0:3 + t0 + 128], dtr.ap[:, ch, :], dtr, zs.ap[:, ch, :], zs, t0)
            else:
                for b in range(16):
                    S.dma(hnat, hnat.ap, self.state_ssd[ii, b].rearrange("(hp h2) p n -> (h2 p) hp n", h2=2))
                    for half in range(2):
                        PP = P[4 + half]
                        for k in range(4):
                            self.tr(PP.ap[:, k * 128:(k + 1) * 128], hnat.ap[:, half * 4 + k, :], self.ident.ap, [hnat, self.ident], [PP])
                        self.actf(HT.ap[:, half * 512:(half + 1) * 512], PP.ap, AF.Copy, [PP], [HT])
                    self.actf(HTb.ap, HT.ap, AF.Copy, [HT], [HTb])
                    for c in range(8):
                        self.tr(Pb[6][:4, c * 128:(c + 1) * 128], zT[:, c, 4 * b:4 * b + 4], self.identb.ap, [zs, self.identb], [P[6]])
                    self.actf(ztk.ap, Pb[6][:4, :], AF.Copy, [P[6]], [ztk])
                    self.tr(P[7].ap[:4, 32:48], dtT.ap[:16, 4 * b:4 * b + 4], self.ident.ap[:16, :16], [dtT, self.ident], [P[7]])
                    self.actf(tk.ap[:4, 6, :], P[7].ap[:4, 32:48], AF.Copy, [P[7]], [tk])
                    self.ssd_chunk(4, lambda c, b=b: xv[:, c, b, 3:7], tk.ap[:4, 6, :], tk, ztk.ap, ztk, 4 * b)
                    self.ssd_state_out(self.o_ssds[ii, b])
        else:
            S.pool(lambda e: e.memset(ybT.ap, 0.0), [], [ybT])
        pcs = [self.load_w(wout[:, j * 4:(j + 1) * 4, :]) for j in range(3)]
        for j, t in enumerate(tiles):
            r = rows(t)
            for half in range(2):
                PP = P[2 * j + half]
                for k in range(12):
                    lhsT = yaT.ap[:, k, j * 128:j * 128 + r] if k < 4 else ybT.ap[:, k - 4, j * 128:j * 128 + r]
                    pt, pw = pcs[k // 4]
                    self.mm(PP.ap[:r, :], lhsT, pw[:, k % 4, half * 512:(half + 1) * 512], [yaT, ybT, pt], [PP], start=(k == 0), stop=(k == 11))
                xs_ = self.X.ap[:r, t, half * 512:(half + 1) * 512]
                self.tt(dve, xs_, xs_, PP.ap[:r, :], ALU.add, [PP, self.xt[t]], [self.xt[t]])
        if sc == 7:
            if "s5" in self.parts:
                for r_ in range(2):
                    S.dma_out(self.o_s5p[ii, :, :, r_].rearrange("(gp g2) p -> (g2 p) gp", g2=2), hp.ap[:, r_, :], hp)
            if "ssd" in self.parts:
                self.ssd_state_out(self.o_ssdp[ii])

    def ssd_state_out(self, dram_ap):
        S = self.S
        E = self._ev
        HT, hnat = E["HT"], E["hnat"]
        P = self.P
        for half in range(2):
            PP = P[4 + half]
            for k in range(4):
                hpi = half * 4 + k
                self.tr(PP.ap[:, k * 128:(k + 1) * 128], HT.ap[:, hpi * 128:(hpi + 1) * 128], self.ident.ap, [HT, self.ident], [PP])
            self.actf(hnat.ap[:, half * 4:half * 4 + 4, :], PP.ap.rearrange("p (a b) -> p a b", a=4), AF.Copy, [PP], [hnat])
        S.dma_out(dram_ap.rearrange("(hp h2) p n -> (h2 p) hp n", h2=2), hnat.ap, hnat)

    def s5_tail(self, cc, rhs, T, t0):
        E = self._ev
        S = self.S
        dve = S.dve_e
        P = self.P
        Clre, Clim, Gv, Gg, gbr, dcol, uT, yaT, xdt, xdd = (E[k] for k in ("Clre", "Clim", "Gv", "Gg", "gbr", "dcol", "uT", "yaT", "xdt", "xdd"))
        hre, him = E["hre"], E["him"]
        PY = P[4 + cc % 2]
        for jj, (gp, hr_ap, hi_ap) in enumerate(rhs):
            self.mm(PY.ap[:, :T], Clre.ap[:, gp, :], hr_ap, [Clre, hre], [PY], start=(jj == 0), stop=False)
            self.mm(PY.ap[:, :T], Clim.ap[:, gp, :], hi_ap, [Clim, him], [PY], start=False, stop=(jj == 3))
        ysb = xdt.ap[:, (cc % 2) * 128:(cc % 2) * 128 + T]
        self.stt(dve, ysb, uT.ap[:, cc, t0:t0 + T], dcol.ap[:, cc:cc + 1], PY.ap[:, :T], ALU.mult, ALU.add, [uT, dcol, PY], [xdt])
        PV, PG = P[6], P[7]
        self.mm(PV.ap[:, :T], Gv.ap[:, cc, :], ysb, [Gv, xdt], [PV], start=True, stop=False)
        self.mm(PV.ap[:, :T], gbr.ap[0:1, cc, 0, :], self.onesb.ap[0:1, :T], [gbr, self.onesb], [PV], start=False, stop=True)
        self.mm(PG.ap[:, :T], Gg.ap[:, cc, :], ysb, [Gg, xdt], [PG], start=True, stop=False)
        self.mm(PG.ap[:, :T], gbr.ap[0:1, cc, 1, :], self.onesb.ap[0:1, :T], [gbr, self.onesb], [PG], start=False, stop=True)
        sg = xdd.ap[:, (cc % 2) * 128:(cc % 2) * 128 + T]
        self.actf(sg, PG.ap[:, :T], AF.Sigmoid, [PG], [xdd])
        self.tt(dve, yaT.ap[:, cc, t0:t0 + T], PV.ap[:, :T], sg, ALU.mult, [PV, xdd], [yaT])

    def s5_chunk(self, t0):
        E = self._ev
        S = self.S
        dve, pool = S.dve_e, S.pool_e
        P = self.P
        W, sm, hp, hre, him, uT = (E[k] for k in ("W", "sm", "hp", "hre", "him", "uT"))
        Ecos, Esin, Blre, Blim = (E[k] for k in ("Ecos", "Esin", "Blre", "Blim"))
        RC = E["RC"]
        v3 = lambda t: t.ap.rearrange("p (a b) -> p a b", a=8)
        for half in range(2):
            g0 = half * 8
            for j in range(8):
                gp = g0 + j
                cc = gp // 4
                self.mm(P[j // 4].ap[:, (j % 4) * 128:(j % 4 + 1) * 128], Blre.ap[:, gp, :], uT.ap[:, cc, t0:t0 + 128], [Blre, uT], [P[j // 4]])
                self.mm(P[2 + j // 4].ap[:, (j % 4) * 128:(j % 4 + 1) * 128], Blim.ap[:, gp, :], uT.ap[:, cc, t0:t0 + 128], [Blim, uT], [P[2 + j // 4]])
            bre = self.PS[:, 0:2, :].rearrange("p a (b c) -> p (a b) c", c=128)
            bim = self.PS[:, 2:4, :].rearrange("p a (b c) -> p (a b) c", c=128)
            ec, es = Ecos.ap[:, g0:g0 + 8, :], Esin.ap[:, g0:g0 + 8, :]
            self.tt(dve, v3(W[0]), bre, ec, ALU.mult, [P[0], P[1], Ecos], [W[0]])
            self.tt(dve, v3(W[1]), bim, es, ALU.mult, [P[2], P[3], Esin], [W[1]])
            self.tt(dve, v3(W[2]), bim, ec, ALU.mult, [P[2], P[3], Ecos], [W[2]])
            self.tt(dve, v3(W[3]), bre, es, ALU.mult, [P[0], P[1], Esin], [W[3]])
            self.tt(pool, W[0].ap, W[0].ap, W[1].ap, ALU.add, [W[0], W[1]], [W[0]])
            self.tt(pool, W[2].ap, W[2].ap, W[3].ap, ALU.subtract, [W[2], W[3]], [W[2]])
            for j in range(8):
                gp = g0 + j
                rb = sm.ap[:, RC, gp:gp + 1].to_broadcast([128, 128])
                S.dve(lambda e: e.tensor_tensor_scan(out=v3(W[1])[:, j, :], data0=rb, data1=v3(W[0])[:, j, :], initial=hp.ap[:, 0, gp:gp + 1],
                                                     op0=ALU.mult, op1=ALU.add), [W[0], sm, hp], [W[1]])
                S.dve(lambda e: e.tensor_tensor_scan(out=v3(W[3])[:, j, :], data0=rb, data1=v3(W[2])[:, j, :], initial=hp.ap[:, 1, gp:gp + 1],
                                                     op0=ALU.mult, op1=ALU.add), [W[2], sm, hp], [W[3]])
            self.tt(dve, v3(W[0]), v3(W[1]), ec, ALU.mult, [W[1], Ecos], [W[0]])
            self.tt(pool, v3(W[2]), v3(W[3]), es, ALU.mult, [W[3], Esin], [W[2]])
            self.tt(pool, hre.ap, v3(W[0]), v3(W[2]), ALU.subtract, [W[0], W[2]], [hre])
            self.tt(dve, v3(W[4]), v3(W[3]), ec, ALU.mult, [W[3], Ecos], [W[4]])
            self.tt(pool, v3(W[5]), v3(W[1]), es, ALU.mult, [W[1], Esin], [W[5]])
            self.tt(pool, him.ap, v3(W[4]), v3(W[5]), ALU.add, [W[4], W[5]], [him])
            self.tt(dve, hp.ap[:, 0, g0:g0 + 8], v3(W[0])[:, :, 127], v3(W[2])[:, :, 127], ALU.subtract, [W[0], W[2]], [hp])
            self.tt(dve, hp.ap[:, 1, g0:g0 + 8], v3(W[4])[:, :, 127], v3(W[5])[:, :, 127], ALU.add, [W[4], W[5]], [hp])
            for ccl in range(2):
                cc = half * 2 + ccl
                rhs = [(cc * 4 + jj, hre.ap[:, cc * 4 + jj - g0, :], him.ap[:, cc * 4 + jj - g0, :]) for jj in range(4)]
                self.s5_tail(cc, rhs, 128, t0)

    def s5_sample(self, ii):
        E = self._ev
        S = self.S
        dve, pool = S.dve_e, S.pool_e
        P = self.P
        W, sm, hs, hre, him, uT = (E[k] for k in ("W", "sm", "hs", "hre", "him", "uT"))
        Blre, Blim = E["Blre"], E["Blim"]
        LR, LI = E["LR"], E["LI"]
        for b in range(16):
            S.dma(hs, hs.ap[:, b, :, :], self.state_s5[ii, b].rearrange("(gp g2) p r -> (g2 p) gp r", g2=2))
        hsr = hs.ap[:, :, :, 0].rearrange("p b g -> p g b")
        hsi = hs.ap[:, :, :, 1].rearrange("p b g -> p g b")
        for gp in range(16):
            cc = gp // 4
            self.mm(P[gp // 8].ap[:, (gp % 8) * 64:(gp % 8 + 1) * 64], Blre.ap[:, gp, :], uT.ap[:, cc, 0:64], [Blre, uT], [P[gp // 8]])
            self.mm(P[2 + gp // 8].ap[:, (gp % 8) * 64:(gp % 8 + 1) * 64], Blim.ap[:, gp, :], uT.ap[:, cc, 0:64], [Blim, uT], [P[2 + gp // 8]])
        bre = self.PS[:, 0:2, :].rearrange("p a (g b t) -> p (a g) b t", b=16, t=4)
        bim = self.PS[:, 2:4, :].rearrange("p a (g b t) -> p (a g) b t", b=16, t=4)
        hall_re = hre.ap.rearrange("p a b -> p (a b)").rearrange("p (g b t) -> p g b t", g=16, b=16)
        hall_im = him.ap.rearrange("p a b -> p (a b)").rearrange("p (g b t) -> p g b t", g=16, b=16)
        wv = lambda i: W[i].ap[:, 0:256].rearrange("p (g b) -> p g b", g=16)
        lr = sm.ap[:, LR, :].unsqueeze(2).to_broadcast([128, 16, 16])
        li = sm.ap[:, LI, :].unsqueeze(2).to_broadcast([128, 16, 16])
        for t in range(4):
            self.tt(dve, wv(0), hsr, lr, ALU.mult, [hs, sm], [W[0]])
            self.tt(dve, wv(1), hsi, li, ALU.mult, [hs, sm], [W[1]])
            self.tt(dve, wv(2), hsi, lr, ALU.mult, [hs, sm], [W[2]])
            self.tt(dve, wv(3), hsr, li, ALU.mult, [hs, sm], [W[3]])
            self.tt(dve, wv(0), wv(0), wv(1), ALU.subtract, [W[0], W[1]], [W[0]])
            self.tt(dve, wv(2), wv(2), wv(3), ALU.add, [W[2], W[3]], [W[2]])
            self.tt(dve, hsr, wv(0), bre[:, :, :, t], ALU.add, [W[0], P[0], P[1]], [hs])
            self.tt(dve, hsi, wv(2), bim[:, :, :, t], ALU.add, [W[2], P[2], P[3]], [hs])
            self.actf(hall_re[:, :, :, t], hsr, AF.Copy, [hs], [hre])
            self.actf(hall_im[:, :, :, t], hsi, AF.Copy, [hs], [him])
        hr2 = hre.ap.rearrange("p a b -> p (a b)").rearrange("p (g n) -> p g n", g=16)
        hi2 = him.ap.rearrange("p a b -> p (a b)").rearrange("p (g n) -> p g n", g=16)
        for cc in range(4):
            rhs = [(cc * 4 + jj, hr2[:, cc * 4 + jj, :], hi2[:, cc * 4 + jj, :]) for jj in range(4)]
            self.s5_tail(cc, rhs, 64, 0)
        for b in range(16):
            S.dma_out(self.o_s5s[ii, b].rearrange("(gp g2) p r -> (g2 p) gp r", g2=2), hs.ap[:, b, :, :], hs)

    def ssd_chunk(self, T, xcv, dt_tok, dt_t, z_tok, z_t, ycol0):
        E = self._ev
        S = self.S
        dve, pool = S.dve_e, S.pool_e
        P, Pb = self.P, self.Pb
        W, Wb, Rbig = E["W"], E["Wb"], E["Rbig"]
        tk, ssdc, xbcT, xtok, btok, xdt, xdd, cbT, yn, HT, HTb, ybT, gncol = (E[k] for k in (
            "tk", "ssdc", "xbcT", "xtok", "btok", "xdt", "xdd", "cbT", "yn", "HT", "HTb", "ybT", "gncol"))
        tkv = lambda j: tk.ap[:T, j, :]
        b3 = lambda ap: ap.unsqueeze(2).to_broadcast([T, 16, 64])
        h3 = lambda ap: ap.rearrange("p (h d) -> p h d", h=16)
        self.tt(dve, tkv(0), dt_tok, ssdc.ap[:T, 0, :], ALU.add, [dt_t, ssdc], [tk])
        self.actf(tkv(0), tkv(0), AF.Exp, [tk], [tk])
        self.actf(tkv(0), tkv(0), AF.Ln, [tk], [tk], bias=1.0)
        self.tt(dve, tkv(1), tkv(0), ssdc.ap[:T, 1, :], ALU.mult, [tk, ssdc], [tk])
        self.ts(dve, tkv(5), tkv(1), -1.0, ALU.mult, [tk], [tk])
        P7 = P[7]
        self.mm(P7.ap[:T, 0:16], self.tri.ap[:T, :T], tkv(1), [self.tri, tk], [P7])
        self.mm(P7.ap[:, 16:32], self.onesf.ap[:T, :], tkv(1), [self.onesf, tk], [P7])
        self.actf(tkv(2), P7.ap[:T, 0:16], AF.Copy, [P7], [tk])
        self.actf(tkv(3), P7.ap[:T, 0:16], AF.Exp, [P7], [tk])
        self.actf(tk.ap[:, 7, :], P7.ap[:, 16:32], AF.Exp, [P7], [tk])
        self.tt(dve, tkv(4), P7.ap[:T, 16:32], tkv(2), ALU.subtract, [P7, tk], [tk])
        self.actf(tkv(4), tkv(4), AF.Exp, [tk], [tk])
        R3 = Rbig[:T, :16 * T].rearrange("p (h l) -> p h l", h=16)
        self.tt(dve, R3, tkv(1).unsqueeze(2).to_broadcast([T, 16, T]), self.tri.ap[:T, :T].unsqueeze(1).to_broadcast([T, 16, T]),
                ALU.mult, [tk, self.tri], [W[0], W[1]])
        hpm = min(16, 512 // T)
        PSLT = self.PS[:T, 0:4, :].rearrange("p a b -> p (a b)")[:, :16 * T].rearrange("p (h l) -> p h l", h=16)
        for h0 in range(0, 16, hpm):
            bank = (h0 * T) // 512
            o = PSLT[:, h0:h0 + hpm, :]
            self.mm(o, self.onesf.ap[:T, :T], R3[:, h0:h0 + hpm, :], [W[0], W[1], self.onesf], [P[bank]], start=True, stop=False)
            self.mm(o, self.tri.ap[:T, :T], tkv(5)[:, h0:h0 + hpm].unsqueeze(2).to_broadcast([T, hpm, T]), [self.tri, tk], [P[bank]], start=False, stop=False)
            self.mm(o, self.ident.ap[:T, :T], self.negm.ap[:T, :T].unsqueeze(1).to_broadcast([T, hpm, T]), [self.ident, self.negm], [P[bank]], start=False, stop=True)
        nb = (16 * T + 511) // 512
        LTb = Wb[2][:T, :16 * T]
        self.actf(LTb, self.PS[:T, 0:4, :].rearrange("p a b -> p (a b)")[:, :16 * T], AF.Exp, [P[i] for i in range(nb)], [W[2]])
        for g in range(4):
            self.mm(P[4].ap[:T, g * T:(g + 1) * T], xcv(8 + g), xcv(12 + g), [xbcT], [P[4]])
        self.actf(cbT.ap[:T, :4 * T], P[4].ap[:T, :4 * T], AF.Copy, [P[4]], [cbT])
        Mb = Wb[3][:T, :16 * T]
        self.tt(pool, Mb.rearrange("p (g r l) -> p g r l", g=4, r=4), LTb.rearrange("p (g r l) -> p g r l", g=4, r=4),
                cbT.ap[:T, :4 * T].rearrange("p (g l) -> p g l", g=4).unsqueeze(2).to_broadcast([T, 4, 4, T]), ALU.mult, [W[2], cbT], [W[3]])
        for c in range(8):
            self.tr(Pb[5][:T, c * 128:(c + 1) * 128], xcv(c), self.identb.ap, [xbcT, self.identb], [P[5]])
        self.actf(xtok.ap[:T, :], Pb[5][:T, :], AF.Copy, [P[5]], [xtok])
        for g in range(4):
            self.tr(Pb[6][:T, g * 128:(g + 1) * 128], xcv(8 + g), self.identb.ap, [xbcT, self.identb], [P[6]])
        self.actf(btok.ap[:T, :], Pb[6][:T, :512], AF.Copy, [P[6]], [btok])
        self.tt(dve, h3(xdt.ap[:T, :]), h3(xtok.ap[:T, :]), b3(tkv(0)), ALU.mult, [xtok, tk], [xdt])
        self.tt(pool, h3(xdd.ap[:T, :]), h3(xdt.ap[:T, :]), b3(tkv(4)), ALU.mult, [xdt, tk], [xdd])
        M3 = Mb.rearrange("p (h l) -> p h l", h=16)
        for h in range(16):
            self.mm(P[h // 8].ap[:T, (h % 8) * 64:(h % 8 + 1) * 64], M3[:, h, :], xdt.ap[:T, h * 64:(h + 1) * 64], [W[3], xdt], [P[h // 8]])
        for g in range(4):
            self.mm(P[2 + g // 2].ap[:T, (g % 2) * 256:(g % 2 + 1) * 256], xcv(12 + g), HTb.ap[:, g * 256:(g + 1) * 256], [xbcT, HTb], [P[2 + g // 2]])
        yo = W[4].ap[:T, :]
        tmp = W[5].ap[:T, :]
        PSO = self.PS[:T, 2:4, :].rearrange("p a b -> p (a b)")
        PSY = self.PS[:T, 0:2, :].rearrange("p a b -> p (a b)")
        self.tt(dve, h3(yo), h3(PSO), b3(tkv(3)), ALU.mult, [P[2], P[3], tk], [W[4]])
        self.tt(dve, yo, yo, PSY, ALU.add, [W[4], P[0], P[1]], [W[4]])
        self.tt(pool, h3(tmp), h3(xtok.ap[:T, :]), b3(ssdc.ap[:T, 2, :]), ALU.mult, [xtok, ssdc], [W[5]])
        self.tt(dve, yo, yo, tmp, ALU.add, [W[4], W[5]], [W[4]])
        self.tt(dve, yo, yo, z_tok, ALU.mult, [W[4], z_t], [W[4]])
        ss = self.ss[0]
        S.pool(lambda e: e.memset(ss.ap[:], 0.0), [], [ss])
        S.act(lambda e: e.activation(out=tmp, in_=yo, func=AF.Square, accum_out=ss.ap[:T, 0:1]), [W[4]], [W[5], ss])
        self.ts(dve, ss.ap[:T, 1:2], ss.ap[:T, 0:1], 1.0 / 1024, ALU.mult, [ss], [ss], s2=EPS, op1=ALU.add)
        self.actf(ss.ap[:T, 1:2], ss.ap[:T, 1:2], AF.Sqrt, [ss], [ss])
        S.dve(lambda e: e.reciprocal(out=ss.ap[:T, 1:2], in_=ss.ap[:T, 1:2]), [ss], [ss])
        self.actf(yn.ap[:T, :], yo, AF.Copy, [W[4], ss], [yn], scale=ss.ap[:T, 1:2])
        for c in range(8):
            self.tr(Pb[5][:, c * T:(c + 1) * T], yn.ap[:T, c * 128:(c + 1) * 128], self.identb.ap[:T, :T], [yn, self.identb], [P[5]])
        for c in range(8):
            self.actf(ybT.ap[:, c, ycol0:ycol0 + T], Pb[5][:, c * T:(c + 1) * T], AF.Copy, [P[5], gncol], [ybT], scale=gncol.ap[:, c:c + 1])
        for g in range(4):
            PPs = P[4] if g < 2 else P[6]
            self.mm(PPs.ap[:, (g % 2) * 256:(g % 2 + 1) * 256], btok.ap[:T, g * 128:(g + 1) * 128], xdd.ap[:T, g * 256:(g + 1) * 256], [btok, xdd], [PPs])
        self.tt(dve, h3(HT.ap), h3(HT.ap), tk.ap[:, 7, :].unsqueeze(2).to_broadcast([128, 16, 64]), ALU.mult, [HT, tk], [HT])
        self.tt(dve, HT.ap[:, 0:512], HT.ap[:, 0:512], P[4].ap, ALU.add, [HT, P[4]], [HT])
        self.tt(dve, HT.ap[:, 512:1024], HT.ap[:, 512:1024], P[6].ap, ALU.add, [HT, P[6]], [HT])
        self.actf(HTb.ap, HT.ap, AF.Copy, [HT], [HTb])


N_CORES = 8
_OUT_SHAPES = [
    ("y_prompt", (8, 2048, 1024)), ("y_sample", (128, 4, 1024)),
    ("s5_prompt", (2, 8, 32, 64, 2)), ("s5_sample", (2, 128, 32, 64, 2)),
    ("ssd_prompt", (2, 8, 16, 64, 128)), ("ssd_sample", (2, 128, 16, 64, 128)),
    ("conv_prompt", (2, 8, 3, 2048)), ("conv_sample", (2, 128, 3, 2048)),
    ("cmp_rows_prompt", (2, 8, 2048, 2, 2, 64)), ("cmp_rows_sample", (2, 128, 4, 2, 2, 64)),
    ("sel_rows_prompt", (2, 8, 2048, 2, 2, 64)), ("sel_rows_sample", (2, 128, 4, 2, 2, 64)),
    ("win_prompt", (2, 8, 512, 2, 2, 64)), ("win_sample", (2, 128, 512, 2, 2, 64)),
]
PARTS = ("mlp", "even", "s5", "ssd", "odd")


def kernel(**inputs):
    mk = MK(parts=PARTS, depth=4)
    nc = mk.build()
    f32 = lambda a: np.ascontiguousarray(np.asarray(a, dtype=np.float32))
    shared = {}
    for k in mk.din:
        if k in ("x_prompt", "x_sample", "state_s5", "state_ssd", "state_conv", "state_win_kv", "page_table"):
            continue
        if k.startswith("cache_"):
            j = int(k[-1])
            src = inputs["cache_cmp_kv"] if k.startswith("cache_cmp_kv") else inputs["cache_sel_kv"]
            shared[k] = f32(np.asarray(src)[j]).reshape(NPOOL * 128, 256)
            continue
        v = inputs[k]
        if k == "norm_final":
            v = np.asarray(v).reshape(1, D)
        shared[k] = f32(v)
    in_maps = []
    for c in range(N_CORES):
        m = dict(shared)
        b0, b1 = 16 * c, 16 * c + 16
        m["x_prompt"] = f32(inputs["x_prompt"][c])
        m["x_sample"] = f32(np.asarray(inputs["x_sample"])[b0:b1].reshape(64, D))
        for k in ("state_s5", "state_ssd", "state_conv", "state_win_kv"):
            if k in mk.din:
                a_ = np.asarray(inputs[k])[:, b0:b1]
                m[k] = f32(a_.reshape(2, 16, 512, 256) if k == "state_win_kv" else a_)
        if "page_table" in mk.din:
            m["page_table"] = np.ascontiguousarray(np.asarray(inputs["page_table"], dtype=np.int32)[b0:b1])
        in_maps.append({k: v for k, v in m.items() if k in mk.din})
    res = run_bass_kernel_spmd(nc, in_maps, core_ids=list(range(N_CORES))).results
    outs = []
    for name, shp in _OUT_SHAPES:
        if name not in mk.dout:
            outs.append(np.zeros(shp, np.float32))
            continue
        per = [np.asarray(r[name], dtype=np.float32) for r in res]
        if name == "y_prompt":
            o = np.stack(per, 0)
        elif name == "y_sample":
            o = np.concatenate([p.reshape(16, 4, D) for p in per], 0)
        elif name.endswith("_prompt"):
            o = np.stack(per, 1)
        else:
            o = np.concatenate(per, 1)
        outs.append(np.ascontiguousarray(o.reshape(shp)))
    return tuple(outs)


ROPE_THETA = 500000.0
BIG = 30000.0


def _decl_odd(self):
    I, O = self.inp, self.outp
    self.norm_mix_odd = I("norm_mix_odd", [2, D])
    self.w_in_odd = I("w_in_odd", [2, D, IN_ODD])
    self.cmp_w1 = I("cmp_w1", [2, 2, 2048, 128])
    self.cmp_w2 = I("cmp_w2", [2, 2, 128, 64])
    self.cmp_pos = I("cmp_pos", [2, 2, 32, 64])
    self.w_out_odd = I("w_out_odd", [2, 1024, 1024])
    self.cache_cmp = [I("cache_cmp_kv_%d" % j, [NPOOL * 128, 256]) for j in range(2)]
    self.cache_sel = [I("cache_sel_kv_%d" % j, [NPOOL * 128, 256]) for j in range(2)]
    self.state_win = I("state_win_kv", [2, 16, 512, 256])
    self.page_table = I("page_table", [16, 16], I32)
    self.o_cmp_p = O("cmp_rows_prompt", [2, TP, 256])
    self.o_cmp_s = O("cmp_rows_sample", [2, TS, 256])
    self.o_sel_p = O("sel_rows_prompt", [2, TP, 256])
    self.o_sel_s = O("sel_rows_sample", [2, TS, 256])
    self.o_win_p = O("win_prompt", [2, 512, 256])
    self.o_win_s = O("win_sample", [2, 16, 512, 256])
    if self.dbg:
        O("dbg_acc", [TT, 1024])


def _rope(self, v1, v2, cosb, sinb, shape, t_src, scr_t, scr):
    dve, pool = self.S.dve_e, self.S.pool_e
    a, b, c, d = scr
    self.tt(dve, a, v1, cosb, ALU.mult, [t_src], [scr_t])
    self.tt(dve, b, v2, sinb, ALU.mult, [t_src], [scr_t])
    self.tt(dve, c, v2, cosb, ALU.mult, [t_src], [scr_t])
    self.tt(dve, d, v1, sinb, ALU.mult, [t_src], [scr_t])
    self.tt(dve, v1, a, b, ALU.subtract, [scr_t], [t_src])
    self.tt(dve, v2, c, d, ALU.add, [scr_t], [t_src])


def _odd_layer(self, l):
    S = self.S
    ii = l // 2
    dve, pool = S.dve_e, S.pool_e
    self.arena_reset()
    A = self.av
    P, Pb = self.P, self.Pb
    o = {}
    self._od = o
    o["ii"] = ii
    KcT = A("KcT", [128, TT], BF16)
    VcT = A("VcT", [128, TT], BF16)
    KsT = A("KsT", [128, TT], BF16)
    KwT = A("KwT", [128, TT], BF16)
    VsT_s = A("VsT_s", [128, 64], BF16)
    VwT_s = A("VwT_s", [128, 64], BF16)
    Vs = A("Vs", [128, 17, 2, 65], BF16)
    Vw = A("Vw", [128, 17, 2, 65], BF16)
    Vp = A("Vp", [128, 17, 2, 65], BF16)
    KpT = A("KpT", [128, 2048], BF16)
    gates = A("gates", [128, 17, 48], F32)
    cs = A("ropecs", [128, 17, 16], F32)
    kc_all = A("kc_all", [128, 17, 128], BF16)
    vca_all = A("vca_all", [128, 17, 2, 64], BF16)
    w2k = A("w2k", [128, 2, 128], BF16)
    w2v = A("w2v", [128, 64], BF16)
    peT = A("peT", [128, 2, 32], BF16)
    peb = A("peb", [128, 2], F32)
    hid = A("hid", [128, 2, 128], BF16)
    Em = A("Em", [128, 17, 128], BF16)
    ncausT = A("ncausT", [128, 128], BF16)
    nwinT = A("nwinT", [128, 128], BF16)
    ncaus4 = A("ncaus4", [128, 4, 128], BF16)
    nwin4 = A("nwin4", [128, 4, 128], BF16)
    ncaus_s = A("ncaus_s", [128, 8, 4], BF16)
    nwin_s = A("nwin_s", [128, 8, 4], BF16)
    selm4 = A("selm4", [128, 512], BF16)
    qTs = A("qTs", [128, 8, 4], BF16)
    SelR = A("SelR", [4, 8, 32], BF16)
    SelT = A("SelT", [32, 8, 4], F32)
    Mw = A("Mw", [128, 256], BF16)
    Msm = A("Msm", [128, 128], BF16)
    xnt = A("xnt", [128, 8, 128], BF16)
    pr = A("pr", [128, 1024], F32)
    prb = A("prb", [128, 1024], BF16)
    rsc = A("rsc", [128, 4, 128], F32)
    qT = A("qT", [128, 8, 128], BF16)
    sc = [A("sc%d" % i, [128, 4, 128], F32) for i in range(2)]
    Pc = A("Pc", [128, 16, 128], BF16)
    PTq = [A("PTq%d" % i, [128, 4, 128], BF16) for i in range(4)]
    PT = None
    acc = A("acc", [128, 1024], F32)
    w2f = S.alias("w2f", acc, acc.ap[:, 256:384].rearrange("p (a b) -> p a b", a=2))
    peTf = S.alias("peTf", acc, acc.ap[:, 384:448].rearrange("p (a b) -> p a b", a=2))
    ob = S.alias("ob", prb, prb.ap)
    oT = A("oT", [128, 8, 128], BF16)
    sml = A("sml", [128, 16, 16], F32)
    imp = A("imp", [128, 2, 64], F32)
    psg = A("psg", [128, 2, 128], F32)
    sbias = A("sbias", [128, 64], F32)
    sel01 = A("sel01", [128, 2, 64], F32)
    mx8 = A("mx8", [128, 2, 8], F32)
    selm = None
    stg = [A("stg%d" % i, [128, 4, 256], BF16) for i in range(2)]
    idx = A("idx", [128, 16, 16], I32)
    idxf = S.alias("idxf", acc, acc.ap[:, 0:256].rearrange("p (a b) -> p a b", a=16))
    o.update(locals())

    for t_ in (Vs, Vw, Vp):
        S.pool(lambda e: e.memset(t_.ap, 1.0), [], [t_])
    S.pool(lambda e: e.memset(vca_all.ap, 0.0), [], [vca_all])
    S.pool(lambda e: e.memset(kc_all.ap, 0.0), [], [kc_all])
    S.pool(lambda e: e.memset(ncausT.ap, 0.0), [], [ncausT])
    S.pool(lambda e: e.affine_select(out=ncausT.ap, in_=ncausT.ap, compare_op=ALU.is_ge, fill=-BIG, base=0,
                                     pattern=[[1, 128]], channel_multiplier=-1), [ncausT], [ncausT])
    S.pool(lambda e: e.memset(nwinT.ap, 0.0), [], [nwinT])
    S.pool(lambda e: e.affine_select(out=nwinT.ap, in_=nwinT.ap, compare_op=ALU.is_gt, fill=-BIG, base=0,
                                     pattern=[[-1, 128]], channel_multiplier=1), [nwinT], [nwinT])
    S.pool(lambda e: e.tensor_copy(ncaus4.ap, ncausT.ap.unsqueeze(1).to_broadcast([128, 4, 128])), [ncausT], [ncaus4])
    S.pool(lambda e: e.tensor_copy(nwin4.ap, nwinT.ap.unsqueeze(1).to_broadcast([128, 4, 128])), [nwinT], [nwin4])
    S.pool(lambda e: e.memset(ncaus_s.ap, 0.0), [], [ncaus_s])
    for hh in range(2):
        nv = ncaus_s.ap[hh * 64:(hh + 1) * 64]
        S.pool(lambda e: e.affine_select(out=nv, in_=nv, compare_op=ALU.is_ge, fill=-BIG, base=0,
                                         pattern=[[0, 8], [1, 4]], channel_multiplier=-1), [ncaus_s], [ncaus_s])
    S.pool(lambda e: e.tensor_copy(nwin_s.ap, nwinT.ap[:, 0:4].unsqueeze(1).to_broadcast([128, 8, 4])), [nwinT], [nwin_s])
    S.pool(lambda e: e.memset(SelR.ap, 1.0), [], [SelR])
    S.pool(lambda e: e.affine_select(out=SelR.ap.rearrange("p i (a s) -> p i a s", a=8), in_=SelR.ap.rearrange("p i (a s) -> p i a s", a=8),
                                     compare_op=ALU.is_equal, fill=0.0, base=0, pattern=[[-4, 8], [4, 8], [1, 4]], channel_multiplier=-1), [SelR], [SelR])
    S.pool(lambda e: e.memset(SelT.ap, 1.0), [], [SelT])
    S.pool(lambda e: e.affine_select(out=SelT.ap, in_=SelT.ap, compare_op=ALU.is_equal, fill=0.0, base=0,
                                     pattern=[[-4, 8], [-1, 4]], channel_multiplier=1), [SelT], [SelT])
    S.pool(lambda e: e.memset(Mw.ap, 0.0), [], [Mw])
    S.pool(lambda e: e.affine_select(out=Mw.ap, in_=Mw.ap, compare_op=ALU.is_ge, fill=-BIG, base=-31 + 16 * 128,
                                     pattern=[[-16, 256]], channel_multiplier=1), [Mw], [Mw])
    S.pool(lambda e: e.memset(Msm.ap, 0.0), [], [Msm])
    S.pool(lambda e: e.memset(Msm.ap[:, 127:128], -BIG), [], [Msm])
    S.pool(lambda e: e.memset(Em.ap, 0.0), [], [Em])
    for hh in range(2):
        ev = Em.ap[hh * 64:(hh + 1) * 64].rearrange("p k (h c) -> p k h c", h=2)
        S.pool(lambda e: e.affine_select(out=ev, in_=ev, compare_op=ALU.not_equal, fill=BIG, base=0,
                                         pattern=[[-2, 17], [-1, 2], [0, 64]], channel_multiplier=1), [Em], [Em])
    posi = S.alias("posi", idx, idx.ap[:, 0, :].bitcast(I32))
    S.pool(lambda e: e.iota(idx.ap[:, 0, :], pattern=[[128, 16]], base=0, channel_multiplier=1), [], [idx])
    S.dve(lambda e: e.tensor_copy(sml.ap[:, 0, :], idx.ap[:, 0, :]), [idx], [sml])
    S.pool(lambda e: e.iota(idx.ap[:, 1, 0:1], pattern=[[0, 1]], base=0, channel_multiplier=1), [], [idx])
    S.dve(lambda e: e.tensor_single_scalar(out=idx.ap[:, 1, 1:2], in_=idx.ap[:, 1, 0:1], scalar=3, op=ALU.bitwise_and), [idx], [idx])
    S.dve(lambda e: e.tensor_copy(sml.ap[:, 1, 0:1], idx.ap[:, 1, 1:2]), [idx], [sml])
    self.ts(dve, sml.ap[:, 1, 0:1], sml.ap[:, 1, 0:1], 2048.0, ALU.add, [sml], [sml])
    for f in range(8):
        S.pool(lambda e: e.memset(sml.ap[:, 2, f:f + 1], float(ROPE_THETA ** (-f / 8.0))), [], [sml])
    ang = S.alias("ang", pr, pr.ap[:, 0:17 * 8].rearrange("p (a b) -> p a b", a=17))
    self.tt(dve, ang.ap[:, 0:16, :], sml.ap[:, 0, :].unsqueeze(2).to_broadcast([128, 16, 8]),
            sml.ap[:, 2, 0:8].unsqueeze(1).to_broadcast([128, 16, 8]), ALU.mult, [sml], [pr])
    self.ts(dve, ang.ap[:, 16, :], sml.ap[:, 2, 0:8], sml.ap[:, 1, 0:1], ALU.mult, [sml], [pr])
    scr = (S.alias("r1", pr, pr.ap[:, 256:392].rearrange("p (a b) -> p a b", a=17)),
           S.alias("r2", pr, pr.ap[:, 512:648].rearrange("p (a b) -> p a b", a=17).bitcast(I32)),
           S.alias("r3", pr, pr.ap[:, 768:904].rearrange("p (a b) -> p a b", a=17)))
    self.sincos(ang.ap, pr, cs.ap[:, :, 8:16], cs.ap[:, :, 0:8], cs, scr)
    w1d = [self.wslot[2], self.wslot[3]]
    w1v = [w.ap.rearrange("p (j h) -> p j h", j=32) for w in w1d]
    for kv in range(2):
        for dup in range(2):
            S.dma(w1d[kv], w1v[kv][dup * 64:(dup + 1) * 64, :, :], self.cmp_w1[ii, kv].rearrange("(j d) h -> d j h", d=64), q="pool")
        S.dma(w2f, w2f.ap[:, kv, :], self.cmp_w2[ii, kv])
        for dup in range(2):
            S.dma(peTf, peTf.ap[dup * 64:(dup + 1) * 64, kv, :], self.cmp_pos[ii, kv].rearrange("j d -> d j"))
    S.dve(lambda e: e.tensor_copy(peT.ap, peTf.ap), [peTf], [peT])
    S.pool(lambda e: e.memset(w2k.ap, 0.0), [], [w2k])
    for g in range(2):
        S.dve(lambda e: e.tensor_copy(w2k.ap[:, g, g * 64:(g + 1) * 64], w2f.ap[:, 0, :]), [w2f], [w2k])
    S.dve(lambda e: e.tensor_copy(w2v.ap, w2f.ap[:, 1, :]), [w2f], [w2v])
    for kv in range(2):
        for j in range(32):
            self.mm(P[7].ap[:, kv:kv + 1], w1v[kv][0:64, j, :], peT.ap[0:64, kv, j:j + 1], [w1d[kv], peT], [P[7]], start=(j == 0), stop=(j == 31))
    self.actf(peb.ap, P[7].ap[:, 0:2], AF.Copy, [P[7]], [peb])
    S.dma(idx, idx.ap.rearrange("p a b -> p (a b)"), self.page_table.rearrange("b j -> (b j)").partition_broadcast(128))
    S.dve(lambda e: e.tensor_copy(idxf.ap, idx.ap), [idx], [idxf])
    S.pool(lambda e: e.iota(idx.ap[:, 0, 0:1], pattern=[[0, 1]], base=0, channel_multiplier=1), [idxf], [idx])
    S.dve(lambda e: e.tensor_copy(sml.ap[:, 3, 0:1], idx.ap[:, 0, 0:1]), [idx], [sml])
    self.ts(dve, idxf.ap, idxf.ap, 128.0, ALU.mult, [idxf, sml], [idxf], s2=sml.ap[:, 3, 0:1], op1=ALU.add)
    S.dve(lambda e: e.tensor_copy(idx.ap, idxf.ap), [idxf], [idx])
    self.load_gb(self.norm_mix_odd[ii:ii + 1, :])
    S.barrier()
    o.update(locals())

    stop = getattr(self, "odd_stop", 99)
    if stop <= 0:
        return
    win = self.w_in_odd[ii].rearrange("(c p) n -> p c n", p=128)
    wkv = [self.load_w(win[:, j * 4:(j + 1) * 4, 1024:1840], slot=j) for j in range(2)]
    for i in range(NT):
        r = rows(i)
        self.norm_T(i, xnt.ap, xnt, 0, pbase=4)
        for (c0, c1, PP) in ((0, 512, P[0]), (512, 816, P[1])):
            for k in range(8):
                wt, w = wkv[k // 4]
                self.mm(PP.ap[:r, :c1 - c0], xnt.ap[:, k, :r], w[:, k % 4, c0:c1], [xnt, wt], [PP], start=(k == 0), stop=(k == 7))
            self.actf(pr.ap[:r, c0:c1], PP.ap[:r, :c1 - c0], AF.Copy, [PP], [pr])
        kv5 = pr.ap[:r, 0:768].rearrange("p (b k g d) -> p b k g d", b=3, k=2, g=2)
        v1, v2 = kv5[:, :, 0, :, 0:8], kv5[:, :, 0, :, 8:16]
        cosb = cs.ap[:r, i, 0:8].unsqueeze(1).unsqueeze(1).to_broadcast([r, 3, 2, 8])
        sinb = cs.ap[:r, i, 8:16].unsqueeze(1).unsqueeze(1).to_broadcast([r, 3, 2, 8])
        rs = [rsc.ap[:r, j, 0:48].rearrange("p (a b c) -> p a b c", a=3, b=2) for j in range(4)]
        _rope(self, v1, v2, cosb, sinb, None, pr, rsc, rs)
        r0 = i * 128
        if i < 16:
            S.dma_out(self.o_cmp_p[ii, r0:r0 + 128, :], pr.ap[:, 0:256], pr)
            S.dma_out(self.o_sel_p[ii, r0:r0 + 128, :], pr.ap[:, 256:512], pr)
            if i >= 12:
                S.dma_out(self.o_win_p[ii, r0 - 1536:r0 - 1536 + 128, :], pr.ap[:, 512:768], pr)
        else:
            S.dma_out(self.o_cmp_s[ii], pr.ap[:64, 0:256], pr)
            S.dma_out(self.o_sel_s[ii], pr.ap[:64, 256:512], pr)
            for b in range(16):
                S.dma_out(self.o_win_s[ii, b, 508:512, :], pr.ap[4 * b:4 * b + 4, 512:768], pr)
                S.dma_fn(lambda e: e.dma_start(out=self.o_win_s[ii, b, 0:508, :], in_=self.state_win[ii, b, 4:512, :]), [], [])
        self.actf(gates.ap[:r, i, :], pr.ap[:r, 768:816], AF.Sigmoid, [pr], [gates])
        self.actf(prb.ap[:r, 0:768], pr.ap[:r, 0:768], AF.Copy, [pr], [prb])
        PPt = P[2]
        ptb = Pb[2]
        srcs = [(0, KcT), (128, VcT), (256, KsT), (512, KwT)]
        if i == 16:
            srcs += [(384, VsT_s), (640, VwT_s)]
        for n_, (c0, dst) in enumerate(srcs):
            self.tr(ptb[:, n_ * 128:n_ * 128 + r], prb.ap[:r, c0:c0 + 128], self.identb.ap[:r, :r], [prb, self.identb], [PPt])
        for n_, (c0, dst) in enumerate(srcs):
            dcol = dst.ap[:, r0:r0 + r] if dst.ap.shape[1] == TT else dst.ap[:, 0:r]
            self.actf(dcol, ptb[:, n_ * 128:n_ * 128 + r], AF.Copy, [PPt], [dst])
        if i < 16:
            S.pool(lambda e: e.tensor_copy(Vs.ap[:, i, :, 0:64], prb.ap[:, 384:512].rearrange("p (g d) -> p g d", g=2)), [prb], [Vs])
            S.pool(lambda e: e.tensor_copy(Vw.ap[:, i, :, 0:64], prb.ap[:, 640:768].rearrange("p (g d) -> p g d", g=2)), [prb], [Vw])

    if stop <= 1:
        return
    _compress(self, KcT.ap[:, 0:2048], KcT, VcT.ap[:, 0:2048], VcT, 16)
    if stop <= 2:
        return
    for b in range(16 if stop > 3 else 1):
        for pg in range(4):
            st = stg[pg % 2]
            for j in range(4):
                S.dma_fn(lambda e: e.indirect_dma_start(out=st.ap[:, j, :], out_offset=None, in_=self.cache_cmp[ii],
                                                        in_offset=bass.IndirectOffsetOnAxis(ap=idx.ap[:, b, pg * 4 + j:pg * 4 + j + 1], axis=0)),
                         [idx], [st], q="pool")
            for kv, dst in ((0, KpT), (1, VcT)):
                PPt = P[2 + kv]
                for j in range(4):
                    self.tr(Pb[2 + kv][:, j * 128:(j + 1) * 128], st.ap[:, j, kv * 128:(kv + 1) * 128], self.identb.ap, [st, self.identb], [PPt])
                self.actf(dst.ap[:, pg * 512:(pg + 1) * 512], Pb[2 + kv][:, 0:512], AF.Copy, [PPt], [dst])
        _compress(self, KpT.ap, KpT, VcT.ap[:, 0:2048], VcT, b)

    if stop <= 3:
        return
    wq = [self.load_w(win[:, j * 4:(j + 1) * 4, 0:1024], slot=j) for j in range(2)]
    wo_d = self.w_out_odd[ii].rearrange("(c p) n -> p c n", p=128)
    wo = []
    for j in range(2):
        slot = self.wslot[2 + j]
        ap = slot.ap[:, :4096].rearrange("p (a b) -> p a b", a=4)
        S.dma(slot, ap, wo_d[:, j * 4:(j + 1) * 4, :], q="pool")
        wo.append((slot, ap))
    o["wo"] = wo
    tl = list(range(NT))
    if stop == 5:
        tl = [0]
    elif stop == 6:
        tl = [9]
    elif stop == 7:
        tl = [16]
    elif stop == 8:
        tl = list(range(16))
    for i in tl:
        r = rows(i)
        self.norm_T(i, xnt.ap, xnt, 0, pbase=4)
        for half in range(2):
            PP = P[half]
            for k in range(8):
                wt, w = wq[k // 4]
                self.mm(PP.ap[:r, :], xnt.ap[:, k, :r], w[:, k % 4, half * 512:(half + 1) * 512], [xnt, wt], [PP], start=(k == 0), stop=(k == 7))
            self.actf(pr.ap[:r, half * 512:(half + 1) * 512], PP.ap[:r, :], AF.Copy, [PP], [pr])
        q3 = pr.ap[:r, :].rearrange("p (h d) -> p h d", h=16)
        cosb = cs.ap[:r, i, 0:8].unsqueeze(1).to_broadcast([r, 16, 8])
        sinb = cs.ap[:r, i, 8:16].unsqueeze(1).to_broadcast([r, 16, 8])
        rs = [rsc.ap[:r, j, :].rearrange("p (a b) -> p a b", a=16) for j in range(4)]
        _rope(self, q3[:, :, 0:8], q3[:, :, 8:16], cosb, sinb, None, pr, rsc, rs)
        self.actf(prb.ap[:r, :].rearrange("p (i g d) -> p i g d", i=8, g=2),
                  pr.ap[:r, :].rearrange("p (g i d) -> p i g d", g=2, i=8), AF.Copy, [pr], [prb])
        PPt = P[2]
        for i8 in range(8):
            self.tr(Pb[2][:, i8 * 128:i8 * 128 + r], prb.ap[:r, i8 * 128:(i8 + 1) * 128], self.identb.ap[:r, :r], [prb, self.identb], [PPt])
        self.actf(qT.ap[:, :, :r], Pb[2][:, :].rearrange("p (a b) -> p a b", a=8)[:, :, :r], AF.Copy, [PPt], [qT])
        if i < 16:
            _attend(self, i, 128, qT.ap[:, :, :], 16, i)
            _acc_to_oT(self, 128, 0, i * 128)
            _out_mm(self, i, 128)
        else:
            for b in range(16):
                _sample_kv(self, b)
                S.pool(lambda e: e.tensor_copy(qTs.ap, qT.ap[:, :, 4 * b:4 * b + 4]), [qT], [qTs])
                _attend(self, 16, 4, qTs.ap, b, None, b)
                _acc_to_oT(self, 4, 4 * b, TP + 4 * b)
            _out_mm(self, 16, 64)


def _compress(self, KT, KT_t, VT, VT_t, slot):
    S = self.S
    o = self._od
    P = self.P
    hid, w2k, w2v, peb, kc_all, vca_all = (o[k] for k in ("hid", "w2k", "w2v", "peb", "kc_all", "vca_all"))
    w1d, w1v = o["w1d"], o["w1v"]
    for kv, (XT, X_t) in enumerate(((KT, KT_t), (VT, VT_t))):
        for g in range(2):
            PP = P[4 + g]
            for j in range(32):
                rhs = XT[g * 64:(g + 1) * 64, j:j + 2017:16]
                self.mm(PP.ap[:, :127], w1v[kv][g * 64:(g + 1) * 64, j, :], rhs, [w1d[kv], X_t], [PP], start=(j == 0), stop=(j == 31))
            self.actf(hid.ap[:, g, :127], PP.ap[:, :127], AF.Silu, [PP, peb], [hid], bias=peb.ap[:, kv:kv + 1])
        if kv == 0:
            PP = P[6]
            for g in range(2):
                self.mm(PP.ap[:, :127], w2k.ap[:, g, :], hid.ap[:, g, :127], [w2k, hid], [PP], start=(g == 0), stop=(g == 1))
            self.actf(kc_all.ap[:, slot, :127], PP.ap[:, :127], AF.Copy, [PP], [kc_all])
        else:
            PP = P[7]
            for g in range(2):
                self.mm(PP.ap[:127, g * 64:(g + 1) * 64], hid.ap[:, g, :127], w2v.ap, [hid, w2v], [PP])
            self.actf(vca_all.ap[:127, slot, :, :], PP.ap[:127, 0:128].rearrange("p (g d) -> p g d", g=2), AF.Copy, [PP], [vca_all])


def _sample_kv(self, b):
    S = self.S
    o = self._od
    ii = o["ii"]
    P, Pb = self.P, self.Pb
    stg, idx, KpT, Vp, KwT, Vw, KsT, VsT_s, VwT_s = (o[k] for k in ("stg", "idx", "KpT", "Vp", "KwT", "Vw", "KsT", "VsT_s", "VwT_s"))
    for pg in range(5):
        st = stg[pg % 2]
        if pg < 4:
            for j in range(4):
                S.dma_fn(lambda e: e.indirect_dma_start(out=st.ap[:, j, :], out_offset=None, in_=self.cache_sel[ii],
                                                        in_offset=bass.IndirectOffsetOnAxis(ap=idx.ap[:, b, pg * 4 + j:pg * 4 + j + 1], axis=0)),
                         [idx], [st], q="pool")
            KT_dst, V_dst, t0 = KpT, Vp, pg * 4
            kcol = pg * 512
        else:
            S.dma(st, st.ap, self.state_win[ii, b].rearrange("(j p) c -> p j c", p=128), q="pool")
            KT_dst, V_dst, t0 = KwT, Vw, 0
            kcol = 0
        PPt = P[3]
        for j in range(4):
            self.tr(Pb[3][:, j * 128:(j + 1) * 128], st.ap[:, j, 0:128], self.identb.ap, [st, self.identb], [PPt])
        self.actf(KT_dst.ap[:, kcol:kcol + 512], Pb[3][:, 0:512], AF.Copy, [PPt], [KT_dst])
        S.pool(lambda e: e.tensor_copy(V_dst.ap[:, t0:t0 + 4, :, 0:64], st.ap[:, :, 128:256].rearrange("p j (g d) -> p j g d", g=2)), [st], [V_dst])
    for (VT_s, V_dst, tile_) in ((VsT_s, Vp, 16), (VwT_s, Vw, 4)):
        PPt = P[3]
        self.tr(Pb[3][:4, 0:128], VT_s.ap[:, 4 * b:4 * b + 4], self.identb.ap, [VT_s, self.identb], [PPt])
        self.actf(V_dst.ap[:4, tile_, :, 0:64], Pb[3][:4, 0:128].rearrange("p (g d) -> p g d", g=2), AF.Copy, [PPt], [V_dst])


def _acc_to_oT(self, r, c0, row0=None):
    S = self.S
    o = self._od
    if self.dbg and row0 is not None:
        S.dma_out(self.dout["dbg_acc"][row0:row0 + r, :], o["acc"].ap[:r, :], o["acc"])
    P, Pb = self.P, self.Pb
    acc, ob, oT = o["acc"], o["ob"], o["oT"]
    self.actf(ob.ap[:r, :], acc.ap[:r, :], AF.Copy, [acc], [ob])
    o4 = ob.ap[:r, :].rearrange("p (c x) -> p c x", c=8)
    PPt = P[7]
    for c in range(8):
        self.tr(Pb[7][:, c * 128:c * 128 + r], o4[:, c, :], self.identb.ap[:r, :r], [ob, self.identb], [PPt])
    self.actf(oT.ap[:, :, c0:c0 + r], Pb[7][:, :].rearrange("p (a b) -> p a b", a=8)[:, :, :r], AF.Copy, [PPt], [oT])


def _out_mm(self, i, r):
    S = self.S
    o = self._od
    P = self.P
    oT, wo = o["oT"], o["wo"]
    dve = S.dve_e
    for half in range(2):
        PP = P[half]
        for k in range(8):
            wt, w = wo[k // 4]
            self.mm(PP.ap[:r, :], oT.ap[:, k, :r], w[:, k % 4, half * 512:(half + 1) * 512], [oT, wt], [PP], start=(k == 0), stop=(k == 7))
        xs_ = self.X.ap[:r, i, half * 512:(half + 1) * 512]
        self.tt(dve, xs_, xs_, PP.ap[:r, :], ALU.add, [PP, self.xt[i]], [self.xt[i]])


def _attend(self, tile_i, r, qT_ap, cslot, qt, b=None):
    S = self.S
    o = self._od
    dve, pool = S.dve_e, S.pool_e
    P, Pb = self.P, self.Pb
    sample = b is not None
    kc_all, vca_all, sc, Pc, acc, sml, imp, psg, sbias, sel01, mx8, gates, Em = (o[k] for k in (
        "kc_all", "vca_all", "sc", "Pc", "acc", "sml", "imp", "psg", "sbias", "sel01", "mx8", "gates", "Em"))
    KsT, KwT, KpT, Vs, Vw, Vp, ncausT, nwinT = (o[k] for k in ("KsT", "KwT", "KpT", "Vs", "Vw", "Vp", "ncausT", "nwinT"))
    grow = (4 * b) if sample else 0
    RS, RINV, MXN, CO, DEN, GT = 4, 5, 6, 7, 8, 9
    if sample:
        PPm = P[7]
        self.mm(PPm.ap[:r, 0:48], self.ident.ap[:64, grow:grow + r], gates.ap[:64, 16, :], [self.ident, gates], [PPm])
        self.actf(sml.ap[:r, GT:GT + 3, :].rearrange("p a b -> p (a b)"), PPm.ap[:r, 0:48], AF.Copy, [PPm], [sml])
        gv = sml.ap[:r, GT:GT + 3, :].rearrange("p a b -> p (a b)").rearrange("p (h k) -> p h k", k=3)
    else:
        gv = gates.ap[:r, tile_i, :].rearrange("p (h k) -> p h k", k=3)
    kc = kc_all.ap[:, cslot, :]
    if sample:
        mk = o["Msm"]
        mk_ap = mk.ap
    else:
        mk = o["Mw"]
        mk_ap = mk.ap[:, 128 - 8 * qt:256 - 8 * qt]
    S.pool(lambda e: e.memset(sml.ap[:, RS, :], 0.0), [], [sml])
    for h in range(16):
        g, i8 = h // 8, h % 8
        self.mm(P[h // 4].ap[:r, (h % 4) * 128:(h % 4 + 1) * 128], qT_ap[g * 64:(g + 1) * 64, i8, :], kc[g * 64:(g + 1) * 64, :], [o["qT"], o["qTs"], kc_all], [P[h // 4]])
    for bk in range(4):
        s4 = sc[bk % 2]
        self.tt(dve, s4.ap[:r], P[bk].ap[:r, :].rearrange("p (a b) -> p a b", a=4), mk_ap[:r, :].unsqueeze(1).to_broadcast([r, 4, 128]), ALU.add, [P[bk], mk], [s4])
        S.dve(lambda e: e.tensor_reduce(out=sml.ap[:r, MXN, bk:bk + 1], in_=s4.ap[:r].rearrange("p a b -> p (a b)"), axis=AX.X, op=ALU.max), [s4], [sml])
        self.ts(dve, sml.ap[:r, MXN, bk:bk + 1], sml.ap[:r, MXN, bk:bk + 1], -10000.0, ALU.max, [sml], [sml], s2=-0.125, op1=ALU.mult)
        self.actf(Pc.ap[:r, bk * 4:bk * 4 + 4, :], s4.ap[:r], AF.Exp, [s4, sml], [Pc], bias=sml.ap[:r, MXN, bk:bk + 1], scale=0.125)
        S.dve(lambda e: e.tensor_reduce(out=sml.ap[:r, RS, bk * 4:bk * 4 + 4], in_=Pc.ap[:r, bk * 4:bk * 4 + 4, :], axis=AX.X, op=ALU.add), [Pc], [sml])
    self.ts(dve, sml.ap[:r, RINV, :], sml.ap[:r, RS, :], 1e-30, ALU.max, [sml], [sml])
    S.dve(lambda e: e.reciprocal(out=sml.ap[:r, RINV, :], in_=sml.ap[:r, RINV, :]), [sml], [sml])
    need_sel = (sample or qt >= 8) and not getattr(self, 'nosel', False)
    selstage = getattr(self, 'selstage', 99)
    if need_sel:
        for h in range(16):
            g = h // 8
            if h % 8 == 0:
                self.ts(dve, psg.ap[:r, g, :], Pc.ap[:r, h, :], sml.ap[:r, RINV, h:h + 1], ALU.mult, [Pc, sml], [psg])
            else:
                self.stt(dve, psg.ap[:r, g, :], Pc.ap[:r, h, :], sml.ap[:r, RINV, h:h + 1], psg.ap[:r, g, :], ALU.mult, ALU.add, [Pc, sml, psg], [psg])
        p4 = psg.ap[:r, :, :].rearrange("p g (j f) -> p g j f", f=4)
        S.pool(lambda e: e.memset(imp.ap[:r], 0.0), [], [imp])
        S.dve(lambda e: e.tensor_reduce(out=imp.ap[:r, :, 0:32], in_=p4, axis=AX.X, op=ALU.add), [psg], [imp])
        self.tt(dve, imp.ap[:r, :, 1:32], imp.ap[:r, :, 1:32], p4[:, :, 0:31, 3], ALU.add, [imp, psg], [imp])
    for h in range(16):
        self.tr(Pb[4 + h // 8][:, (h % 8) * 128:(h % 8) * 128 + r], Pc.ap[:r, h, :], self.identb.ap[:r, :r], [Pc, self.identb], [P[4 + h // 8]])
    PTq = o["PTq"]
    for hb in range(4):
        self.actf(PTq[hb].ap[:, :, :r], Pb[4 + hb // 2][:, (hb % 2) * 512:(hb % 2 + 1) * 512].rearrange("p (a b) -> p a b", a=4)[:, :, :r], AF.Copy, [P[4 + hb // 2]], [PTq[hb]])
    for h in range(16):
        g = h // 8
        self.mm(P[6 + h // 8].ap[:r, (h % 8) * 64:(h % 8 + 1) * 64], PTq[h // 4].ap[:, h % 4, :r], vca_all.ap[:, cslot, g, :], [PTq[h // 4], vca_all], [P[6 + h // 8]])
    self.tt(dve, sml.ap[:r, CO, :], gv[:, :, 0], sml.ap[:r, RINV, :], ALU.mult, [gates, sml], [sml])
    OC = self.PS[:r, 6:8, :].rearrange("p a b -> p (a b)").rearrange("p (h d) -> p h d", h=16)
    self.tt(dve, acc.ap[:r, :].rearrange("p (h d) -> p h d", h=16), OC, sml.ap[:r, CO, :].unsqueeze(2).to_broadcast([r, 16, 64]), ALU.mult,
            [P[6], P[7], sml], [acc])
    dbr = getattr(self, "dbg_br", (0, 1, 2))
    if 0 not in dbr:
        S.pool(lambda e: e.memset(acc.ap[:r, :], 0.0), [], [acc])
    nj = 33 if sample else 32
    if need_sel and selstage >= 2:
        S.pool(lambda e: e.memset(sbias.ap[:r], -1e30), [], [sbias])
        if sample:
            S.pool(lambda e: e.memset(sbias.ap[:r, 0:33], 0.0), [], [sbias])
            for j in (0, 31, 32):
                S.pool(lambda e: e.memset(sbias.ap[:r, j:j + 1], 1e4), [], [sbias])
        else:
            S.pool(lambda e: e.memset(sbias.ap[0:64, 0:2 * qt + 1], 0.0), [], [sbias])
            S.pool(lambda e: e.memset(sbias.ap[64:128, 0:2 * qt + 2], 0.0), [], [sbias])
            S.pool(lambda e: e.memset(sbias.ap[:, 0:1], 1e4), [], [sbias])
            S.pool(lambda e: e.memset(sbias.ap[0:64, 2 * qt - 1:2 * qt + 1], 1e4), [], [sbias])
            S.pool(lambda e: e.memset(sbias.ap[64:128, 2 * qt:2 * qt + 2], 1e4), [], [sbias])
        S.pool(lambda e: e.memset(imp.ap[:r, :, 32:64], 0.0), [], [imp])
        self.tt(dve, imp.ap[:r], imp.ap[:r], sbias.ap[:r, :].unsqueeze(1).to_broadcast([r, 2, 64]), ALU.add, [imp, sbias], [imp])
        for g in range(2 if selstage >= 3 else 0):
            S.dve(lambda e: e.max(out=mx8.ap[:r, g, :], in_=imp.ap[:r, g, :]), [imp], [mx8])
            S.dve(lambda e: e.match_replace(out=sel01.ap[:r, g, :], in_to_replace=mx8.ap[:r, g, :], in_values=imp.ap[:r, g, :], imm_value=-1e30), [mx8, imp], [sel01])
            S.dve(lambda e: e.max(out=mx8.ap[:r, g, :], in_=sel01.ap[:r, g, :]), [sel01], [mx8])
            S.dve(lambda e: e.tensor_reduce(out=sml.ap[:r, DEN, g:g + 1], in_=mx8.ap[:r, g, :], axis=AX.X, op=ALU.min), [mx8], [sml])
            self.ts(dve, sel01.ap[:r, g, :], imp.ap[:r, g, :], sml.ap[:r, DEN, g:g + 1], ALU.is_ge, [imp, sml], [sel01])
        if selstage >= 4:
            self.tr(P[7].ap[:, 0:r], sel01.ap[:r].rearrange("p g j -> p (g j)"), self.ident.ap[:r, :r], [sel01, self.ident], [P[7]])
            nrep = 8 if sample else 4
            self.ts(dve, o["selm4"].ap[:, 0:nrep * r].rearrange("p (i s) -> p i s", i=nrep),
                    P[7].ap[:, 0:r].unsqueeze(1).to_broadcast([128, nrep, r]), -1.0, ALU.add, [P[7]], [o["selm4"]])
    if sample:
        sel_tiles = [(KpT.ap[:, j * 128:(j + 1) * 128], KpT, Vp.ap[:, j], Vp, 128, None, j) for j in range(16)]
        ncs, nws = o["ncaus_s"], o["nwin_s"]
        sel_tiles.append((KsT.ap[:, TP + 4 * b:TP + 4 * b + 4], KsT, Vp.ap[:4, 16], Vp, 4, ncs, None))
        win_tiles = [(KwT.ap[:, j * 128:(j + 1) * 128], KwT, Vw.ap[:, j], Vw, 128, (nws if j == 0 else None), None) for j in range(4)]
        win_tiles.append((KwT.ap[:, TP + 4 * b:TP + 4 * b + 4], KwT, Vw.ap[:4, 4], Vw, 4, ncs, None))
    else:
        nc4, nw4 = o["ncaus4"], o["nwin4"]
        sel_tiles = [(KsT.ap[:, j * 128:(j + 1) * 128], KsT, Vs.ap[:, j], Vs, 128, (nc4 if j == qt else None), (j if (need_sel and selstage >= 5) else None)) for j in range(qt + 1)]
        win_tiles = [(KwT.ap[:, j * 128:(j + 1) * 128], KwT, Vw.ap[:, j], Vw, 128, (nc4 if j == qt else (nw4 if j == qt - 4 else None)), None)
                     for j in range(max(0, qt - 4), qt + 1)]
    PTq = o["PTq"]
    step = 0
    pendq = []

    def emit_pv(pu):
        (PTt_, V_, V_t_, nk_, g_, ti_, nt__, hq_, br_) = pu
        for j in range(4):
            self.mm(P[4 + j].ap[:r, 0:65], PTt_.ap[:nk_, j, :r], V_[:nk_, g_, :], [PTt_, V_t_], [P[4 + j]], start=(ti_ == 0), stop=(ti_ == nt__ - 1))
        if ti_ == nt__ - 1:
            h0_ = (hq_ // 2) * 8 + (hq_ % 2) * 4
            S.dve(lambda e: e.reciprocal(out=sml.ap[:r, DEN, h0_:h0_ + 4], in_=self.PS[:r, 4:8, 64]), [P[4], P[5], P[6], P[7]], [sml])
            self.tt(dve, sml.ap[:r, CO, h0_:h0_ + 4], gv[:, h0_:h0_ + 4, br_], sml.ap[:r, DEN, h0_:h0_ + 4], ALU.mult, [gates, sml], [sml])
            for j in range(4):
                h = h0_ + j
                a_h = acc.ap[:r, h * 64:(h + 1) * 64]
                self.stt(dve, a_h, P[4 + j].ap[:r, 0:64], sml.ap[:r, CO, h:h + 1], a_h, ALU.mult, ALU.add, [P[4 + j], sml, acc], [acc])

    if sample:
        _attend_sample_branches(self, r, sel_tiles, win_tiles, gv, dbr)
        return
    for br, tiles in ((1, sel_tiles), (2, win_tiles)):
        nt_ = len(tiles)
        if br not in dbr:
            continue
        for hq in range(4):
            g, i0 = hq // 2, (hq % 2) * 4
            for ti, (KT, KT_t, V, V_t, nk, mT, ej) in enumerate(tiles):
                SB = P[step % 3]
                PTt = PTq[step % 4]
                step += 1
                out2 = SB.ap[:nk, 0:4 * r]
                last = (ej is None and mT is None)
                self.mm(out2, KT[g * 64:(g + 1) * 64, :], qT_ap[g * 64:(g + 1) * 64, i0:i0 + 4, :].rearrange("p a b -> p (a b)"),
                        [KT_t, o["qT"], o["qTs"]], [SB], start=True, stop=last)
                if ej is not None:
                    self.mm(out2, Em.ap[g * 64:(g + 1) * 64, ej, :nk], o["selm4"].ap[g * 64:(g + 1) * 64, 0:4 * r], [Em, o["selm4"]], [SB], start=False, stop=(mT is None))
                if mT is not None:
                    pb_ = g * 64 if nk < 128 else 0
                    self.mm(out2, self.identb.ap[pb_:pb_ + nk, pb_:pb_ + nk], mT.ap[pb_:pb_ + nk].rearrange("p a b -> p (a b)")[:, 0:4 * r], [self.identb, mT], [SB], start=False, stop=True)
                self.actf(PTt.ap[:nk, :, :r], out2.rearrange("p (h s) -> p h s", h=4), AF.Exp, [SB], [PTt], scale=0.125)
                pendq.append((PTt, V, V_t, nk, g, ti, nt_, hq, br))
                if len(pendq) > 2:
                    emit_pv(pendq.pop(0))
    while pendq:
        emit_pv(pendq.pop(0))


def _attend_sample_branches(self, r, sel_tiles, win_tiles, gv, dbr):
    S = self.S
    o = self._od
    dve = S.dve_e
    P = self.P
    acc, sml, gates, Em, PTq, rsc, SelR, SelT = (o[k] for k in ("acc", "sml", "gates", "Em", "PTq", "rsc", "SelR", "SelT"))
    acc32 = S.alias("acc32", rsc, rsc.ap[:32, 0, :].rearrange("p (g d) -> p g d", g=2))
    G32 = S.alias("G32", rsc, rsc.ap[:32, 1, 0:8].rearrange("p (g k) -> p g k", g=2))
    gtsb = S.alias("gtsb", rsc, rsc.ap[:4, 2, 0:24].bitcast(BF16))
    cf = S.alias("cf", rsc, rsc.ap[:32, 3, 0:8])
    qTs = o["qTs"]
    self.actf(gtsb.ap, gv.rearrange("p h k -> p (h k)"), AF.Copy, [sml], [rsc])
    for g in range(2):
        for i8 in range(8):
            h = g * 8 + i8
            self.mm(P[6].ap[:32, g * 4:g * 4 + 3], SelR.ap[:4, i8, :], gtsb.ap[:4, h * 3:h * 3 + 3], [SelR, rsc], [P[6]], start=(i8 == 0), stop=(i8 == 7))
    self.actf(G32.ap[:, :, 0:3], P[6].ap[:32, 0:8].rearrange("p (g k) -> p g k", g=2)[:, :, 0:3], AF.Copy, [P[6]], [rsc])
    step = 0
    pendq = []
    first = {0: True, 1: True}

    def emit_pv(pu):
        (PTt_, V_, V_t_, nk_, g_, ti_, nt__, br_) = pu
        PO = P[4 + g_]
        self.mm(PO.ap[:32, 0:65], PTt_.ap[:nk_, 0, 0:32], V_[:nk_, g_, :], [PTt_, V_t_], [PO], start=(ti_ == 0), stop=(ti_ == nt__ - 1))
        if ti_ == nt__ - 1:
            S.dve(lambda e: e.reciprocal(out=cf.ap[:, 0:1], in_=PO.ap[:32, 64:65]), [PO], [rsc])
            self.tt(dve, cf.ap[:, 1:2], cf.ap[:, 0:1], G32.ap[:, g_, br_:br_ + 1], ALU.mult, [rsc], [rsc])
            if first[g_]:
                self.ts(dve, acc32.ap[:, g_, :], PO.ap[:32, 0:64], cf.ap[:, 1:2], ALU.mult, [PO, rsc], [rsc])
                first[g_] = False
            else:
                self.stt(dve, acc32.ap[:, g_, :], PO.ap[:32, 0:64], cf.ap[:, 1:2], acc32.ap[:, g_, :], ALU.mult, ALU.add, [PO, rsc], [rsc])

    for br, tiles in ((1, sel_tiles), (2, win_tiles)):
        nt_ = len(tiles)
        if br not in dbr:
            continue
        for g in range(2):
            for ti, (KT, KT_t, V, V_t, nk, mT, ej) in enumerate(tiles):
                SB = P[step % 3]
                PTt = PTq[step % 4]
                step += 1
                out2 = SB.ap[:nk, 0:32]
                last = (ej is None and mT is None)
                self.mm(out2, KT[g * 64:(g + 1) * 64, :], qTs.ap[g * 64:(g + 1) * 64].rearrange("p a b -> p (a b)"), [KT_t, qTs], [SB], start=True, stop=last)
                if ej is not None:
                    self.mm(out2, Em.ap[g * 64:(g + 1) * 64, ej, :nk], o["selm4"].ap[g * 64:(g + 1) * 64, 0:32], [Em, o["selm4"]], [SB], start=False, stop=(mT is None))
                if mT is not None:
                    pb_ = g * 64 if nk < 128 else 0
                    self.mm(out2, self.identb.ap[pb_:pb_ + nk, pb_:pb_ + nk], mT.ap[pb_:pb_ + nk].rearrange("p a b -> p (a b)"), [self.identb, mT], [SB], start=False, stop=True)
                self.actf(PTt.ap[:nk, 0, 0:32], out2, AF.Exp, [SB], [PTt], scale=0.125)
                pendq.append((PTt, V, V_t, nk, g, ti, nt_, br))
                if len(pendq) > 2:
                    emit_pv(pendq.pop(0))
    while pendq:
        emit_pv(pendq.pop(0))
    for g in range(2):
        if first[g]:
            continue
        PP = P[6 + g]
        for i8 in range(8):
            self.mm(PP.ap[:4, i8 * 64:(i8 + 1) * 64], SelT.ap[:32, i8, :], acc32.ap[:, g, :], [SelT, rsc], [PP])
        a_g = acc.ap[:4, g * 512:(g + 1) * 512]
        self.tt(dve, a_g, a_g, PP.ap[:4, :], ALU.add, [PP, acc], [acc])


MK.decl_odd = _decl_odd
MK.odd_layer = _odd_layer
```

```python
import numpy as np
import concourse.bass as bass
import concourse.mybir as mybir
from concourse.bass_utils import run_bass_kernel_spmd

F32 = mybir.dt.float32
BF16 = mybir.dt.bfloat16
I32 = mybir.dt.int32
AF = mybir.ActivationFunctionType
ALU = mybir.AluOpType
AX = mybir.AxisListType

EPOCH = 30000
SELF_WAIT = True


class T:
    def __init__(self, name, ap):
        self.name = name
        self.ap = ap
        self.last_w = None
        self.reads = {}
        self.root = self


class Eng:
    def __init__(self, S, name, obj, self_wait=True):
        self.S = S
        self.name = name
        self.obj = obj
        self.epoch = 0
        self.count = 0
        self.sem = S.new_sem(name + "_e0")
        self.seen = {}
        self.self_wait = self_wait

    def key(self):
        return (self.name, self.epoch)

    def next_token(self):
        if self.count >= EPOCH:
            self.epoch += 1
            self.count = 0
            self.sem = self.S.new_sem("%s_e%d" % (self.name, self.epoch))
        self.count += 1
        k = self.key()
        self.S.semmap[k] = self.sem
        return (k, self.count)


class Sched:
    def __init__(self, nc, n_dma_sems=12):
        self.nc = nc
        self.semmap = {}
        self.nsem = 0
        self.pe_e = Eng(self, "pe", nc.tensor, self_wait=False)
        sw = SELF_WAIT
        self.act_e = Eng(self, "act", nc.scalar, self_wait=sw)
        self.dve_e = Eng(self, "dve", nc.vector, self_wait=sw)
        self.pool_e = Eng(self, "pool", nc.gpsimd, self_wait=True)
        self.sp_e = Eng(self, "sp", nc.sync)
        self.engs = [self.pe_e, self.act_e, self.dve_e, self.pool_e, self.sp_e]
        self.dma_pools = {}
        for qn in ("sp", "pool", "act"):
            sems = []
            for i in range(n_dma_sems):
                s = self.new_sem("dma_%s_%d" % (qn, i))
                k = ("dma_" + qn, i)
                self.semmap[k] = s
                sems.append([k, 0])
            self.dma_pools[qn] = [sems, 0]
        self.out_tokens = []
        self.nins = 0

    def new_sem(self, name):
        self.nsem += 1
        return self.nc.alloc_semaphore(name)

    def sb(self, name, shape, dtype):
        h = self.nc.alloc_sbuf_tensor(name, list(shape), dtype)
        return T(name, h[:])

    def ps(self, name, shape, dtype):
        h = self.nc.alloc_psum_tensor(name, list(shape), dtype)
        return T(name, h[:])

    def view(self, name, ap):
        return T(name, ap)

    def alias(self, name, base, ap):
        t = T(name, ap)
        t.root = base.root
        return t

    def _wait(self, eng, deps):
        best = {}
        for d in deps:
            if d is None:
                continue
            k, v = d
            if best.get(k, 0) < v:
                best[k] = v
        for k, v in best.items():
            if k == eng.key() and not eng.self_wait:
                continue
            if eng.seen.get(k, 0) >= v:
                continue
            eng.obj.wait_ge(self.semmap[k], v)
            eng.seen[k] = v

    def _deps(self, reads, writes):
        deps = []
        for t in reads:
            deps.append(t.root.last_w)
        for t in writes:
            t = t.root
            deps.append(t.last_w)
            for k, v in t.reads.items():
                deps.append((k, v))
        return deps

    def _commit(self, tok, reads, writes):
        k, v = tok
        for t in reads:
            t = t.root
            if t.reads.get(k, 0) < v:
                t.reads[k] = v
        for t in writes:
            t = t.root
            t.last_w = tok
            t.reads = {}

    def op(self, eng, fn, reads, writes):
        self._wait(eng, self._deps(reads, writes))
        ins = fn(eng.obj)
        tok = eng.next_token()
        ins.then_inc(eng.sem, 1)
        self._commit(tok, reads, writes)
        self.nins += 1
        return tok

    def pe(self, fn, reads, writes):
        return self.op(self.pe_e, fn, reads, writes)

    def act(self, fn, reads, writes):
        return self.op(self.act_e, fn, reads, writes)

    def dve(self, fn, reads, writes):
        return self.op(self.dve_e, fn, reads, writes)

    def pool(self, fn, reads, writes):
        return self.op(self.pool_e, fn, reads, writes)

    def dma_fn(self, fn, reads, writes, q="sp"):
        eng = {"sp": self.sp_e, "pool": self.pool_e, "act": self.act_e}[q]
        pool = self.dma_pools[q]
        sems, idx = pool
        ent = sems[idx]
        pool[1] = (idx + 1) % len(sems)
        k = ent[0]
        deps = self._deps(reads, writes)
        if ent[1] > 0:
            deps.append((k, ent[1]))
        self._wait(eng, deps)
        ins = fn(eng.obj)
        ent[1] += 16
        tok = (k, ent[1])
        ins.then_inc(self.semmap[k], 16)
        self._commit(tok, reads, writes)
        self.nins += 1
        return tok

    def dma(self, dst_t, dst_ap, src_ap, q="sp", src_t=None, **kw):
        kw.setdefault("allow_slow_non_contiguous", True)
        reads = [src_t] if src_t is not None else []
        return self.dma_fn(lambda e: e.dma_start(out=dst_ap, in_=src_ap, **kw), reads, [dst_t], q=q)

    def dma_out(self, dst_ap, src_ap, src_t, q="sp", dst_t=None, **kw):
        kw.setdefault("allow_slow_non_contiguous", True)
        writes = [dst_t] if dst_t is not None else []
        tok = self.dma_fn(lambda e: e.dma_start(out=dst_ap, in_=src_ap, **kw), [src_t], writes, q=q)
        self.out_tokens.append(tok)
        return tok

    def make_identity(self, t):
        n = t.ap.shape[-1]
        self.pool(lambda e: e.memset(t.ap[:], 0.0), [], [t])
        self.pool(lambda e: e.affine_select(out=t.ap[:], in_=t.ap[:], compare_op=ALU.not_equal, fill=1.0,
                                            base=0, pattern=[[-1, n]], channel_multiplier=1), [t], [t])

    def barrier(self):
        deps = []
        for qn, (sems, _) in self.dma_pools.items():
            for k, v in sems:
                if v > 0:
                    deps.append((k, v))
        for e in self.engs:
            if e.count > 0:
                deps.append((e.key(), e.count))
        for e in self.engs:
            self._wait(e, deps)

    def finish(self):
        eng = self.sp_e
        deps = list(self.out_tokens)
        for qn, (sems, _) in self.dma_pools.items():
            for k, v in sems:
                if v > 0:
                    deps.append((k, v))
        for e in self.engs:
            if e.count > 0:
                deps.append((e.key(), e.count))
        self._wait(eng, deps)


D = 1024
NT = 17
TP = 2048
TS = 64
TT = TP + TS
DFF = 4096
EPS = 1e-5
IN_EVEN = 3600
IN_ODD = 1840
NPOOL = 2560


def rows(i):
    return 128 if i < 16 else 64


import math
TWO_PI = 2.0 * math.pi


def _prod(xs):
    r = 1
    for v in xs:
        r *= v
    return r


class MK:
    ARENA = 96 * 1024

    def __init__(self, parts=("mlp",), depth=4, dbg=False):
        self.parts = set(parts)
        self.depth = depth
        self.dbg = dbg
        nc = bass.Bass("TRN2", target_bir_lowering=False)
        self.nc = nc
        self.S = Sched(nc)
        self.din = {}
        self.dout = {}

    def inp(self, name, shape, dtype=F32):
        t = self.nc.dram_tensor(name, list(shape), dtype, kind="ExternalInput").ap()
        self.din[name] = t
        return t

    def outp(self, name, shape, dtype=F32):
        t = self.nc.dram_tensor(name, list(shape), dtype, kind="ExternalOutput").ap()
        self.dout[name] = t
        return t

    def arena_reset(self):
        self.S.barrier()
        self.aoff = 0

    def av(self, name, shape, dtype):
        esz = 2 if dtype == BF16 else 4
        n = _prod(shape[1:])
        nbytes = (n * esz + 63) // 64 * 64
        off = self.aoff
        self.aoff += nbytes
        assert self.aoff <= self.ARENA, ("arena overflow", name, self.aoff)
        ap = self.arena[:, off // 2: off // 2 + nbytes // 2]
        if dtype != BF16:
            ap = ap.bitcast(dtype)
        ap = ap[:shape[0], :n]
        if len(shape) > 2:
            names = "abcd"[:len(shape) - 1]
            pat = "p (%s) -> p %s" % (" ".join(names), " ".join(names))
            ap = ap.rearrange(pat, **{names[j]: shape[1 + j] for j in range(len(shape) - 1)})
        return T(name, ap)

    def build(self):
        S = self.S
        nc = self.nc
        xp = self.inp("x_prompt", [TP, D])
        xs = self.inp("x_sample", [TS, D])
        self.norm_mlp = self.inp("norm_mlp", [4, D])
        self.w_up = self.inp("w_up", [4, D, DFF])
        self.w_down = self.inp("w_down", [4, DFF, D])
        self.norm_final = self.inp("norm_final", [1, D])
        yp = self.outp("y_prompt", [TP, D])
        ys = self.outp("y_sample", [TS, D])
        if "even" in self.parts:
            self.decl_even()
        if "odd" in self.parts:
            self.decl_odd()

        self.X = S.sb("X", [128, NT, D], F32)
        self.xt = [S.view("X%d" % i, None) for i in range(NT)]
        self.wslot = [S.sb("wslot%d" % i, [128, 4096], BF16) for i in range(4)]
        self.wrr = 0
        self.ident = S.sb("ident", [128, 128], F32)
        self.identb = S.sb("identb", [128, 128], BF16)
        self.tri = S.sb("tri", [128, 128], F32)
        self.negm = S.sb("negm", [128, 128], F32)
        self.negmb = S.sb("negmb", [128, 128], BF16)
        self.onesf = S.sb("onesf", [128, 128], F32)
        self.onesb = S.sb("onesb", [128, 256], BF16)
        self.gb = S.sb("gb", [128, D], F32)
        self.ss = [S.sb("ss%d" % i, [128, 2], F32) for i in range(2)]
        self.xnb = [S.sb("xnb%d" % i, [128, D], BF16) for i in range(2)]
        self.arena = nc.alloc_sbuf_tensor("arena", [128, self.ARENA // 2], BF16)
        self.aoff = 0
        self.PS = nc.alloc_psum_tensor("PS", [128, 8, 512], F32)
        self.P = [S.view("P%d" % i, self.PS[:, i, :]) for i in range(8)]
        self.Pb = [self.PS[:, i, :].bitcast(BF16) for i in range(8)]
        self.nrm_i = 0

        S.make_identity(self.ident)
        S.dve(lambda e: e.tensor_copy(self.identb.ap[:], self.ident.ap[:]), [self.ident], [self.identb])
        S.pool(lambda e: e.memset(self.onesf.ap[:], 1.0), [], [self.onesf])
        S.pool(lambda e: e.memset(self.onesb.ap[:], 1.0), [], [self.onesb])
        S.pool(lambda e: e.memset(self.tri.ap[:], 1.0), [], [self.tri])
        S.pool(lambda e: e.affine_select(out=self.tri.ap[:], in_=self.tri.ap[:], compare_op=ALU.is_ge, fill=0.0, base=0,
                                         pattern=[[1, 128]], channel_multiplier=-1), [self.tri], [self.tri])
        S.pool(lambda e: e.memset(self.negm.ap[:], 0.0), [], [self.negm])
        S.pool(lambda e: e.affine_select(out=self.negm.ap[:], in_=self.negm.ap[:], compare_op=ALU.is_ge, fill=-30000.0, base=0,
                                         pattern=[[1, 128]], channel_multiplier=-1), [self.negm], [self.negm])
        S.dve(lambda e: e.tensor_copy(self.negmb.ap[:], self.negm.ap[:]), [self.negm], [self.negmb])
        for i in range(16):
            S.dma(self.xt[i], self.X.ap[:, i, :], xp[i * 128:(i + 1) * 128, :])
        S.dma(self.xt[16], self.X.ap[:64, 16, :], xs[:, :])

        for l in range(self.depth):
            if l % 2 == 0 and "even" in self.parts:
                self.even_layer(l)
            if l % 2 == 1 and "odd" in self.parts:
                self.odd_layer(l)
            if "mlp" in self.parts:
                self.mlp_layer(l)
        self.arena_reset()
        self.xn = [self.av("xnf%d" % i, [128, D], F32) for i in range(2)]
        self.load_gb(self.norm_final[0:1, :])
        for i in range(NT):
            r = rows(i)
            xn = self.norm_tile(i)
            dst = yp[i * 128:(i + 1) * 128, :] if i < 16 else ys[:, :]
            S.dma_out(dst, xn.ap[:r, :], xn)
        S.finish()
        return nc

    def load_gb(self, row_ap):
        self.S.dma(self.gb, self.gb.ap[:], row_ap.partition_broadcast(128))

    def norm_tile(self, i):
        S = self.S
        r = rows(i)
        k = self.nrm_i % 2
        self.nrm_i += 1
        xn, ss = self.xn[k], self.ss[k]
        xi = self.X.ap[:r, i, :]
        S.pool(lambda e: e.memset(ss.ap[:], 0.0), [], [ss])
        S.act(lambda e: e.activation(out=xn.ap[:r, :], in_=xi, func=AF.Square, accum_out=ss.ap[:r, 0:1]),
              [self.xt[i]], [xn, ss])
        S.dve(lambda e: e.tensor_scalar(out=ss.ap[:r, 1:2], in0=ss.ap[:r, 0:1], scalar1=1.0 / D, scalar2=EPS,
                                        op0=ALU.mult, op1=ALU.add), [ss], [ss])
        S.act(lambda e: e.activation(out=ss.ap[:r, 1:2], in_=ss.ap[:r, 1:2], func=AF.Sqrt), [ss], [ss])
        S.dve(lambda e: e.reciprocal(out=ss.ap[:r, 1:2], in_=ss.ap[:r, 1:2]), [ss], [ss])
        S.dve(lambda e: e.scalar_tensor_tensor(out=xn.ap[:r, :], in0=xi, scalar=ss.ap[:r, 1:2], in1=self.gb.ap[:r, :],
                                               op0=ALU.mult, op1=ALU.mult), [self.xt[i], ss, self.gb], [xn])
        return xn

    def norm_T(self, i, dstT, dst_t, col0, pbase=0):
        S = self.S
        r = rows(i)
        k = self.nrm_i % 2
        self.nrm_i += 1
        ss, xb = self.ss[k], self.xnb[k]
        xi = self.X.ap[:r, i, :]
        S.pool(lambda e: e.memset(ss.ap[:], 0.0), [], [ss])
        S.act(lambda e: e.activation(out=xb.ap[:r, :], in_=xi, func=AF.Square, accum_out=ss.ap[:r, 0:1]),
              [self.xt[i]], [xb, ss])
        S.dve(lambda e: e.tensor_scalar(out=ss.ap[:r, 1:2], in0=ss.ap[:r, 0:1], scalar1=1.0 / D, scalar2=EPS,
                                        op0=ALU.mult, op1=ALU.add), [ss], [ss])
        S.act(lambda e: e.activation(out=ss.ap[:r, 1:2], in_=ss.ap[:r, 1:2], func=AF.Sqrt), [ss], [ss])
        S.dve(lambda e: e.reciprocal(out=ss.ap[:r, 1:2], in_=ss.ap[:r, 1:2]), [ss], [ss])
        S.dve(lambda e: e.scalar_tensor_tensor(out=xb.ap[:r, :], in0=xi, scalar=ss.ap[:r, 1:2], in1=self.gb.ap[:r, :],
                                               op0=ALU.mult, op1=ALU.mult), [self.xt[i], ss, self.gb], [xb])
        PP = self.P[pbase]
        pb = self.Pb[pbase]
        for cc in range(8):
            S.pe(lambda e: e.transpose(pb[:, cc * 128:cc * 128 + r], xb.ap[:r, cc * 128:(cc + 1) * 128],
                                       self.identb.ap[:r, :r]), [xb, self.identb], [PP])
        S.act(lambda e: e.activation(out=dstT[:, :, col0:col0 + r],
                                     in_=pb[:, :].rearrange("p (c n) -> p c n", c=8)[:, :, :r],
                                     func=AF.Copy), [PP], [dst_t])

    def load_w(self, dram_ap, slot=None):
        if slot is None:
            slot = self.wrr % 4
            self.wrr += 1
        slot = self.wslot[slot]
        a, b = dram_ap.shape[1], dram_ap.shape[2]
        assert a * b <= 4096
        ap = slot.ap[:, :a * b].rearrange("p (a b) -> p a b", a=a)
        self.S.dma(slot, ap, dram_ap, q="pool")
        return slot, ap

    def mlp_layer(self, l):
        S = self.S
        self.arena_reset()
        xnT = self.av("xnT", [128, 8, TT], BF16)
        xnT_t = [S.view("xnT%d" % i, None) for i in range(NT)]
        hTs = [self.av("hT%d" % i, [128, 4, 256], BF16) for i in range(2)]
        hrs = [self.av("hr%d" % i, [128, 256], F32) for i in range(2)]
        self.load_gb(self.norm_mlp[l:l + 1, :])
        for i in range(NT):
            self.norm_T(i, xnT.ap, xnT_t[i], i * 128, pbase=(i % 2) * 2)
        wu_d = self.w_up[l].rearrange("(c p) n -> p c n", p=128)
        wd_d = self.w_down[l].rearrange("(c p) n -> p c n", p=128)
        groups = [(g * 256, 256, [2 * g, 2 * g + 1]) for g in range(8)] + [(TP, 64, [16])]
        gi = 0
        pend = None

        def emit_down(pd):
            (hT_, wd_t_, wd_, tiles_) = pd
            for ti, t in enumerate(tiles_):
                r = rows(t)
                for half in range(2):
                    P = self.P[2 + (ti * 2 + half)]
                    for j in range(4):
                        S.pe(lambda e: e.matmul(P.ap[:r, :], hT_.ap[:, j, ti * 128:ti * 128 + r],
                                                wd_[:, j, half * 512:(half + 1) * 512], start=(j == 0), stop=(j == 3)),
                             [hT_, wd_t_], [P])
                    S.dve(lambda e: e.tensor_tensor(out=self.X.ap[:r, t, half * 512:(half + 1) * 512],
                                                    in0=self.X.ap[:r, t, half * 512:(half + 1) * 512],
                                                    in1=P.ap[:r, :], op=ALU.add), [P, self.xt[t]], [self.xt[t]])

        for q in range(8):
            wu_t, wu = self.load_w(wu_d[:, :, q * 512:(q + 1) * 512])
            wd_t, wd = self.load_w(wd_d[:, q * 4:(q + 1) * 4, :])
            for (t0, nt, tiles) in groups:
                hT = hTs[gi % 2]
                gi += 1
                for j in range(4):
                    P = self.P[j % 2]
                    hr = hrs[j % 2]
                    for k in range(8):
                        S.pe(lambda e: e.matmul(P.ap[:, :nt], wu[:, k, j * 128:(j + 1) * 128], xnT.ap[:, k, t0:t0 + nt],
                                                start=(k == 0), stop=(k == 7)),
                             [wu_t] + [xnT_t[t] for t in tiles], [P])
                    S.act(lambda e: e.activation(out=hr.ap[:, :nt], in_=P.ap[:, :nt], func=AF.Relu), [P], [hr])
                    S.pool(lambda e: e.tensor_tensor(out=hT.ap[:, j, :nt], in0=hr.ap[:, :nt], in1=hr.ap[:, :nt], op=ALU.mult),
                           [hr], [hT])
                if pend is not None:
                    emit_down(pend)
                pend = (hT, wd_t, wd, tiles)
        emit_down(pend)

    def decl_even(self):
        I = self.inp
        O = self.outp
        self.norm_mix_even = I("norm_mix_even", [2, D])
        self.w_in_even = I("w_in_even", [2, D, IN_EVEN])
        self.s5_a_re = I("s5_a_re", [2, 32, 64])
        self.s5_a_im = I("s5_a_im", [2, 32, 64])
        self.s5_log_dt = I("s5_log_dt", [2, 32])
        self.s5_b_re = I("s5_b_re", [2, 32, 64, 16])
        self.s5_b_im = I("s5_b_im", [2, 32, 64, 16])
        self.s5_c_re = I("s5_c_re", [2, 32, 16, 64])
        self.s5_c_im = I("s5_c_im", [2, 32, 16, 64])
        self.s5_d = I("s5_d", [2, 512])
        self.s5_glu_w = I("s5_glu_w", [2, 32, 16, 32])
        self.s5_glu_b = I("s5_glu_b", [2, 32, 32])
        self.ssd_conv_w = I("ssd_conv_w", [2, 4, 2048])
        self.ssd_conv_b = I("ssd_conv_b", [2, 2048])
        self.ssd_dt_bias = I("ssd_dt_bias", [2, 16])
        self.ssd_a_log = I("ssd_a_log", [2, 16])
        self.ssd_d = I("ssd_d", [2, 16])
        self.ssd_norm = I("ssd_norm", [2, 1024])
        self.w_out_even = I("w_out_even", [2, 1536, 1024])
        self.state_s5 = I("state_s5", [2, 16, 32, 64, 2])
        self.state_ssd = I("state_ssd", [2, 16, 16, 64, 128])
        self.state_conv = I("state_conv", [2, 16, 3, 2048])
        self.o_s5p = O("s5_prompt", [2, 32, 64, 2])
        self.o_s5s = O("s5_sample", [2, 16, 32, 64, 2])
        self.o_ssdp = O("ssd_prompt", [2, 16, 64, 128])
        self.o_ssds = O("ssd_sample", [2, 16, 16, 64, 128])
        self.o_convp = O("conv_prompt", [2, 3, 2048])
        self.o_convs = O("conv_sample", [2, 16, 3, 2048])

    def tt(self, eng, out, a, b, op, R, W):
        return self.S.op(eng, lambda e: e.tensor_tensor(out=out, in0=a, in1=b, op=op), R, W)

    def ts(self, eng, out, a, s1, op0, R, W, s2=None, op1=None):
        if op1 is None:
            return self.S.op(eng, lambda e: e.tensor_scalar(out=out, in0=a, scalar1=s1, scalar2=None, op0=op0), R, W)
        return self.S.op(eng, lambda e: e.tensor_scalar(out=out, in0=a, scalar1=s1, scalar2=s2, op0=op0, op1=op1), R, W)

    def stt(self, eng, out, a, sc, b, op0, op1, R, W):
        return self.S.op(eng, lambda e: e.scalar_tensor_tensor(out=out, in0=a, scalar=sc, in1=b, op0=op0, op1=op1), R, W)

    def actf(self, out, a, func, R, W, bias=None, scale=None):
        kw = {}
        if bias is not None:
            kw["bias"] = bias
        if scale is not None:
            kw["scale"] = scale
        return self.S.act(lambda e: e.activation(out=out, in_=a, func=func, **kw), R, W)

    def mm(self, out, lhsT, rhs, R, W, start=True, stop=True):
        return self.S.pe(lambda e: e.matmul(out, lhsT, rhs, start=start, stop=stop), R, W)

    def tr(self, out, in_, ident, R, W):
        return self.S.pe(lambda e: e.transpose(out, in_, ident), R, W)

    def sincos(self, x, x_t, sin_out, cos_out, out_t, scr):
        dve = self.S.dve_e
        kf, ki, y = scr
        self.ts(dve, kf.ap, x, 1.0 / TWO_PI, ALU.mult, [x_t], [kf])
        self.S.dve(lambda e: e.tensor_copy(ki.ap, kf.ap), [kf], [ki])
        self.S.dve(lambda e: e.tensor_copy(kf.ap, ki.ap), [ki], [kf])
        for shift, o in ((0.0, sin_out), (math.pi / 2, cos_out)):
            self.stt(dve, y.ap, kf.ap, -TWO_PI, x, ALU.mult, ALU.add, [kf, x_t], [y])
            if shift:
                self.ts(dve, y.ap, y.ap, shift, ALU.add, [y], [y])
            for thr, opc, add in ((math.pi, ALU.is_gt, -TWO_PI), (-math.pi, ALU.is_lt, TWO_PI)):
                self.ts(dve, ki.ap.bitcast(F32), y.ap, thr, opc, [y], [ki])
                self.stt(dve, y.ap, ki.ap.bitcast(F32), add, y.ap, ALU.mult, ALU.add, [ki, y], [y])
            self.actf(o, y.ap, AF.Sin, [y], [out_t])

    def even_layer(self, l):
        S = self.S
        ii = l // 2
        dve, pool = S.dve_e, S.pool_e
        self.arena_reset()
        A = self.av
        P, Pb = self.P, self.Pb
        W = [A("W%d" % i, [128, 1024], F32) for i in range(6)]
        Wb = [w.ap.bitcast(BF16) for w in W]
        w0off = 0
        Rbig = self.arena[:, 0:4096].bitcast(F32)
        uT = A("uT", [128, 4, 256], BF16)
        zs = A("zs", [128, 2, 1024], BF16)
        xbcT = A("xbcT", [128, 16, 259], BF16)
        carry = A("carry", [128, 16, 3], BF16)
        dtr = A("dtr", [128, 2, 16], F32)
        dtT = A("dtT", [16, 64], F32)
        yaT = A("yaT", [128, 4, 256], BF16)
        ybT = A("ybT", [128, 8, 256], BF16)
        xnT = ybT
        Ecos = A("Ecos", [128, 16, 128], BF16)
        Esin = A("Esin", [128, 16, 128], BF16)
        Blre = A("Blre", [128, 16, 128], BF16)
        Blim = A("Blim", [128, 16, 128], BF16)
        Clre = A("Clre", [128, 16, 128], BF16)
        Clim = A("Clim", [128, 16, 128], BF16)
        Gv = A("Gv", [128, 4, 128], BF16)
        Gg = A("Gg", [128, 4, 128], BF16)
        gbr = A("gbr", [1, 4, 2, 128], BF16)
        sm = A("sm", [128, 16, 16], F32)
        hp = A("hp", [128, 2, 16], F32)
        HT = A("HT", [128, 1024], F32)
        hs = S.alias("hs", HT, HT.ap[:, 0:512].rearrange("p (b g r) -> p b g r", b=16, g=16))
        HTb = A("HTb", [128, 1024], BF16)
        ssdc = A("ssdc", [128, 6, 16], F32)
        cw = A("cw", [128, 16, 4], F32)
        cb = A("cb", [128, 16], F32)
        gncol = A("gncol", [128, 8], F32)
        dcol = A("dcol", [128, 4], F32)
        wdt = A("wdt", [128, 8, 16], BF16)
        dg = [A("dg%d" % i, [128, 128], BF16) for i in range(8)]
        tk = A("tk", [128, 8, 16], F32)
        xtok = A("xtok", [128, 1024], BF16)
        btok = A("btok", [128, 512], BF16)
        xdt = A("xdt", [128, 1024], BF16)
        xdd = A("xdd", [128, 1024], BF16)
        cbT = A("cbT", [128, 512], BF16)
        yn = A("yn", [128, 1024], BF16)
        hre = S.alias("hre", xtok, xtok.ap.rearrange("p (a b) -> p a b", a=8))
        him = S.alias("him", yn, yn.ap.rearrange("p (a b) -> p a b", a=8))
        ztk = S.alias("ztk", yn, yn.ap[:4, :])
        clast = S.alias("clast", W[4], W[4].ap[:, 0:768].rearrange("p (c b k) -> p c b k", c=16, b=16))
        hnat = S.alias("hnat", W[5], W[5].ap.rearrange("p (a b) -> p a b", a=8))

        SM = lambda j: sm.ap[:, j, :]
        ARE, AIM, DT, ZR, ZI, RC, SN, CS, LR, LI, FRE, FIM, T0, T1, T2, T3 = range(16)

        S.dma(sm, SM(ARE), self.s5_a_re[ii].rearrange("(gp g2) p -> (g2 p) gp", g2=2))
        S.dma(sm, SM(AIM), self.s5_a_im[ii].rearrange("(gp g2) p -> (g2 p) gp", g2=2))
        ldv = self.s5_log_dt[ii:ii + 1, :].rearrange("o (gp g2) -> o g2 gp", g2=2)
        for g2 in range(2):
            S.dma(sm, sm.ap[g2 * 64:(g2 + 1) * 64, DT, :], ldv[:, g2, :].partition_broadcast(64))
        self.actf(SM(DT), SM(DT), AF.Exp, [sm], [sm])
        self.tt(dve, SM(ZR), SM(ARE), SM(DT), ALU.mult, [sm], [sm])
        self.tt(dve, SM(ZI), SM(AIM), SM(DT), ALU.mult, [sm], [sm])
        self.actf(SM(RC), SM(ZR), AF.Exp, [sm], [sm])
        scr_s = (S.view("s1", W[0].ap[:, 0:16]), S.view("s2", W[0].ap[:, 16:32].bitcast(I32)), S.view("s3", W[0].ap[:, 32:48]))
        self.sincos(SM(ZI), sm, SM(SN), SM(CS), sm, scr_s)
        self.tt(dve, SM(LR), SM(RC), SM(CS), ALU.mult, [sm], [sm])
        self.tt(dve, SM(LI), SM(RC), SM(SN), ALU.mult, [sm], [sm])
        self.tt(dve, SM(T0), SM(ARE), SM(ARE), ALU.mult, [sm], [sm])
        self.tt(dve, SM(T1), SM(AIM), SM(AIM), ALU.mult, [sm], [sm])
        self.tt(dve, SM(T0), SM(T0), SM(T1), ALU.add, [sm], [sm])
        S.dve(lambda e: e.reciprocal(out=SM(T0), in_=SM(T0)), [sm], [sm])
        self.ts(dve, SM(T1), SM(LR), -1.0, ALU.add, [sm], [sm])
        self.tt(dve, SM(T2), SM(T1), SM(ARE), ALU.mult, [sm], [sm])
        self.tt(dve, SM(T3), SM(LI), SM(AIM), ALU.mult, [sm], [sm])
        self.tt(dve, SM(T2), SM(T2), SM(T3), ALU.add, [sm], [sm])
        self.tt(dve, SM(FRE), SM(T2), SM(T0), ALU.mult, [sm], [sm])
        self.tt(dve, SM(T2), SM(LI), SM(ARE), ALU.mult, [sm], [sm])
        self.tt(dve, SM(T3), SM(T1), SM(AIM), ALU.mult, [sm], [sm])
        self.tt(dve, SM(T2), SM(T2), SM(T3), ALU.subtract, [sm], [sm])
        self.tt(dve, SM(FIM), SM(T2), SM(T0), ALU.mult, [sm], [sm])
        iot = S.view("iot", W[5].ap[:, 0:128])
        ioti = S.view("ioti", W[5].ap[:, 128:256].bitcast(I32))
        S.pool(lambda e: e.iota(ioti.ap, pattern=[[1, 128]], base=1, channel_multiplier=0), [], [ioti])
        S.dve(lambda e: e.tensor_copy(iot.ap, ioti.ap), [ioti], [iot])
        v3 = lambda t: t.ap.rearrange("p (a b) -> p a b", a=8)
        S.barrier()
        ang = S.view("ang", v3(W[0]))
        scr = (S.view("k1", v3(W[1])), S.view("k2", v3(W[2]).bitcast(I32)), S.view("k3", v3(W[3])))
        for half in range(2):
            g0 = half * 8
            self.tt(dve, ang.ap, sm.ap[:, ZI, g0:g0 + 8].unsqueeze(2).to_broadcast([128, 8, 128]),
                    iot.ap.unsqueeze(1).to_broadcast([128, 8, 128]), ALU.mult, [sm, iot], [ang, W[0]])
            self.sincos(ang.ap, ang, Esin.ap[:, g0:g0 + 8, :], Ecos.ap[:, g0:g0 + 8, :], Esin, scr)
        S.barrier()
        bre = S.view("bre", W[4].ap[:, 0:256].rearrange("p (a b) -> p a b", a=16))
        bim = S.view("bim", W[4].ap[:, 256:512].rearrange("p (a b) -> p a b", a=16))
        bbr = S.view("bbr", W[4].ap[:, 512:768].rearrange("p (a b) -> p a b", a=16))
        bbi = S.view("bbi", W[5].ap[:, 256:512].rearrange("p (a b) -> p a b", a=16))
        btmp = S.view("btmp", W[5].ap[:, 512:768].rearrange("p (a b) -> p a b", a=16))
        S.dma(bre, bre.ap, self.s5_b_re[ii].rearrange("(gp g2) p c -> (g2 p) gp c", g2=2))
        S.dma(bim, bim.ap, self.s5_b_im[ii].rearrange("(gp g2) p c -> (g2 p) gp c", g2=2))
        fre_b = sm.ap[:, FRE, :].unsqueeze(2).to_broadcast([128, 16, 16])
        fim_b = sm.ap[:, FIM, :].unsqueeze(2).to_broadcast([128, 16, 16])
        self.tt(dve, bbr.ap, bre.ap, fre_b, ALU.mult, [bre, sm], [bbr])
        self.tt(dve, btmp.ap, bim.ap, fim_b, ALU.mult, [bim, sm], [btmp])
        self.tt(dve, bbr.ap, bbr.ap, btmp.ap, ALU.subtract, [bbr, btmp], [bbr])
        self.tt(dve, bbi.ap, bim.ap, fre_b, ALU.mult, [bim, sm], [bbi])
        self.tt(dve, btmp.ap, bre.ap, fim_b, ALU.mult, [bre, sm], [btmp])
        self.tt(dve, bbi.ap, bbi.ap, btmp.ap, ALU.add, [bbi, btmp], [bbi])
        BBv = [S.view("BB%d" % i, W[i].ap) for i in range(2)]
        for (src, dstB) in ((bbr, Blre), (bbi, Blim)):
            for half in range(2):
                BB = BBv[half]
                S.pool(lambda e: e.memset(BB.ap, 0.0), [], [BB])
                BB6 = BB.ap.rearrange("q (cc gl a b c) -> q cc gl a b c", cc=2, gl=4, a=4, b=2)
                s5v = src.ap.rearrange("q (cc gl) c -> q cc gl c", gl=4)
                for gl in range(4):
                    for g2 in range(2):
                        S.pool(lambda e: e.tensor_copy(BB6[g2 * 64:(g2 + 1) * 64, :, gl, gl, g2, :],
                                                       s5v[g2 * 64:(g2 + 1) * 64, half * 2:half * 2 + 2, gl, :]), [src], [BB])
                for j in range(2):
                    PP = P[half * 2 + j]
                    for k in range(4):
                        self.tr(PP.ap[:, k * 128:(k + 1) * 128], BB.ap[:, (j * 4 + k) * 128:(j * 4 + k + 1) * 128], self.ident.ap, [BB, self.ident], [PP])
                    gp0 = half * 8 + j * 4
                    self.actf(dstB.ap[:, gp0:gp0 + 4, :], PP.ap.rearrange("p (a b) -> p a b", a=4), AF.Copy, [PP], [dstB])
        S.barrier()
        for (csrc, dstC, sgn) in ((self.s5_c_re, Clre, 1.0), (self.s5_c_im, Clim, -1.0)):
            cn = S.view("cn", Rbig[:, 0:512].rearrange("p (a b c) -> p a b c", a=4, b=2))
            for dup in range(2):
                S.dma(cn, cn.ap[:, :, dup, :], csrc[ii].rearrange("(cc g8) c p -> (g8 c) cc p", g8=8))
            S.pool(lambda e: e.memset(dstC.ap, 0.0), [], [dstC])
            PP = P[4]
            for cc in range(4):
                self.tr(PP.ap[:, cc * 128:(cc + 1) * 128], cn.ap[:, cc, :, :].rearrange("p a b -> p (a b)"), self.ident.ap, [cn, self.ident], [PP])
            for cc in range(4):
                for gl in range(4):
                    for g2 in range(2):
                        col = (2 * gl + g2) * 16
                        self.actf(dstC.ap[g2 * 64:(g2 + 1) * 64, cc * 4 + gl, col:col + 16],
                                  PP.ap[g2 * 64:(g2 + 1) * 64, cc * 128 + col:cc * 128 + col + 16], AF.Copy, [PP], [dstC], scale=sgn)
            S.barrier()
        gwn = S.view("gwn", W[2].ap[:, 0:128].rearrange("p (a b) -> p a b", a=4))
        bd = S.view("bd", W[2].ap[:, 128:256].rearrange("p (a b) -> p a b", a=8))
        S.dma(gwn, gwn.ap, self.s5_glu_w[ii].rearrange("(cc g8) c e -> (g8 c) cc e", g8=8))
        S.pool(lambda e: e.memset(bd.ap, 1.0), [], [bd])
        S.pool(lambda e: e.affine_select(out=bd.ap, in_=bd.ap, compare_op=ALU.is_ge, fill=0.0, base=0,
                                         pattern=[[-16, 8], [0, 16]], channel_multiplier=1), [bd], [bd])
        S.pool(lambda e: e.affine_select(out=bd.ap, in_=bd.ap, compare_op=ALU.is_ge, fill=0.0, base=15,
                                         pattern=[[16, 8], [0, 16]], channel_multiplier=-1), [bd], [bd])
        for cc in range(4):
            self.tt(dve, Gv.ap[:, cc, :].rearrange("p (a b) -> p a b", a=8), bd.ap,
                    gwn.ap[:, cc, 0:16].unsqueeze(1).to_broadcast([128, 8, 16]), ALU.mult, [bd, gwn], [Gv])
            self.tt(dve, Gg.ap[:, cc, :].rearrange("p (a b) -> p a b", a=8), bd.ap,
                    gwn.ap[:, cc, 16:32].unsqueeze(1).to_broadcast([128, 8, 16]), ALU.mult, [bd, gwn], [Gg])
        gbf = S.view("gbf", W[3].ap[0:1, 0:1024].rearrange("p (a b) -> p a b", a=32))
        S.dma(gbf, gbf.ap, self.s5_glu_b[ii:ii + 1, :, :])
        for vg in range(2):
            S.dve(lambda e: e.tensor_copy(gbr.ap[:, :, vg, :].rearrange("p a (b c) -> p a b c", b=8),
                                          gbf.ap.rearrange("p (a b) c -> p a b c", a=4)[:, :, :, vg * 16:(vg + 1) * 16]), [gbf], [gbr])
        S.dma(dcol, dcol.ap, self.s5_d[ii:ii + 1, :].rearrange("o (cc q) -> q (o cc)", q=128))
        S.dma(ssdc, ssdc.ap[:, 0, :], self.ssd_dt_bias[ii:ii + 1, :].partition_broadcast(128))
        S.dma(ssdc, ssdc.ap[:, 1, :], self.ssd_a_log[ii:ii + 1, :].partition_broadcast(128))
        S.dma(ssdc, ssdc.ap[:, 2, :], self.ssd_d[ii:ii + 1, :].partition_broadcast(128))
        self.actf(ssdc.ap[:, 1, :], ssdc.ap[:, 1, :], AF.Exp, [ssdc], [ssdc])
        self.ts(dve, ssdc.ap[:, 1, :], ssdc.ap[:, 1, :], -1.0, ALU.mult, [ssdc], [ssdc])
        for k in range(4):
            S.dma(cw, cw.ap[:, :, k], self.ssd_conv_w[ii, k:k + 1, :].rearrange("o (c q) -> q (o c)", q=128))
        S.dma(cb, cb.ap, self.ssd_conv_b[ii:ii + 1, :].rearrange("o (c q) -> q (o c)", q=128))
        S.dma(gncol, gncol.ap, self.ssd_norm[ii:ii + 1, :].rearrange("o (c q) -> q (o c)", q=128))
        win = self.w_in_even[ii].rearrange("(c p) n -> p c n", p=128)
        S.dma(wdt, wdt.ap, win[:, :, 3584:3600], q="pool")
        wout = self.w_out_even[ii].rearrange("(c p) n -> p c n", p=128)
        self.load_gb(self.norm_mix_even[ii:ii + 1, :])
        S.barrier()
        self._ev = dict(locals())
        for sc in range(9):
            self.even_superchunk(l, sc)

    def even_superchunk(self, l, sc):
        S = self.S
        E = self._ev
        ii = l // 2
        dve, pool = S.dve_e, S.pool_e
        P, Pb = self.P, self.Pb
        W, Wb, Rbig = E["W"], E["Wb"], E["Rbig"]
        xnT, uT, zs, xbcT, carry, dtr, dtT, yaT, ybT = (E[k] for k in ("xnT", "uT", "zs", "xbcT", "carry", "dtr", "dtT", "yaT", "ybT"))
        cw, cb, wdt, dg, clast, win, wout = (E[k] for k in ("cw", "cb", "wdt", "dg", "clast", "win", "wout"))
        HT, HTb, hnat, hp, tk, ztk = (E[k] for k in ("HT", "HTb", "hnat", "hp", "tk", "ztk"))
        sample = (sc == 8)
        ntok = 64 if sample else 256
        tiles = [16] if sample else [2 * sc, 2 * sc + 1]
        xv = xbcT.ap[:, :, 0:112].rearrange("p c (b k) -> p c b k", k=7)
        zT = zs.ap[:, 0, 0:512].rearrange("p (c n) -> p c n", c=8)
        for j, t in enumerate(tiles):
            self.norm_T(t, xnT.ap, xnT, j * 128, pbase=(j % 2) * 2)
        if sample:
            stc = Rbig[:48, :]
            S.dma_fn(lambda e: e.dma_start(out=stc, in_=self.state_conv[ii].rearrange("b k c -> (b k) c")), [], [W[0], W[1]])
            for half in range(2):
                PP = P[half]
                for c8 in range(8):
                    c = half * 8 + c8
                    self.tr(PP.ap[:, c8 * 48:(c8 + 1) * 48], stc[:, c * 128:(c + 1) * 128], self.ident.ap[:48, :48], [W[0], W[1], self.ident], [PP])
                self.actf(xv[:, half * 8:half * 8 + 8, :, 0:3], PP.ap[:, 0:384].rearrange("p (c b k) -> p c b k", c=8, b=16), AF.Copy, [PP], [xbcT])
        elif sc == 0:
            S.pool(lambda e: e.memset(xbcT.ap[:, :, 0:3], 0.0), [], [xbcT])
            S.pool(lambda e: e.memset(HT.ap, 0.0), [], [HT])
            S.pool(lambda e: e.memset(HTb.ap, 0.0), [], [HTb])
            S.pool(lambda e: e.memset(hp.ap, 0.0), [], [hp])
        else:
            S.pool(lambda e: e.tensor_copy(xbcT.ap[:, :, 0:3], carry.ap), [carry], [xbcT])
        wt, w = self.load_w(win[:, :, 0:512])
        for m in range(4):
            PP = P[m % 2]
            for k in range(8):
                self.mm(PP.ap[:, :ntok], w[:, k, m * 128:(m + 1) * 128], xnT.ap[:, k, :ntok], [wt, xnT], [PP], start=(k == 0), stop=(k == 7))
            self.actf(uT.ap[:, m, :ntok], PP.ap[:, :ntok], AF.Copy, [PP], [uT])
        for piece in range(2):
            wt, w = self.load_w(win[:, :, 512 + piece * 512:1024 + piece * 512])
            if not sample:
                for j in range(2):
                    PP = P[2 + j]
                    for k in range(8):
                        self.mm(PP.ap[:, :], xnT.ap[:, k, j * 128:(j + 1) * 128], w[:, k, :], [wt, xnT], [PP], start=(k == 0), stop=(k == 7))
                    self.actf(zs.ap[:, j, piece * 512:(piece + 1) * 512], PP.ap[:, :], AF.Silu, [PP], [zs])
            else:
                for m in range(4):
                    PP = P[2 + m % 2]
                    for k in range(8):
                        self.mm(PP.ap[:, :64], w[:, k, m * 128:(m + 1) * 128], xnT.ap[:, k, :64], [wt, xnT], [PP], start=(k == 0), stop=(k == 7))
                    self.actf(zT[:, piece * 4 + m, :], PP.ap[:, :64], AF.Silu, [PP], [zs])
        for piece in range(4):
            wt, w = self.load_w(win[:, :, 1536 + piece * 512:2048 + piece * 512])
            for m in range(4):
                c = piece * 4 + m
                PP = P[4 + m % 2]
                for k in range(8):
                    self.mm(PP.ap[:, :ntok], w[:, k, m * 128:(m + 1) * 128], xnT.ap[:, k, :ntok], [wt, xnT], [PP], start=(k == 0), stop=(k == 7))
                if sample:
                    pv = PP.ap[:, :64].rearrange("p (b k) -> p b k", k=4)
                    self.actf(xv[:, c, :, 3:7], pv, AF.Copy, [PP], [xbcT])
                    self.actf(clast.ap[:, c, :, :], pv[:, :, 1:4], AF.Copy, [PP], [clast])
                else:
                    self.actf(xbcT.ap[:, c, 3:3 + 256], PP.ap[:, :256], AF.Copy, [PP], [xbcT])
                    if sc == 7:
                        self.actf(clast.ap[:, c, 0, :], PP.ap[:, 253:256], AF.Copy, [PP], [clast])
        if sc == 7 or sample:
            nr = 48 if sample else 3
            cl2 = clast.ap.rearrange("p c b k -> p c (b k)")
            cout = Rbig[:nr, :]
            for half in range(2):
                PP = P[half]
                for c8 in range(8):
                    c = half * 8 + c8
                    self.tr(PP.ap[:nr, c8 * 64:c8 * 64 + 64].bitcast(F32)[:, 0:64] if False else PP.ap[:nr, c8 * 128:(c8 + 1) * 128] if c8 < 4 else P[2 + half].ap[:nr, (c8 - 4) * 128:(c8 - 3) * 128],
                            cl2[:, c, 0:nr], self.ident.ap, [clast, self.ident], [PP, P[2 + half]])
                self.actf(cout[:, half * 1024:half * 1024 + 512], PP.ap[:nr, :], AF.Copy, [PP], [W[0], W[1]])
                self.actf(cout[:, half * 1024 + 512:half * 1024 + 1024], P[2 + half].ap[:nr, :], AF.Copy, [P[2 + half]], [W[0], W[1]])
            dst = self.o_convs[ii].rearrange("b k c -> (b k) c") if sample else self.o_convp[ii]
            S.dma_fn(lambda e: e.dma_start(out=dst, in_=cout), [W[0], W[1]], [])
            S.out_tokens.append(None)
        P7 = P[7]
        if not sample:
            for j in range(2):
                for k in range(8):
                    self.mm(P7.ap[:, j * 16:(j + 1) * 16], xnT.ap[:, k, j * 128:(j + 1) * 128], wdt.ap[:, k, :], [wdt, xnT], [P7], start=(k == 0), stop=(k == 7))
            self.actf(dtr.ap.rearrange("p a b -> p (a b)"), P7.ap[:, 0:32], AF.Copy, [P7], [dtr])
        else:
            for k in range(8):
                self.mm(P7.ap[:16, :64], wdt.ap[:, k, :], xnT.ap[:, k, :64], [wdt, xnT], [P7], start=(k == 0), stop=(k == 7))
            self.actf(dtT.ap, P7.ap[:16, :64], AF.Copy, [P7], [dtT])
        if not sample:
            S.pool(lambda e: e.tensor_copy(carry.ap, xbcT.ap[:, :, 256:259]), [xbcT], [carry])
        for c in range(16):
            PP = P[c % 2]
            dgc = dg[(c % 2) * 4:(c % 2) * 4 + 4]
            for k in range(4):
                self.actf(dgc[k].ap, self.identb.ap, AF.Copy, [self.identb, cw], [dgc[k]], scale=cw.ap[:, c, k:k + 1])
            for k in range(4):
                if sample:
                    self.mm(PP.ap[:, :64].rearrange("p (b k) -> p b k", k=4), dgc[k].ap, xv[:, c, :, k:k + 4], [dgc[k], xbcT], [PP], start=(k == 0), stop=(k == 3))
                else:
                    self.mm(PP.ap[:, :256], dgc[k].ap, xbcT.ap[:, c, k:k + 256], [dgc[k], xbcT], [PP], start=(k == 0), stop=(k == 3))
            if sample:
                self.actf(xv[:, c, :, 3:7], PP.ap[:, :64].rearrange("p (b k) -> p b k", k=4), AF.Silu, [PP, cb], [xbcT], bias=cb.ap[:, c:c + 1])
            else:
                self.actf(xbcT.ap[:, c, 3:259], PP.ap[:, :256], AF.Silu, [PP, cb], [xbcT], bias=cb.ap[:, c:c + 1])
        if "s5" in self.parts:
            if not sample:
                for ch in range(2):
                    self.s5_chunk(ch * 128)
            else:
                self.s5_sample(ii)
        else:
            S.pool(lambda e: e.memset(yaT.ap, 0.0), [], [yaT])
        if "ssd" in self.parts:
            if not sample:
                for ch in range(2):
                    t0 = ch * 128
                    self.ssd_chunk(128, lambda c, t0=t0: xbcT.ap[:, c, 3 + t0:3 + t0 + 128], dtr.ap[:, ch, :], dtr, zs.ap[:, ch, :], zs, t0)
            else:
                for b in range(16):
                    S.dma(hnat, hnat.ap, self.state_ssd[ii, b].rearrange("(hp h2) p n -> (h2 p) hp n", h2=2))
                    for half in range(2):
                        PP = P[4 + half]
                        for k in range(4):
                            self.tr(PP.ap[:, k * 128:(k + 1) * 128], hnat.ap[:, half * 4 + k, :], self.ident.ap, [hnat, self.ident], [PP])
                        self.actf(HT.ap[:, half * 512:(half + 1) * 512], PP.ap, AF.Copy, [PP], [HT])
                    self.actf(HTb.ap, HT.ap, AF.Copy, [HT], [HTb])
                    for c in range(8):
                        self.tr(Pb[6][:4, c * 128:(c + 1) * 128], zT[:, c, 4 * b:4 * b + 4], self.identb.ap, [zs, self.identb], [P[6]])
                    self.actf(ztk.ap, Pb[6][:4, :], AF.Copy, [P[6]], [ztk])
                    self.tr(P[7].ap[:4, 32:48], dtT.ap[:16, 4 * b:4 * b + 4], self.ident.ap[:16, :16], [dtT, self.ident], [P[7]])
                    self.actf(tk.ap[:4, 6, :], P[7].ap[:4, 32:48], AF.Copy, [P[7]], [tk])
                    self.ssd_chunk(4, lambda c, b=b: xv[:, c, b, 3:7], tk.ap[:4, 6, :], tk, ztk.ap, ztk, 4 * b)
                    self.ssd_state_out(self.o_ssds[ii, b])
        else:
            S.pool(lambda e: e.memset(ybT.ap, 0.0), [], [ybT])
        pcs = [self.load_w(wout[:, j * 4:(j + 1) * 4, :]) for j in range(3)]
        for j, t in enumerate(tiles):
            r = rows(t)
            for half in range(2):
                PP = P[2 * j + half]
                for k in range(12):
                    lhsT = yaT.ap[:, k, j * 128:j * 128 + r] if k < 4 else ybT.ap[:, k - 4, j * 128:j * 128 + r]
                    pt, pw = pcs[k // 4]
                    self.mm(PP.ap[:r, :], lhsT, pw[:, k % 4, half * 512:(half + 1) * 512], [yaT, ybT, pt], [PP], start=(k == 0), stop=(k == 11))
                xs_ = self.X.ap[:r, t, half * 512:(half + 1) * 512]
                self.tt(dve, xs_, xs_, PP.ap[:r, :], ALU.add, [PP, self.xt[t]], [self.xt[t]])
        if sc == 7:
            if "s5" in self.parts:
                for r_ in range(2):
                    S.dma_out(self.o_s5p[ii, :, :, r_].rearrange("(gp g2) p -> (g2 p) gp", g2=2), hp.ap[:, r_, :], hp)
            if "ssd" in self.parts:
                self.ssd_state_out(self.o_ssdp[ii])

    def ssd_state_out(self, dram_ap):
        S = self.S
        E = self._ev
        HT, hnat = E["HT"], E["hnat"]
        P = self.P
        for half in range(2):
            PP = P[4 + half]
            for k in range(4):
                hpi = half * 4 + k
                self.tr(PP.ap[:, k * 128:(k + 1) * 128], HT.ap[:, hpi * 128:(hpi + 1) * 128], self.ident.ap, [HT, self.ident], [PP])
            self.actf(hnat.ap[:, half * 4:half * 4 + 4, :], PP.ap.rearrange("p (a b) -> p a b", a=4), AF.Copy, [PP], [hnat])
        S.dma_out(dram_ap.rearrange("(hp h2) p n -> (h2 p) hp n", h2=2), hnat.ap, hnat)

    def s5_tail(self, cc, rhs, T, t0):
        E = self._ev
        S = self.S
        dve = S.dve_e
        P = self.P
        Clre, Clim, Gv, Gg, gbr, dcol, uT, yaT, xdt, xdd = (E[k] for k in ("Clre", "Clim", "Gv", "Gg", "gbr", "dcol", "uT", "yaT", "xdt", "xdd"))
        hre, him = E["hre"], E["him"]
        PY = P[4 + cc % 2]
        for jj, (gp, hr_ap, hi_ap) in enumerate(rhs):
            self.mm(PY.ap[:, :T], Clre.ap[:, gp, :], hr_ap, [Clre, hre], [PY], start=(jj == 0), stop=False)
            self.mm(PY.ap[:, :T], Clim.ap[:, gp, :], hi_ap, [Clim, him], [PY], start=False, stop=(jj == 3))
        ysb = xdt.ap[:, (cc % 2) * 128:(cc % 2) * 128 + T]
        self.stt(dve, ysb, uT.ap[:, cc, t0:t0 + T], dcol.ap[:, cc:cc + 1], PY.ap[:, :T], ALU.mult, ALU.add, [uT, dcol, PY], [xdt])
        PV, PG = P[6], P[7]
        self.mm(PV.ap[:, :T], Gv.ap[:, cc, :], ysb, [Gv, xdt], [PV], start=True, stop=False)
        self.mm(PV.ap[:, :T], gbr.ap[0:1, cc, 0, :], self.onesb.ap[0:1, :T], [gbr, self.onesb], [PV], start=False, stop=True)
        self.mm(PG.ap[:, :T], Gg.ap[:, cc, :], ysb, [Gg, xdt], [PG], start=True, stop=False)
        self.mm(PG.ap[:, :T], gbr.ap[0:1, cc, 1, :], self.onesb.ap[0:1, :T], [gbr, self.onesb], [PG], start=False, stop=True)
        sg = xdd.ap[:, (cc % 2) * 128:(cc % 2) * 128 + T]
        self.actf(sg, PG.ap[:, :T], AF.Sigmoid, [PG], [xdd])
        self.tt(dve, yaT.ap[:, cc, t0:t0 + T], PV.ap[:, :T], sg, ALU.mult, [PV, xdd], [yaT])

    def s5_chunk(self, t0):
        E = self._ev
        S = self.S
        dve, pool = S.dve_e, S.pool_e
        P = self.P
        W, sm, hp, hre, him, uT = (E[k] for k in ("W", "sm", "hp", "hre", "him", "uT"))
        Ecos, Esin, Blre, Blim = (E[k] for k in ("Ecos", "Esin", "Blre", "Blim"))
        RC = E["RC"]
        v3 = lambda t: t.ap.rearrange("p (a b) -> p a b", a=8)
        for half in range(2):
            g0 = half * 8
            for j in range(8):
                gp = g0 + j
                cc = gp // 4
                self.mm(P[j // 4].ap[:, (j % 4) * 128:(j % 4 + 1) * 128], Blre.ap[:, gp, :], uT.ap[:, cc, t0:t0 + 128], [Blre, uT], [P[j // 4]])
                self.mm(P[2 + j // 4].ap[:, (j % 4) * 128:(j % 4 + 1) * 128], Blim.ap[:, gp, :], uT.ap[:, cc, t0:t0 + 128], [Blim, uT], [P[2 + j // 4]])
            bre = self.PS[:, 0:2, :].rearrange("p a (b c) -> p (a b) c", c=128)
            bim = self.PS[:, 2:4, :].rearrange("p a (b c) -> p (a b) c", c=128)
            ec, es = Ecos.ap[:, g0:g0 + 8, :], Esin.ap[:, g0:g0 + 8, :]
            self.tt(dve, v3(W[0]), bre, ec, ALU.mult, [P[0], P[1], Ecos], [W[0]])
            self.tt(dve, v3(W[1]), bim, es, ALU.mult, [P[2], P[3], Esin], [W[1]])
            self.tt(dve, v3(W[2]), bim, ec, ALU.mult, [P[2], P[3], Ecos], [W[2]])
            self.tt(dve, v3(W[3]), bre, es, ALU.mult, [P[0], P[1], Esin], [W[3]])
            self.tt(pool, W[0].ap, W[0].ap, W[1].ap, ALU.add, [W[0], W[1]], [W[0]])
            self.tt(pool, W[2].ap, W[2].ap, W[3].ap, ALU.subtract, [W[2], W[3]], [W[2]])
            for j in range(8):
                gp = g0 + j
                rb = sm.ap[:, RC, gp:gp + 1].to_broadcast([128, 128])
                S.dve(lambda e: e.tensor_tensor_scan(out=v3(W[1])[:, j, :], data0=rb, data1=v3(W[0])[:, j, :], initial=hp.ap[:, 0, gp:gp + 1],
                                                     op0=ALU.mult, op1=ALU.add), [W[0], sm, hp], [W[1]])
                S.dve(lambda e: e.tensor_tensor_scan(out=v3(W[3])[:, j, :], data0=rb, data1=v3(W[2])[:, j, :], initial=hp.ap[:, 1, gp:gp + 1],
                                                     op0=ALU.mult, op1=ALU.add), [W[2], sm, hp], [W[3]])
            self.tt(dve, v3(W[0]), v3(W[1]), ec, ALU.mult, [W[1], Ecos], [W[0]])
            self.tt(pool, v3(W[2]), v3(W[3]), es, ALU.mult, [W[3], Esin], [W[2]])
            self.tt(pool, hre.ap, v3(W[0]), v3(W[2]), ALU.subtract, [W[0], W[2]], [hre])
            self.tt(dve, v3(W[4]), v3(W[3]), ec, ALU.mult, [W[3], Ecos], [W[4]])
            self.tt(pool, v3(W[5]), v3(W[1]), es, ALU.mult, [W[1], Esin], [W[5]])
            self.tt(pool, him.ap, v3(W[4]), v3(W[5]), ALU.add, [W[4], W[5]], [him])
            self.tt(dve, hp.ap[:, 0, g0:g0 + 8], v3(W[0])[:, :, 127], v3(W[2])[:, :, 127], ALU.subtract, [W[0], W[2]], [hp])
            self.tt(dve, hp.ap[:, 1, g0:g0 + 8], v3(W[4])[:, :, 127], v3(W[5])[:, :, 127], ALU.add, [W[4], W[5]], [hp])
            for ccl in range(2):
                cc = half * 2 + ccl
                rhs = [(cc * 4 + jj, hre.ap[:, cc * 4 + jj - g0, :], him.ap[:, cc * 4 + jj - g0, :]) for jj in range(4)]
                self.s5_tail(cc, rhs, 128, t0)

    def s5_sample(self, ii):
        E = self._ev
        S = self.S
        dve, pool = S.dve_e, S.pool_e
        P = self.P
        W, sm, hs, hre, him, uT = (E[k] for k in ("W", "sm", "hs", "hre", "him", "uT"))
        Blre, Blim = E["Blre"], E["Blim"]
        LR, LI = E["LR"], E["LI"]
        for b in range(16):
            S.dma(hs, hs.ap[:, b, :, :], self.state_s5[ii, b].rearrange("(gp g2) p r -> (g2 p) gp r", g2=2))
        hsr = hs.ap[:, :, :, 0].rearrange("p b g -> p g b")
        hsi = hs.ap[:, :, :, 1].rearrange("p b g -> p g b")
        for gp in range(16):
            cc = gp // 4
            self.mm(P[gp // 8].ap[:, (gp % 8) * 64:(gp % 8 + 1) * 64], Blre.ap[:, gp, :], uT.ap[:, cc, 0:64], [Blre, uT], [P[gp // 8]])
            self.mm(P[2 + gp // 8].ap[:, (gp % 8) * 64:(gp % 8 + 1) * 64], Blim.ap[:, gp, :], uT.ap[:, cc, 0:64], [Blim, uT], [P[2 + gp // 8]])
        bre = self.PS[:, 0:2, :].rearrange("p a (g b t) -> p (a g) b t", b=16, t=4)
        bim = self.PS[:, 2:4, :].rearrange("p a (g b t) -> p (a g) b t", b=16, t=4)
        hall_re = hre.ap.rearrange("p a b -> p (a b)").rearrange("p (g b t) -> p g b t", g=16, b=16)
        hall_im = him.ap.rearrange("p a b -> p (a b)").rearrange("p (g b t) -> p g b t", g=16, b=16)
        wv = lambda i: W[i].ap[:, 0:256].rearrange("p (g b) -> p g b", g=16)
        lr = sm.ap[:, LR, :].unsqueeze(2).to_broadcast([128, 16, 16])
        li = sm.ap[:, LI, :].unsqueeze(2).to_broadcast([128, 16, 16])
        for t in range(4):
            self.tt(dve, wv(0), hsr, lr, ALU.mult, [hs, sm], [W[0]])
            self.tt(dve, wv(1), hsi, li, ALU.mult, [hs, sm], [W[1]])
            self.tt(dve, wv(2), hsi, lr, ALU.mult, [hs, sm], [W[2]])
            self.tt(dve, wv(3), hsr, li, ALU.mult, [hs, sm], [W[3]])
            self.tt(dve, wv(0), wv(0), wv(1), ALU.subtract, [W[0], W[1]], [W[0]])
            self.tt(dve, wv(2), wv(2), wv(3), ALU.add, [W[2], W[3]], [W[2]])
            self.tt(dve, hsr, wv(0), bre[:, :, :, t], ALU.add, [W[0], P[0], P[1]], [hs])
            self.tt(dve, hsi, wv(2), bim[:, :, :, t], ALU.add, [W[2], P[2], P[3]], [hs])
            self.actf(hall_re[:, :, :, t], hsr, AF.Copy, [hs], [hre])
            self.actf(hall_im[:, :, :, t], hsi, AF.Copy, [hs], [him])
        hr2 = hre.ap.rearrange("p a b -> p (a b)").rearrange("p (g n) -> p g n", g=16)
        hi2 = him.ap.rearrange("p a b -> p (a b)").rearrange("p (g n) -> p g n", g=16)
        for cc in range(4):
            rhs = [(cc * 4 + jj, hr2[:, cc * 4 + jj, :], hi2[:, cc * 4 + jj, :]) for jj in range(4)]
            self.s5_tail(cc, rhs, 64, 0)
        for b in range(16):
            S.dma_out(self.o_s5s[ii, b].rearrange("(gp g2) p r -> (g2 p) gp r", g2=2), hs.ap[:, b, :, :], hs)

    def ssd_chunk(self, T, xcv, dt_tok, dt_t, z_tok, z_t, ycol0):
        E = self._ev
        S = self.S
        dve, pool = S.dve_e, S.pool_e
        P, Pb = self.P, self.Pb
        W, Wb, Rbig = E["W"], E["Wb"], E["Rbig"]
        tk, ssdc, xbcT, xtok, btok, xdt, xdd, cbT, yn, HT, HTb, ybT, gncol = (E[k] for k in (
            "tk", "ssdc", "xbcT", "xtok", "btok", "xdt", "xdd", "cbT", "yn", "HT", "HTb", "ybT", "gncol"))
        tkv = lambda j: tk.ap[:T, j, :]
        b3 = lambda ap: ap.unsqueeze(2).to_broadcast([T, 16, 64])
        h3 = lambda ap: ap.rearrange("p (h d) -> p h d", h=16)
        self.tt(dve, tkv(0), dt_tok, ssdc.ap[:T, 0, :], ALU.add, [dt_t, ssdc], [tk])
        self.actf(tkv(0), tkv(0), AF.Exp, [tk], [tk])
        self.actf(tkv(0), tkv(0), AF.Ln, [tk], [tk], bias=1.0)
        self.tt(dve, tkv(1), tkv(0), ssdc.ap[:T, 1, :], ALU.mult, [tk, ssdc], [tk])
        self.ts(dve, tkv(5), tkv(1), -1.0, ALU.mult, [tk], [tk])
        P7 = P[7]
        self.mm(P7.ap[:T, 0:16], self.tri.ap[:T, :T], tkv(1), [self.tri, tk], [P7])
        self.mm(P7.ap[:, 16:32], self.onesf.ap[:T, :], tkv(1), [self.onesf, tk], [P7])
        self.actf(tkv(2), P7.ap[:T, 0:16], AF.Copy, [P7], [tk])
        self.actf(tkv(3), P7.ap[:T, 0:16], AF.Exp, [P7], [tk])
        self.actf(tk.ap[:, 7, :], P7.ap[:, 16:32], AF.Exp, [P7], [tk])
        self.tt(dve, tkv(4), P7.ap[:T, 16:32], tkv(2), ALU.subtract, [P7, tk], [tk])
        self.actf(tkv(4), tkv(4), AF.Exp, [tk], [tk])
        R3 = Rbig[:T, :16 * T].rearrange("p (h l) -> p h l", h=16)
        self.tt(dve, R3, tkv(1).unsqueeze(2).to_broadcast([T, 16, T]), self.tri.ap[:T, :T].unsqueeze(1).to_broadcast([T, 16, T]),
                ALU.mult, [tk, self.tri], [W[0], W[1]])
        hpm = min(16, 512 // T)
        PSLT = self.PS[:T, 0:4, :].rearrange("p a b -> p (a b)")[:, :16 * T].rearrange("p (h l) -> p h l", h=16)
        for h0 in range(0, 16, hpm):
            bank = (h0 * T) // 512
            o = PSLT[:, h0:h0 + hpm, :]
            self.mm(o, self.onesf.ap[:T, :T], R3[:, h0:h0 + hpm, :], [W[0], W[1], self.onesf], [P[bank]], start=True, stop=False)
            self.mm(o, self.tri.ap[:T, :T], tkv(5)[:, h0:h0 + hpm].unsqueeze(2).to_broadcast([T, hpm, T]), [self.tri, tk], [P[bank]], start=False, stop=False)
            self.mm(o, self.identb.ap[:T, :T], self.negmb.ap[:T, :T].unsqueeze(1).to_broadcast([T, hpm, T]), [self.identb, self.negmb], [P[bank]], start=False, stop=True)
        nb = (16 * T + 511) // 512
        LTb = Wb[2][:T, :16 * T]
        self.actf(LTb, self.PS[:T, 0:4, :].rearrange("p a b -> p (a b)")[:, :16 * T], AF.Exp, [P[i] for i in range(nb)], [W[2]])
        for g in range(4):
            self.mm(P[4].ap[:T, g * T:(g + 1) * T], xcv(8 + g), xcv(12 + g), [xbcT], [P[4]])
        self.actf(cbT.ap[:T, :4 * T], P[4].ap[:T, :4 * T], AF.Copy, [P[4]], [cbT])
        Mb = Wb[3][:T, :16 * T]
        self.tt(pool, Mb.rearrange("p (g r l) -> p g r l", g=4, r=4), LTb.rearrange("p (g r l) -> p g r l", g=4, r=4),
                cbT.ap[:T, :4 * T].rearrange("p (g l) -> p g l", g=4).unsqueeze(2).to_broadcast([T, 4, 4, T]), ALU.mult, [W[2], cbT], [W[3]])
        for c in range(8):
            self.tr(Pb[5][:T, c * 128:(c + 1) * 128], xcv(c), self.identb.ap, [xbcT, self.identb], [P[5]])
        self.actf(xtok.ap[:T, :], Pb[5][:T, :], AF.Copy, [P[5]], [xtok])
        for g in range(4):
            self.tr(Pb[6][:T, g * 128:(g + 1) * 128], xcv(8 + g), self.identb.ap, [xbcT, self.identb], [P[6]])
        self.actf(btok.ap[:T, :], Pb[6][:T, :512], AF.Copy, [P[6]], [btok])
        self.tt(dve, h3(xdt.ap[:T, :]), h3(xtok.ap[:T, :]), b3(tkv(0)), ALU.mult, [xtok, tk], [xdt])
        self.tt(pool, h3(xdd.ap[:T, :]), h3(xdt.ap[:T, :]), b3(tkv(4)), ALU.mult, [xdt, tk], [xdd])
        M3 = Mb.rearrange("p (h l) -> p h l", h=16)
        for h in range(16):
            self.mm(P[h // 8].ap[:T, (h % 8) * 64:(h % 8 + 1) * 64], M3[:, h, :], xdt.ap[:T, h * 64:(h + 1) * 64], [W[3], xdt], [P[h // 8]])
        for g in range(4):
            self.mm(P[2 + g // 2].ap[:T, (g % 2) * 256:(g % 2 + 1) * 256], xcv(12 + g), HTb.ap[:, g * 256:(g + 1) * 256], [xbcT, HTb], [P[2 + g // 2]])
        yo = W[4].ap[:T, :]
        tmp = W[5].ap[:T, :]
        PSO = self.PS[:T, 2:4, :].rearrange("p a b -> p (a b)")
        PSY = self.PS[:T, 0:2, :].rearrange("p a b -> p (a b)")
        self.tt(dve, h3(yo), h3(PSO), b3(tkv(3)), ALU.mult, [P[2], P[3], tk], [W[4]])
        self.tt(dve, yo, yo, PSY, ALU.add, [W[4], P[0], P[1]], [W[4]])
        self.tt(pool, h3(tmp), h3(xtok.ap[:T, :]), b3(ssdc.ap[:T, 2, :]), ALU.mult, [xtok, ssdc], [W[5]])
        self.tt(dve, yo, yo, tmp, ALU.add, [W[4], W[5]], [W[4]])
        self.tt(dve, yo, yo, z_tok, ALU.mult, [W[4], z_t], [W[4]])
        ss = self.ss[0]
        S.pool(lambda e: e.memset(ss.ap[:], 0.0), [], [ss])
        S.act(lambda e: e.activation(out=tmp, in_=yo, func=AF.Square, accum_out=ss.ap[:T, 0:1]), [W[4]], [W[5], ss])
        self.ts(dve, ss.ap[:T, 1:2], ss.ap[:T, 0:1], 1.0 / 1024, ALU.mult, [ss], [ss], s2=EPS, op1=ALU.add)
        self.actf(ss.ap[:T, 1:2], ss.ap[:T, 1:2], AF.Sqrt, [ss], [ss])
        S.dve(lambda e: e.reciprocal(out=ss.ap[:T, 1:2], in_=ss.ap[:T, 1:2]), [ss], [ss])
        self.actf(yn.ap[:T, :], yo, AF.Copy, [W[4], ss], [yn], scale=ss.ap[:T, 1:2])
        for c in range(8):
            self.tr(Pb[5][:, c * T:(c + 1) * T], yn.ap[:T, c * 128:(c + 1) * 128], self.identb.ap[:T, :T], [yn, self.identb], [P[5]])
        for c in range(8):
            self.actf(ybT.ap[:, c, ycol0:ycol0 + T], Pb[5][:, c * T:(c + 1) * T], AF.Copy, [P[5], gncol], [ybT], scale=gncol.ap[:, c:c + 1])
        for g in range(4):
            PPs = P[4] if g < 2 else P[6]
            self.mm(PPs.ap[:, (g % 2) * 256:(g % 2 + 1) * 256], btok.ap[:T, g * 128:(g + 1) * 128], xdd.ap[:T, g * 256:(g + 1) * 256], [btok, xdd], [PPs])
        self.tt(dve, h3(HT.ap), h3(HT.ap), tk.ap[:, 7, :].unsqueeze(2).to_broadcast([128, 16, 64]), ALU.mult, [HT, tk], [HT])
        self.tt(dve, HT.ap[:, 0:512], HT.ap[:, 0:512], P[4].ap, ALU.add, [HT, P[4]], [HT])
        self.tt(dve, HT.ap[:, 512:1024], HT.ap[:, 512:1024], P[6].ap, ALU.add, [HT, P[6]], [HT])
        self.actf(HTb.ap, HT.ap, AF.Copy, [HT], [HTb])


N_CORES = 8
_OUT_SHAPES = [
    ("y_prompt", (8, 2048, 1024)), ("y_sample", (128, 4, 1024)),
    ("s5_prompt", (2, 8, 32, 64, 2)), ("s5_sample", (2, 128, 32, 64, 2)),
    ("ssd_prompt", (2, 8, 16, 64, 128)), ("ssd_sample", (2, 128, 16, 64, 128)),
    ("conv_prompt", (2, 8, 3, 2048)), ("conv_sample", (2, 128, 3, 2048)),
    ("cmp_rows_prompt", (2, 8, 2048, 2, 2, 64)), ("cmp_rows_sample", (2, 128, 4, 2, 2, 64)),
    ("sel_rows_prompt", (2, 8, 2048, 2, 2, 64)), ("sel_rows_sample", (2, 128, 4, 2, 2, 64)),
    ("win_prompt", (2, 8, 512, 2, 2, 64)), ("win_sample", (2, 128, 512, 2, 2, 64)),
]
PARTS = ("mlp", "even", "s5", "ssd", "odd")


def kernel(**inputs):
    mk = MK(parts=PARTS, depth=4)
    nc = mk.build()
    f32 = lambda a: np.ascontiguousarray(np.asarray(a, dtype=np.float32))
    shared = {}
    for k in mk.din:
        if k in ("x_prompt", "x_sample", "state_s5", "state_ssd", "state_conv", "state_win_kv", "page_table"):
            continue
        if k.startswith("cache_"):
            j = int(k[-1])
            src = inputs["cache_cmp_kv"] if k.startswith("cache_cmp_kv") else inputs["cache_sel_kv"]
            shared[k] = f32(np.asarray(src)[j]).reshape(NPOOL * 128, 256)
            continue
        v = inputs[k]
        if k == "norm_final":
            v = np.asarray(v).reshape(1, D)
        shared[k] = f32(v)
    in_maps = []
    for c in range(N_CORES):
        m = dict(shared)
        b0, b1 = 16 * c, 16 * c + 16
        m["x_prompt"] = f32(inputs["x_prompt"][c])
        m["x_sample"] = f32(np.asarray(inputs["x_sample"])[b0:b1].reshape(64, D))
        for k in ("state_s5", "state_ssd", "state_conv", "state_win_kv"):
            if k in mk.din:
                a_ = np.asarray(inputs[k])[:, b0:b1]
                m[k] = f32(a_.reshape(2, 16, 512, 256) if k == "state_win_kv" else a_)
        if "page_table" in mk.din:
            m["page_table"] = np.ascontiguousarray(np.asarray(inputs["page_table"], dtype=np.int32)[b0:b1])
        in_maps.append({k: v for k, v in m.items() if k in mk.din})
    res = run_bass_kernel_spmd(nc, in_maps, core_ids=list(range(N_CORES))).results
    outs = []
    for name, shp in _OUT_SHAPES:
        if name not in mk.dout:
            outs.append(np.zeros(shp, np.float32))
            continue
        per = [np.asarray(r[name], dtype=np.float32) for r in res]
        if name == "y_prompt":
            o = np.stack(per, 0)
        elif name == "y_sample":
            o = np.concatenate([p.reshape(16, 4, D) for p in per], 0)
        elif name.endswith("_prompt"):
            o = np.stack(per, 1)
        else:
            o = np.concatenate(per, 1)
        outs.append(np.ascontiguousarray(o.reshape(shp)))
    return tuple(outs)


ROPE_THETA = 500000.0
BIG = 30000.0


def _decl_odd(self):
    I, O = self.inp, self.outp
    self.norm_mix_odd = I("norm_mix_odd", [2, D])
    self.w_in_odd = I("w_in_odd", [2, D, IN_ODD])
    self.cmp_w1 = I("cmp_w1", [2, 2, 2048, 128])
    self.cmp_w2 = I("cmp_w2", [2, 2, 128, 64])
    self.cmp_pos = I("cmp_pos", [2, 2, 32, 64])
    self.w_out_odd = I("w_out_odd", [2, 1024, 1024])
    self.cache_cmp = [I("cache_cmp_kv_%d" % j, [NPOOL * 128, 256]) for j in range(2)]
    self.cache_sel = [I("cache_sel_kv_%d" % j, [NPOOL * 128, 256]) for j in range(2)]
    self.state_win = I("state_win_kv", [2, 16, 512, 256])
    self.page_table = I("page_table", [16, 16], I32)
    self.o_cmp_p = O("cmp_rows_prompt", [2, TP, 256])
    self.o_cmp_s = O("cmp_rows_sample", [2, TS, 256])
    self.o_sel_p = O("sel_rows_prompt", [2, TP, 256])
    self.o_sel_s = O("sel_rows_sample", [2, TS, 256])
    self.o_win_p = O("win_prompt", [2, 512, 256])
    self.o_win_s = O("win_sample", [2, 16, 512, 256])
    if self.dbg:
        O("dbg_acc", [TT, 1024])


def _rope(self, v1, v2, cosb, sinb, shape, t_src, scr_t, scr):
    dve, pool = self.S.dve_e, self.S.pool_e
    a, b, c, d = scr
    self.tt(dve, a, v1, cosb, ALU.mult, [t_src], [scr_t])
    self.tt(dve, b, v2, sinb, ALU.mult, [t_src], [scr_t])
    self.tt(dve, c, v2, cosb, ALU.mult, [t_src], [scr_t])
    self.tt(dve, d, v1, sinb, ALU.mult, [t_src], [scr_t])
    self.tt(dve, v1, a, b, ALU.subtract, [scr_t], [t_src])
    self.tt(dve, v2, c, d, ALU.add, [scr_t], [t_src])


def _odd_layer(self, l):
    S = self.S
    ii = l // 2
    dve, pool = S.dve_e, S.pool_e
    self.arena_reset()
    A = self.av
    P, Pb = self.P, self.Pb
    o = {}
    self._od = o
    o["ii"] = ii
    KcT = A("KcT", [128, TT], BF16)
    VcT = A("VcT", [128, TT], BF16)
    KsT = A("KsT", [128, TT], BF16)
    KwT = A("KwT", [128, TT], BF16)
    VsT_s = A("VsT_s", [128, 64], BF16)
    VwT_s = A("VwT_s", [128, 64], BF16)
    Vs = A("Vs", [128, 17, 2, 65], BF16)
    Vw = A("Vw", [128, 17, 2, 65], BF16)
    Vp = A("Vp", [128, 17, 2, 65], BF16)
    KpT = A("KpT", [128, 2048], BF16)
    gates = A("gates", [128, 17, 48], F32)
    cs = A("ropecs", [128, 17, 16], F32)
    kc_all = A("kc_all", [128, 17, 128], BF16)
    vca_all = A("vca_all", [128, 17, 2, 64], BF16)
    w2k = A("w2k", [128, 2, 128], BF16)
    w2v = A("w2v", [128, 64], BF16)
    peT = A("peT", [128, 2, 32], BF16)
    peb = A("peb", [128, 2], F32)
    hid = A("hid", [128, 2, 128], BF16)
    Em = A("Em", [128, 17, 128], BF16)
    ncausT = A("ncausT", [128, 128], BF16)
    nwinT = A("nwinT", [128, 128], BF16)
    ncaus4 = A("ncaus4", [128, 4, 128], BF16)
    nwin4 = A("nwin4", [128, 4, 128], BF16)
    ncaus_s = A("ncaus_s", [128, 8, 4], BF16)
    nwin_s = A("nwin_s", [128, 8, 4], BF16)
    selm4 = A("selm4", [128, 512], BF16)
    qTs = A("qTs", [128, 8, 4], BF16)
    SelR = A("SelR", [4, 8, 32], BF16)
    SelT = A("SelT", [32, 8, 4], F32)
    Mw = A("Mw", [128, 256], BF16)
    Msm = A("Msm", [128, 128], BF16)
    xnt = A("xnt", [128, 8, 128], BF16)
    pr = A("pr", [128, 1024], F32)
    prb = A("prb", [128, 1024], BF16)
    rsc = A("rsc", [128, 4, 128], F32)
    qT = A("qT", [128, 8, 128], BF16)
    sc = [A("sc%d" % i, [128, 4, 128], F32) for i in range(2)]
    Pc = A("Pc", [128, 16, 128], BF16)
    PTq = [A("PTq%d" % i, [128, 4, 128], BF16) for i in range(4)]
    PT = None
    acc = A("acc", [128, 1024], F32)
    w2f = S.alias("w2f", acc, acc.ap[:, 256:384].rearrange("p (a b) -> p a b", a=2))
    peTf = S.alias("peTf", acc, acc.ap[:, 384:448].rearrange("p (a b) -> p a b", a=2))
    ob = S.alias("ob", prb, prb.ap)
    oT = A("oT", [128, 8, 128], BF16)
    sml = A("sml", [128, 16, 16], F32)
    imp = A("imp", [128, 2, 64], F32)
    psg = A("psg", [128, 2, 128], F32)
    sbias = A("sbias", [128, 64], F32)
    sel01 = A("sel01", [128, 2, 64], F32)
    mx8 = A("mx8", [128, 2, 8], F32)
    selm = None
    stg = [A("stg%d" % i, [128, 4, 256], BF16) for i in range(2)]
    idx = A("idx", [128, 16, 16], I32)
    idxf = S.alias("idxf", acc, acc.ap[:, 0:256].rearrange("p (a b) -> p a b", a=16))
    o.update(locals())

    for t_ in (Vs, Vw, Vp):
        S.pool(lambda e: e.memset(t_.ap, 1.0), [], [t_])
    S.pool(lambda e: e.memset(vca_all.ap, 0.0), [], [vca_all])
    S.pool(lambda e: e.memset(kc_all.ap, 0.0), [], [kc_all])
    S.pool(lambda e: e.memset(ncausT.ap, 0.0), [], [ncausT])
    S.pool(lambda e: e.affine_select(out=ncausT.ap, in_=ncausT.ap, compare_op=ALU.is_ge, fill=-BIG, base=0,
                                     pattern=[[1, 128]], channel_multiplier=-1), [ncausT], [ncausT])
    S.pool(lambda e: e.memset(nwinT.ap, 0.0), [], [nwinT])
    S.pool(lambda e: e.affine_select(out=nwinT.ap, in_=nwinT.ap, compare_op=ALU.is_gt, fill=-BIG, base=0,
                                     pattern=[[-1, 128]], channel_multiplier=1), [nwinT], [nwinT])
    S.pool(lambda e: e.tensor_copy(ncaus4.ap, ncausT.ap.unsqueeze(1).to_broadcast([128, 4, 128])), [ncausT], [ncaus4])
    S.pool(lambda e: e.tensor_copy(nwin4.ap, nwinT.ap.unsqueeze(1).to_broadcast([128, 4, 128])), [nwinT], [nwin4])
    S.pool(lambda e: e.memset(ncaus_s.ap, 0.0), [], [ncaus_s])
    for hh in range(2):
        nv = ncaus_s.ap[hh * 64:(hh + 1) * 64]
        S.pool(lambda e: e.affine_select(out=nv, in_=nv, compare_op=ALU.is_ge, fill=-BIG, base=0,
                                         pattern=[[0, 8], [1, 4]], channel_multiplier=-1), [ncaus_s], [ncaus_s])
    S.pool(lambda e: e.tensor_copy(nwin_s.ap, nwinT.ap[:, 0:4].unsqueeze(1).to_broadcast([128, 8, 4])), [nwinT], [nwin_s])
    S.pool(lambda e: e.memset(SelR.ap, 1.0), [], [SelR])
    S.pool(lambda e: e.affine_select(out=SelR.ap.rearrange("p i (a s) -> p i a s", a=8), in_=SelR.ap.rearrange("p i (a s) -> p i a s", a=8),
                                     compare_op=ALU.is_equal, fill=0.0, base=0, pattern=[[-4, 8], [4, 8], [1, 4]], channel_multiplier=-1), [SelR], [SelR])
    S.pool(lambda e: e.memset(SelT.ap, 1.0), [], [SelT])
    S.pool(lambda e: e.affine_select(out=SelT.ap, in_=SelT.ap, compare_op=ALU.is_equal, fill=0.0, base=0,
                                     pattern=[[-4, 8], [-1, 4]], channel_multiplier=1), [SelT], [SelT])
    S.pool(lambda e: e.memset(Mw.ap, 0.0), [], [Mw])
    S.pool(lambda e: e.affine_select(out=Mw.ap, in_=Mw.ap, compare_op=ALU.is_ge, fill=-BIG, base=-31 + 16 * 128,
                                     pattern=[[-16, 256]], channel_multiplier=1), [Mw], [Mw])
    S.pool(lambda e: e.memset(Msm.ap, 0.0), [], [Msm])
    S.pool(lambda e: e.memset(Msm.ap[:, 127:128], -BIG), [], [Msm])
    S.pool(lambda e: e.memset(Em.ap, 0.0), [], [Em])
    for hh in range(2):
        ev = Em.ap[hh * 64:(hh + 1) * 64].rearrange("p k (h c) -> p k h c", h=2)
        S.pool(lambda e: e.affine_select(out=ev, in_=ev, compare_op=ALU.not_equal, fill=BIG, base=0,
                                         pattern=[[-2, 17], [-1, 2], [0, 64]], channel_multiplier=1), [Em], [Em])
    posi = S.alias("posi", idx, idx.ap[:, 0, :].bitcast(I32))
    S.pool(lambda e: e.iota(idx.ap[:, 0, :], pattern=[[128, 16]], base=0, channel_multiplier=1), [], [idx])
    S.dve(lambda e: e.tensor_copy(sml.ap[:, 0, :], idx.ap[:, 0, :]), [idx], [sml])
    S.pool(lambda e: e.iota(idx.ap[:, 1, 0:1], pattern=[[0, 1]], base=0, channel_multiplier=1), [], [idx])
    S.dve(lambda e: e.tensor_single_scalar(out=idx.ap[:, 1, 1:2], in_=idx.ap[:, 1, 0:1], scalar=3, op=ALU.bitwise_and), [idx], [idx])
    S.dve(lambda e: e.tensor_copy(sml.ap[:, 1, 0:1], idx.ap[:, 1, 1:2]), [idx], [sml])
    self.ts(dve, sml.ap[:, 1, 0:1], sml.ap[:, 1, 0:1], 2048.0, ALU.add, [sml], [sml])
    for f in range(8):
        S.pool(lambda e: e.memset(sml.ap[:, 2, f:f + 1], float(ROPE_THETA ** (-f / 8.0))), [], [sml])
    ang = S.alias("ang", pr, pr.ap[:, 0:17 * 8].rearrange("p (a b) -> p a b", a=17))
    self.tt(dve, ang.ap[:, 0:16, :], sml.ap[:, 0, :].unsqueeze(2).to_broadcast([128, 16, 8]),
            sml.ap[:, 2, 0:8].unsqueeze(1).to_broadcast([128, 16, 8]), ALU.mult, [sml], [pr])
    self.ts(dve, ang.ap[:, 16, :], sml.ap[:, 2, 0:8], sml.ap[:, 1, 0:1], ALU.mult, [sml], [pr])
    scr = (S.alias("r1", pr, pr.ap[:, 256:392].rearrange("p (a b) -> p a b", a=17)),
           S.alias("r2", pr, pr.ap[:, 512:648].rearrange("p (a b) -> p a b", a=17).bitcast(I32)),
           S.alias("r3", pr, pr.ap[:, 768:904].rearrange("p (a b) -> p a b", a=17)))
    self.sincos(ang.ap, pr, cs.ap[:, :, 8:16], cs.ap[:, :, 0:8], cs, scr)
    w1d = [self.wslot[2], self.wslot[3]]
    w1v = [w.ap.rearrange("p (j h) -> p j h", j=32) for w in w1d]
    for kv in range(2):
        for dup in range(2):
            S.dma(w1d[kv], w1v[kv][dup * 64:(dup + 1) * 64, :, :], self.cmp_w1[ii, kv].rearrange("(j d) h -> d j h", d=64), q="pool")
        S.dma(w2f, w2f.ap[:, kv, :], self.cmp_w2[ii, kv])
        for dup in range(2):
            S.dma(peTf, peTf.ap[dup * 64:(dup + 1) * 64, kv, :], self.cmp_pos[ii, kv].rearrange("j d -> d j"))
    S.dve(lambda e: e.tensor_copy(peT.ap, peTf.ap), [peTf], [peT])
    S.pool(lambda e: e.memset(w2k.ap, 0.0), [], [w2k])
    for g in range(2):
        S.dve(lambda e: e.tensor_copy(w2k.ap[:, g, g * 64:(g + 1) * 64], w2f.ap[:, 0, :]), [w2f], [w2k])
    S.dve(lambda e: e.tensor_copy(w2v.ap, w2f.ap[:, 1, :]), [w2f], [w2v])
    for kv in range(2):
        for j in range(32):
            self.mm(P[7].ap[:, kv:kv + 1], w1v[kv][0:64, j, :], peT.ap[0:64, kv, j:j + 1], [w1d[kv], peT], [P[7]], start=(j == 0), stop=(j == 31))
    self.actf(peb.ap, P[7].ap[:, 0:2], AF.Copy, [P[7]], [peb])
    S.dma(idx, idx.ap.rearrange("p a b -> p (a b)"), self.page_table.rearrange("b j -> (b j)").partition_broadcast(128))
    S.dve(lambda e: e.tensor_copy(idxf.ap, idx.ap), [idx], [idxf])
    S.pool(lambda e: e.iota(idx.ap[:, 0, 0:1], pattern=[[0, 1]], base=0, channel_multiplier=1), [idxf], [idx])
    S.dve(lambda e: e.tensor_copy(sml.ap[:, 3, 0:1], idx.ap[:, 0, 0:1]), [idx], [sml])
    self.ts(dve, idxf.ap, idxf.ap, 128.0, ALU.mult, [idxf, sml], [idxf], s2=sml.ap[:, 3, 0:1], op1=ALU.add)
    S.dve(lambda e: e.tensor_copy(idx.ap, idxf.ap), [idxf], [idx])
    self.load_gb(self.norm_mix_odd[ii:ii + 1, :])
    S.barrier()
    o.update(locals())

    stop = getattr(self, "odd_stop", 99)
    if stop <= 0:
        return
    win = self.w_in_odd[ii].rearrange("(c p) n -> p c n", p=128)
    wkv = [self.load_w(win[:, j * 4:(j + 1) * 4, 1024:1840], slot=j) for j in range(2)]
    for i in range(NT):
        r = rows(i)
        self.norm_T(i, xnt.ap, xnt, 0, pbase=4)
        for (c0, c1, PP) in ((0, 512, P[0]), (512, 816, P[1])):
            for k in range(8):
                wt, w = wkv[k // 4]
                self.mm(PP.ap[:r, :c1 - c0], xnt.ap[:, k, :r], w[:, k % 4, c0:c1], [xnt, wt], [PP], start=(k == 0), stop=(k == 7))
            self.actf(pr.ap[:r, c0:c1], PP.ap[:r, :c1 - c0], AF.Copy, [PP], [pr])
        kv5 = pr.ap[:r, 0:768].rearrange("p (b k g d) -> p b k g d", b=3, k=2, g=2)
        v1, v2 = kv5[:, :, 0, :, 0:8], kv5[:, :, 0, :, 8:16]
        cosb = cs.ap[:r, i, 0:8].unsqueeze(1).unsqueeze(1).to_broadcast([r, 3, 2, 8])
        sinb = cs.ap[:r, i, 8:16].unsqueeze(1).unsqueeze(1).to_broadcast([r, 3, 2, 8])
        rs = [rsc.ap[:r, j, 0:48].rearrange("p (a b c) -> p a b c", a=3, b=2) for j in range(4)]
        _rope(self, v1, v2, cosb, sinb, None, pr, rsc, rs)
        r0 = i * 128
        if i < 16:
            S.dma_out(self.o_cmp_p[ii, r0:r0 + 128, :], pr.ap[:, 0:256], pr)
            S.dma_out(self.o_sel_p[ii, r0:r0 + 128, :], pr.ap[:, 256:512], pr)
            if i >= 12:
                S.dma_out(self.o_win_p[ii, r0 - 1536:r0 - 1536 + 128, :], pr.ap[:, 512:768], pr)
        else:
            S.dma_out(self.o_cmp_s[ii], pr.ap[:64, 0:256], pr)
            S.dma_out(self.o_sel_s[ii], pr.ap[:64, 256:512], pr)
            for b in range(16):
                S.dma_out(self.o_win_s[ii, b, 508:512, :], pr.ap[4 * b:4 * b + 4, 512:768], pr)
                S.dma_fn(lambda e: e.dma_start(out=self.o_win_s[ii, b, 0:508, :], in_=self.state_win[ii, b, 4:512, :]), [], [])
        self.actf(gates.ap[:r, i, :], pr.ap[:r, 768:816], AF.Sigmoid, [pr], [gates])
        self.actf(prb.ap[:r, 0:768], pr.ap[:r, 0:768], AF.Copy, [pr], [prb])
        PPt = P[2]
        ptb = Pb[2]
        srcs = [(0, KcT), (128, VcT), (256, KsT), (512, KwT)]
        if i == 16:
            srcs += [(384, VsT_s), (640, VwT_s)]
        for n_, (c0, dst) in enumerate(srcs):
            self.tr(ptb[:, n_ * 128:n_ * 128 + r], prb.ap[:r, c0:c0 + 128], self.identb.ap[:r, :r], [prb, self.identb], [PPt])
        for n_, (c0, dst) in enumerate(srcs):
            dcol = dst.ap[:, r0:r0 + r] if dst.ap.shape[1] == TT else dst.ap[:, 0:r]
            self.actf(dcol, ptb[:, n_ * 128:n_ * 128 + r], AF.Copy, [PPt], [dst])
        if i < 16:
            S.pool(lambda e: e.tensor_copy(Vs.ap[:, i, :, 0:64], prb.ap[:, 384:512].rearrange("p (g d) -> p g d", g=2)), [prb], [Vs])
            S.pool(lambda e: e.tensor_copy(Vw.ap[:, i, :, 0:64], prb.ap[:, 640:768].rearrange("p (g d) -> p g d", g=2)), [prb], [Vw])

    if stop <= 1:
        return
    _compress(self, KcT.ap[:, 0:2048], KcT, VcT.ap[:, 0:2048], VcT, 16)
    if stop <= 2:
        return
    for b in range(16 if stop > 3 else 1):
        for pg in range(4):
            st = stg[pg % 2]
            for j in range(4):
                S.dma_fn(lambda e: e.indirect_dma_start(out=st.ap[:, j, :], out_offset=None, in_=self.cache_cmp[ii],
                                                        in_offset=bass.IndirectOffsetOnAxis(ap=idx.ap[:, b, pg * 4 + j:pg * 4 + j + 1], axis=0)),
                         [idx], [st], q="pool")
            for kv, dst in ((0, KpT), (1, VcT)):
                PPt = P[2 + kv]
                for j in range(4):
                    self.tr(Pb[2 + kv][:, j * 128:(j + 1) * 128], st.ap[:, j, kv * 128:(kv + 1) * 128], self.identb.ap, [st, self.identb], [PPt])
                self.actf(dst.ap[:, pg * 512:(pg + 1) * 512], Pb[2 + kv][:, 0:512], AF.Copy, [PPt], [dst])
        _compress(self, KpT.ap, KpT, VcT.ap[:, 0:2048], VcT, b)

    if stop <= 3:
        return
    wq = [self.load_w(win[:, j * 4:(j + 1) * 4, 0:1024], slot=j) for j in range(2)]
    wo_d = self.w_out_odd[ii].rearrange("(c p) n -> p c n", p=128)
    wo = []
    for j in range(2):
        slot = self.wslot[2 + j]
        ap = slot.ap[:, :4096].rearrange("p (a b) -> p a b", a=4)
        S.dma(slot, ap, wo_d[:, j * 4:(j + 1) * 4, :], q="pool")
        wo.append((slot, ap))
    o["wo"] = wo
    tl = list(range(NT))
    if stop == 5:
        tl = [0]
    elif stop == 6:
        tl = [9]
    elif stop == 7:
        tl = [16]
    elif stop == 8:
        tl = list(range(16))
    for i in tl:
        r = rows(i)
        self.norm_T(i, xnt.ap, xnt, 0, pbase=4)
        for half in range(2):
            PP = P[half]
            for k in range(8):
                wt, w = wq[k // 4]
                self.mm(PP.ap[:r, :], xnt.ap[:, k, :r], w[:, k % 4, half * 512:(half + 1) * 512], [xnt, wt], [PP], start=(k == 0), stop=(k == 7))
            self.actf(pr.ap[:r, half * 512:(half + 1) * 512], PP.ap[:r, :], AF.Copy, [PP], [pr])
        q3 = pr.ap[:r, :].rearrange("p (h d) -> p h d", h=16)
        cosb = cs.ap[:r, i, 0:8].unsqueeze(1).to_broadcast([r, 16, 8])
        sinb = cs.ap[:r, i, 8:16].unsqueeze(1).to_broadcast([r, 16, 8])
        rs = [rsc.ap[:r, j, :].rearrange("p (a b) -> p a b", a=16) for j in range(4)]
        _rope(self, q3[:, :, 0:8], q3[:, :, 8:16], cosb, sinb, None, pr, rsc, rs)
        self.actf(prb.ap[:r, :].rearrange("p (i g d) -> p i g d", i=8, g=2),
                  pr.ap[:r, :].rearrange("p (g i d) -> p i g d", g=2, i=8), AF.Copy, [pr], [prb])
        PPt = P[2]
        for i8 in range(8):
            self.tr(Pb[2][:, i8 * 128:i8 * 128 + r], prb.ap[:r, i8 * 128:(i8 + 1) * 128], self.identb.ap[:r, :r], [prb, self.identb], [PPt])
        self.actf(qT.ap[:, :, :r], Pb[2][:, :].rearrange("p (a b) -> p a b", a=8)[:, :, :r], AF.Copy, [PPt], [qT])
        if i < 16:
            _attend(self, i, 128, qT.ap[:, :, :], 16, i)
            _acc_to_oT(self, 128, 0, i * 128)
            _out_mm(self, i, 128)
        else:
            for b in range(16):
                _sample_kv(self, b)
                S.pool(lambda e: e.tensor_copy(qTs.ap, qT.ap[:, :, 4 * b:4 * b + 4]), [qT], [qTs])
                _attend(self, 16, 4, qTs.ap, b, None, b)
                _acc_to_oT(self, 4, 4 * b, TP + 4 * b)
            _out_mm(self, 16, 64)


def _compress(self, KT, KT_t, VT, VT_t, slot):
    S = self.S
    o = self._od
    P = self.P
    hid, w2k, w2v, peb, kc_all, vca_all = (o[k] for k in ("hid", "w2k", "w2v", "peb", "kc_all", "vca_all"))
    w1d, w1v = o["w1d"], o["w1v"]
    for kv, (XT, X_t) in enumerate(((KT, KT_t), (VT, VT_t))):
        for g in range(2):
            PP = P[4 + g]
            for j in range(32):
                rhs = XT[g * 64:(g + 1) * 64, j:j + 2017:16]
                self.mm(PP.ap[:, :127], w1v[kv][g * 64:(g + 1) * 64, j, :], rhs, [w1d[kv], X_t], [PP], start=(j == 0), stop=(j == 31))
            self.actf(hid.ap[:, g, :127], PP.ap[:, :127], AF.Silu, [PP, peb], [hid], bias=peb.ap[:, kv:kv + 1])
        if kv == 0:
            PP = P[6]
            for g in range(2):
                self.mm(PP.ap[:, :127], w2k.ap[:, g, :], hid.ap[:, g, :127], [w2k, hid], [PP], start=(g == 0), stop=(g == 1))
            self.actf(kc_all.ap[:, slot, :127], PP.ap[:, :127], AF.Copy, [PP], [kc_all])
        else:
            PP = P[7]
            for g in range(2):
                self.mm(PP.ap[:127, g * 64:(g + 1) * 64], hid.ap[:, g, :127], w2v.ap, [hid, w2v], [PP])
            self.actf(vca_all.ap[:127, slot, :, :], PP.ap[:127, 0:128].rearrange("p (g d) -> p g d", g=2), AF.Copy, [PP], [vca_all])


def _sample_kv(self, b):
    S = self.S
    o = self._od
    ii = o["ii"]
    P, Pb = self.P, self.Pb
    stg, idx, KpT, Vp, KwT, Vw, KsT, VsT_s, VwT_s = (o[k] for k in ("stg", "idx", "KpT", "Vp", "KwT", "Vw", "KsT", "VsT_s", "VwT_s"))
    for pg in range(5):
        st = stg[pg % 2]
        if pg < 4:
            for j in range(4):
                S.dma_fn(lambda e: e.indirect_dma_start(out=st.ap[:, j, :], out_offset=None, in_=self.cache_sel[ii],
                                                        in_offset=bass.IndirectOffsetOnAxis(ap=idx.ap[:, b, pg * 4 + j:pg * 4 + j + 1], axis=0)),
                         [idx], [st], q="pool")
            KT_dst, V_dst, t0 = KpT, Vp, pg * 4
            kcol = pg * 512
        else:
            S.dma(st, st.ap, self.state_win[ii, b].rearrange("(j p) c -> p j c", p=128), q="pool")
            KT_dst, V_dst, t0 = KwT, Vw, 0
            kcol = 0
        PPt = P[3]
        for j in range(4):
            self.tr(Pb[3][:, j * 128:(j + 1) * 128], st.ap[:, j, 0:128], self.identb.ap, [st, self.identb], [PPt])
        self.actf(KT_dst.ap[:, kcol:kcol + 512], Pb[3][:, 0:512], AF.Copy, [PPt], [KT_dst])
        S.pool(lambda e: e.tensor_copy(V_dst.ap[:, t0:t0 + 4, :, 0:64], st.ap[:, :, 128:256].rearrange("p j (g d) -> p j g d", g=2)), [st], [V_dst])
    for (VT_s, V_dst, tile_) in ((VsT_s, Vp, 16), (VwT_s, Vw, 4)):
        PPt = P[3]
        self.tr(Pb[3][:4, 0:128], VT_s.ap[:, 4 * b:4 * b + 4], self.identb.ap, [VT_s, self.identb], [PPt])
        self.actf(V_dst.ap[:4, tile_, :, 0:64], Pb[3][:4, 0:128].rearrange("p (g d) -> p g d", g=2), AF.Copy, [PPt], [V_dst])


def _acc_to_oT(self, r, c0, row0=None):
    S = self.S
    o = self._od
    if self.dbg and row0 is not None:
        S.dma_out(self.dout["dbg_acc"][row0:row0 + r, :], o["acc"].ap[:r, :], o["acc"])
    P, Pb = self.P, self.Pb
    acc, ob, oT = o["acc"], o["ob"], o["oT"]
    self.actf(ob.ap[:r, :], acc.ap[:r, :], AF.Copy, [acc], [ob])
    o4 = ob.ap[:r, :].rearrange("p (c x) -> p c x", c=8)
    PPt = P[7]
    for c in range(8):
        self.tr(Pb[7][:, c * 128:c * 128 + r], o4[:, c, :], self.identb.ap[:r, :r], [ob, self.identb], [PPt])
    self.actf(oT.ap[:, :, c0:c0 + r], Pb[7][:, :].rearrange("p (a b) -> p a b", a=8)[:, :, :r], AF.Copy, [PPt], [oT])


def _out_mm(self, i, r):
    S = self.S
    o = self._od
    P = self.P
    oT, wo = o["oT"], o["wo"]
    dve = S.dve_e
    for half in range(2):
        PP = P[half]
        for k in range(8):
            wt, w = wo[k // 4]
            self.mm(PP.ap[:r, :], oT.ap[:, k, :r], w[:, k % 4, half * 512:(half + 1) * 512], [oT, wt], [PP], start=(k == 0), stop=(k == 7))
        xs_ = self.X.ap[:r, i, half * 512:(half + 1) * 512]
        self.tt(dve, xs_, xs_, PP.ap[:r, :], ALU.add, [PP, self.xt[i]], [self.xt[i]])


def _attend(self, tile_i, r, qT_ap, cslot, qt, b=None):
    S = self.S
    o = self._od
    dve, pool = S.dve_e, S.pool_e
    P, Pb = self.P, self.Pb
    sample = b is not None
    kc_all, vca_all, sc, Pc, acc, sml, imp, psg, sbias, sel01, mx8, gates, Em = (o[k] for k in (
        "kc_all", "vca_all", "sc", "Pc", "acc", "sml", "imp", "psg", "sbias", "sel01", "mx8", "gates", "Em"))
    KsT, KwT, KpT, Vs, Vw, Vp, ncausT, nwinT = (o[k] for k in ("KsT", "KwT", "KpT", "Vs", "Vw", "Vp", "ncausT", "nwinT"))
    grow = (4 * b) if sample else 0
    RS, RINV, MXN, CO, DEN, GT = 4, 5, 6, 7, 8, 9
    if sample:
        PPm = P[7]
        self.mm(PPm.ap[:r, 0:48], self.ident.ap[:64, grow:grow + r], gates.ap[:64, 16, :], [self.ident, gates], [PPm])
        self.actf(sml.ap[:r, GT:GT + 3, :].rearrange("p a b -> p (a b)"), PPm.ap[:r, 0:48], AF.Copy, [PPm], [sml])
        gv = sml.ap[:r, GT:GT + 3, :].rearrange("p a b -> p (a b)").rearrange("p (h k) -> p h k", k=3)
    else:
        gv = gates.ap[:r, tile_i, :].rearrange("p (h k) -> p h k", k=3)
    kc = kc_all.ap[:, cslot, :]
    if sample:
        mk = o["Msm"]
        mk_ap = mk.ap
    else:
        mk = o["Mw"]
        mk_ap = mk.ap[:, 128 - 8 * qt:256 - 8 * qt]
    S.pool(lambda e: e.memset(sml.ap[:, RS, :], 0.0), [], [sml])
    for h in range(16):
        g, i8 = h // 8, h % 8
        self.mm(P[h // 4].ap[:r, (h % 4) * 128:(h % 4 + 1) * 128], qT_ap[g * 64:(g + 1) * 64, i8, :], kc[g * 64:(g + 1) * 64, :], [o["qT"], o["qTs"], kc_all], [P[h // 4]])
    for bk in range(4):
        s4 = sc[bk % 2]
        self.tt(dve, s4.ap[:r], P[bk].ap[:r, :].rearrange("p (a b) -> p a b", a=4), mk_ap[:r, :].unsqueeze(1).to_broadcast([r, 4, 128]), ALU.add, [P[bk], mk], [s4])
        S.dve(lambda e: e.tensor_reduce(out=sml.ap[:r, MXN, bk:bk + 1], in_=s4.ap[:r].rearrange("p a b -> p (a b)"), axis=AX.X, op=ALU.max), [s4], [sml])
        self.ts(dve, sml.ap[:r, MXN, bk:bk + 1], sml.ap[:r, MXN, bk:bk + 1], -10000.0, ALU.max, [sml], [sml], s2=-0.125, op1=ALU.mult)
        self.actf(Pc.ap[:r, bk * 4:bk * 4 + 4, :], s4.ap[:r], AF.Exp, [s4, sml], [Pc], bias=sml.ap[:r, MXN, bk:bk + 1], scale=0.125)
        S.dve(lambda e: e.tensor_reduce(out=sml.ap[:r, RS, bk * 4:bk * 4 + 4], in_=Pc.ap[:r, bk * 4:bk * 4 + 4, :], axis=AX.X, op=ALU.add), [Pc], [sml])
    self.ts(dve, sml.ap[:r, RINV, :], sml.ap[:r, RS, :], 1e-30, ALU.max, [sml], [sml])
    S.dve(lambda e: e.reciprocal(out=sml.ap[:r, RINV, :], in_=sml.ap[:r, RINV, :]), [sml], [sml])
    need_sel = (sample or qt >= 8) and not getattr(self, 'nosel', False)
    selstage = getattr(self, 'selstage', 99)
    if need_sel:
        for h in range(16):
            g = h // 8
            if h % 8 == 0:
                self.ts(dve, psg.ap[:r, g, :], Pc.ap[:r, h, :], sml.ap[:r, RINV, h:h + 1], ALU.mult, [Pc, sml], [psg])
            else:
                self.stt(dve, psg.ap[:r, g, :], Pc.ap[:r, h, :], sml.ap[:r, RINV, h:h + 1], psg.ap[:r, g, :], ALU.mult, ALU.add, [Pc, sml, psg], [psg])
        p4 = psg.ap[:r, :, :].rearrange("p g (j f) -> p g j f", f=4)
        S.pool(lambda e: e.memset(imp.ap[:r], 0.0), [], [imp])
        S.dve(lambda e: e.tensor_reduce(out=imp.ap[:r, :, 0:32], in_=p4, axis=AX.X, op=ALU.add), [psg], [imp])
        self.tt(dve, imp.ap[:r, :, 1:32], imp.ap[:r, :, 1:32], p4[:, :, 0:31, 3], ALU.add, [imp, psg], [imp])
    for h in range(16):
        self.tr(Pb[4 + h // 8][:, (h % 8) * 128:(h % 8) * 128 + r], Pc.ap[:r, h, :], self.identb.ap[:r, :r], [Pc, self.identb], [P[4 + h // 8]])
    PTq = o["PTq"]
    for hb in range(4):
        self.actf(PTq[hb].ap[:, :, :r], Pb[4 + hb // 2][:, (hb % 2) * 512:(hb % 2 + 1) * 512].rearrange("p (a b) -> p a b", a=4)[:, :, :r], AF.Copy, [P[4 + hb // 2]], [PTq[hb]])
    for h in range(16):
        g = h // 8
        self.mm(P[6 + h // 8].ap[:r, (h % 8) * 64:(h % 8 + 1) * 64], PTq[h // 4].ap[:, h % 4, :r], vca_all.ap[:, cslot, g, :], [PTq[h // 4], vca_all], [P[6 + h // 8]])
    self.tt(dve, sml.ap[:r, CO, :], gv[:, :, 0], sml.ap[:r, RINV, :], ALU.mult, [gates, sml], [sml])
    OC = self.PS[:r, 6:8, :].rearrange("p a b -> p (a b)").rearrange("p (h d) -> p h d", h=16)
    self.tt(dve, acc.ap[:r, :].rearrange("p (h d) -> p h d", h=16), OC, sml.ap[:r, CO, :].unsqueeze(2).to_broadcast([r, 16, 64]), ALU.mult,
            [P[6], P[7], sml], [acc])
    dbr = getattr(self, "dbg_br", (0, 1, 2))
    if 0 not in dbr:
        S.pool(lambda e: e.memset(acc.ap[:r, :], 0.0), [], [acc])
    nj = 33 if sample else 32
    if need_sel and selstage >= 2:
        S.pool(lambda e: e.memset(sbias.ap[:r], -1e30), [], [sbias])
        if sample:
            S.pool(lambda e: e.memset(sbias.ap[:r, 0:33], 0.0), [], [sbias])
            for j in (0, 31, 32):
                S.pool(lambda e: e.memset(sbias.ap[:r, j:j + 1], 1e4), [], [sbias])
        else:
            S.pool(lambda e: e.memset(sbias.ap[0:64, 0:2 * qt + 1], 0.0), [], [sbias])
            S.pool(lambda e: e.memset(sbias.ap[64:128, 0:2 * qt + 2], 0.0), [], [sbias])
            S.pool(lambda e: e.memset(sbias.ap[:, 0:1], 1e4), [], [sbias])
            S.pool(lambda e: e.memset(sbias.ap[0:64, 2 * qt - 1:2 * qt + 1], 1e4), [], [sbias])
            S.pool(lambda e: e.memset(sbias.ap[64:128, 2 * qt:2 * qt + 2], 1e4), [], [sbias])
        S.pool(lambda e: e.memset(imp.ap[:r, :, 32:64], 0.0), [], [imp])
        self.tt(dve, imp.ap[:r], imp.ap[:r], sbias.ap[:r, :].unsqueeze(1).to_broadcast([r, 2, 64]), ALU.add, [imp, sbias], [imp])
        for g in range(2 if selstage >= 3 else 0):
            S.dve(lambda e: e.max(out=mx8.ap[:r, g, :], in_=imp.ap[:r, g, :]), [imp], [mx8])
            S.dve(lambda e: e.match_replace(out=sel01.ap[:r, g, :], in_to_replace=mx8.ap[:r, g, :], in_values=imp.ap[:r, g, :], imm_value=-1e30), [mx8, imp], [sel01])
            S.dve(lambda e: e.max(out=mx8.ap[:r, g, :], in_=sel01.ap[:r, g, :]), [sel01], [mx8])
            S.dve(lambda e: e.tensor_reduce(out=sml.ap[:r, DEN, g:g + 1], in_=mx8.ap[:r, g, :], axis=AX.X, op=ALU.min), [mx8], [sml])
            self.ts(dve, sel01.ap[:r, g, :], imp.ap[:r, g, :], sml.ap[:r, DEN, g:g + 1], ALU.is_ge, [imp, sml], [sel01])
        if selstage >= 4:
            self.tr(P[7].ap[:, 0:r], sel01.ap[:r].rearrange("p g j -> p (g j)"), self.ident.ap[:r, :r], [sel01, self.ident], [P[7]])
            nrep = 8 if sample else 4
            self.ts(dve, o["selm4"].ap[:, 0:nrep * r].rearrange("p (i s) -> p i s", i=nrep),
                    P[7].ap[:, 0:r].unsqueeze(1).to_broadcast([128, nrep, r]), -1.0, ALU.add, [P[7]], [o["selm4"]])
    if sample:
        sel_tiles = [(KpT.ap[:, j * 128:(j + 1) * 128], KpT, Vp.ap[:, j], Vp, 128, None, j) for j in range(16)]
        ncs, nws = o["ncaus_s"], o["nwin_s"]
        sel_tiles.append((KsT.ap[:, TP + 4 * b:TP + 4 * b + 4], KsT, Vp.ap[:4, 16], Vp, 4, ncs, None))
        win_tiles = [(KwT.ap[:, j * 128:(j + 1) * 128], KwT, Vw.ap[:, j], Vw, 128, (nws if j == 0 else None), None) for j in range(4)]
        win_tiles.append((KwT.ap[:, TP + 4 * b:TP + 4 * b + 4], KwT, Vw.ap[:4, 4], Vw, 4, ncs, None))
    else:
        nc4, nw4 = o["ncaus4"], o["nwin4"]
        sel_tiles = [(KsT.ap[:, j * 128:(j + 1) * 128], KsT, Vs.ap[:, j], Vs, 128, (nc4 if j == qt else None), (j if (need_sel and selstage >= 5) else None)) for j in range(qt + 1)]
        win_tiles = [(KwT.ap[:, j * 128:(j + 1) * 128], KwT, Vw.ap[:, j], Vw, 128, (nc4 if j == qt else (nw4 if j == qt - 4 else None)), None)
                     for j in range(max(0, qt - 4), qt + 1)]
    PTq = o["PTq"]
    step = 0
    pendq = []

    def emit_pv(pu):
        (PTt_, V_, V_t_, nk_, g_, ti_, nt__, hq_, br_) = pu
        for j in range(4):
            self.mm(P[4 + j].ap[:r, 0:65], PTt_.ap[:nk_, j, :r], V_[:nk_, g_, :], [PTt_, V_t_], [P[4 + j]], start=(ti_ == 0), stop=(ti_ == nt__ - 1))
        if ti_ == nt__ - 1:
            h0_ = (hq_ // 2) * 8 + (hq_ % 2) * 4
            S.dve(lambda e: e.reciprocal(out=sml.ap[:r, DEN, h0_:h0_ + 4], in_=self.PS[:r, 4:8, 64]), [P[4], P[5], P[6], P[7]], [sml])
            self.tt(dve, sml.ap[:r, CO, h0_:h0_ + 4], gv[:, h0_:h0_ + 4, br_], sml.ap[:r, DEN, h0_:h0_ + 4], ALU.mult, [gates, sml], [sml])
            for j in range(4):
                h = h0_ + j
                a_h = acc.ap[:r, h * 64:(h + 1) * 64]
                self.stt(dve, a_h, P[4 + j].ap[:r, 0:64], sml.ap[:r, CO, h:h + 1], a_h, ALU.mult, ALU.add, [P[4 + j], sml, acc], [acc])

    if sample:
        _attend_sample_branches(self, r, sel_tiles, win_tiles, gv, dbr)
        return
    for br, tiles in ((1, sel_tiles), (2, win_tiles)):
        nt_ = len(tiles)
        if br not in dbr:
            continue
        for hq in range(4):
            g, i0 = hq // 2, (hq % 2) * 4
            for ti, (KT, KT_t, V, V_t, nk, mT, ej) in enumerate(tiles):
                SB = P[step % 3]
                PTt = PTq[step % 4]
                step += 1
                out2 = SB.ap[:nk, 0:4 * r]
                last = (ej is None and mT is None)
                self.mm(out2, KT[g * 64:(g + 1) * 64, :], qT_ap[g * 64:(g + 1) * 64, i0:i0 + 4, :].rearrange("p a b -> p (a b)"),
                        [KT_t, o["qT"], o["qTs"]], [SB], start=True, stop=last)
                if ej is not None:
                    self.mm(out2, Em.ap[g * 64:(g + 1) * 64, ej, :nk], o["selm4"].ap[g * 64:(g + 1) * 64, 0:4 * r], [Em, o["selm4"]], [SB], start=False, stop=(mT is None))
                if mT is not None:
                    pb_ = g * 64 if nk < 128 else 0
                    self.mm(out2, self.identb.ap[pb_:pb_ + nk, pb_:pb_ + nk], mT.ap[pb_:pb_ + nk].rearrange("p a b -> p (a b)")[:, 0:4 * r], [self.identb, mT], [SB], start=False, stop=True)
                self.actf(PTt.ap[:nk, :, :r], out2.rearrange("p (h s) -> p h s", h=4), AF.Exp, [SB], [PTt], scale=0.125)
                pendq.append((PTt, V, V_t, nk, g, ti, nt_, hq, br))
                if len(pendq) > 2:
                    emit_pv(pendq.pop(0))
    while pendq:
        emit_pv(pendq.pop(0))


def _attend_sample_branches(self, r, sel_tiles, win_tiles, gv, dbr):
    S = self.S
    o = self._od
    dve = S.dve_e
    P = self.P
    acc, sml, gates, Em, PTq, rsc, SelR, SelT = (o[k] for k in ("acc", "sml", "gates", "Em", "PTq", "rsc", "SelR", "SelT"))
    acc32 = S.alias("acc32", rsc, rsc.ap[:32, 0, :].rearrange("p (g d) -> p g d", g=2))
    G32 = S.alias("G32", rsc, rsc.ap[:32, 1, 0:8].rearrange("p (g k) -> p g k", g=2))
    gtsb = S.alias("gtsb", rsc, rsc.ap[:4, 2, 0:24].bitcast(BF16))
    cf = S.alias("cf", rsc, rsc.ap[:32, 3, 0:8])
    qTs = o["qTs"]
    self.actf(gtsb.ap, gv.rearrange("p h k -> p (h k)"), AF.Copy, [sml], [rsc])
    for g in range(2):
        for i8 in range(8):
            h = g * 8 + i8
            self.mm(P[6].ap[:32, g * 4:g * 4 + 3], SelR.ap[:4, i8, :], gtsb.ap[:4, h * 3:h * 3 + 3], [SelR, rsc], [P[6]], start=(i8 == 0), stop=(i8 == 7))
    self.actf(G32.ap[:, :, 0:3], P[6].ap[:32, 0:8].rearrange("p (g k) -> p g k", g=2)[:, :, 0:3], AF.Copy, [P[6]], [rsc])
    step = 0
    pendq = []
    first = {0: True, 1: True}

    def emit_pv(pu):
        (PTt_, V_, V_t_, nk_, g_, ti_, nt__, br_) = pu
        PO = P[4 + g_]
        self.mm(PO.ap[:32, 0:65], PTt_.ap[:nk_, 0, 0:32], V_[:nk_, g_, :], [PTt_, V_t_], [PO], start=(ti_ == 0), stop=(ti_ == nt__ - 1))
        if ti_ == nt__ - 1:
            S.dve(lambda e: e.reciprocal(out=cf.ap[:, 0:1], in_=PO.ap[:32, 64:65]), [PO], [rsc])
            self.tt(dve, cf.ap[:, 1:2], cf.ap[:, 0:1], G32.ap[:, g_, br_:br_ + 1], ALU.mult, [rsc], [rsc])
            if first[g_]:
                self.ts(dve, acc32.ap[:, g_, :], PO.ap[:32, 0:64], cf.ap[:, 1:2], ALU.mult, [PO, rsc], [rsc])
                first[g_] = False
            else:
                self.stt(dve, acc32.ap[:, g_, :], PO.ap[:32, 0:64], cf.ap[:, 1:2], acc32.ap[:, g_, :], ALU.mult, ALU.add, [PO, rsc], [rsc])

    for br, tiles in ((1, sel_tiles), (2, win_tiles)):
        nt_ = len(tiles)
        if br not in dbr:
            continue
        for g in range(2):
            for ti, (KT, KT_t, V, V_t, nk, mT, ej) in enumerate(tiles):
                SB = P[step % 3]
                PTt = PTq[step % 4]
                step += 1
                out2 = SB.ap[:nk, 0:32]
                last = (ej is None and mT is None)
                self.mm(out2, KT[g * 64:(g + 1) * 64, :], qTs.ap[g * 64:(g + 1) * 64].rearrange("p a b -> p (a b)"), [KT_t, qTs], [SB], start=True, stop=last)
                if ej is not None:
                    self.mm(out2, Em.ap[g * 64:(g + 1) * 64, ej, :nk], o["selm4"].ap[g * 64:(g + 1) * 64, 0:32], [Em, o["selm4"]], [SB], start=False, stop=(mT is None))
                if mT is not None:
                    pb_ = g * 64 if nk < 128 else 0
                    self.mm(out2, self.identb.ap[pb_:pb_ + nk, pb_:pb_ + nk], mT.ap[pb_:pb_ + nk].rearrange("p a b -> p (a b)"), [self.identb, mT], [SB], start=False, stop=True)
                self.actf(PTt.ap[:nk, 0, 0:32], out2, AF.Exp, [SB], [PTt], scale=0.125)
                pendq.append((PTt, V, V_t, nk, g, ti, nt_, br))
                if len(pendq) > 2:
                    emit_pv(pendq.pop(0))
    while pendq:
        emit_pv(pendq.pop(0))
    for g in range(2):
        if first[g]:
            continue
        PP = P[6 + g]
        for i8 in range(8):
            self.mm(PP.ap[:4, i8 * 64:(i8 + 1) * 64], SelT.ap[:32, i8, :], acc32.ap[:, g, :], [SelT, rsc], [PP])
        a_g = acc.ap[:4, g * 512:(g + 1) * 512]
        self.tt(dve, a_g, a_g, PP.ap[:4, :], ALU.add, [PP, acc], [acc])


MK.decl_odd = _decl_odd
MK.odd_layer = _odd_layer
```
